# Optimizing a Trainium2 kernel written in Bass

```python
import math
import jax, jax.numpy as jnp
from jax import lax
import numpy as np

D_MODEL = 1024
BATCH = 8
SEQ = 2048
DEPTH = 1
DEC_BATCH = 32
DEC_SEQ = 4
PAST_LEN = 8192
PAGE_SIZE = 128

DIL_GROUPS = ((128, 1), (512, 4), (2048, 16))
DIL_HPG = 4
DIL_N_HEADS = DIL_HPG * len(DIL_GROUPS)
DIL_HEAD_DIM = 64
DIL_QKV_W = DIL_N_HEADS * DIL_HEAD_DIM
DIL_Q_BLOCK = 128
REL_BUCKETS = 32
REL_MAX_DIST = 2048
GDN_HEADS = 8
GDN_HEAD_DIM = 128
GDN_W = GDN_HEADS * GDN_HEAD_DIM
GDN_QKV_W = 3 * GDN_W
GDN_CONV = 4
GDN_CHUNK = 64
MEM_LEN = 256
MEM_HEADS = 4
MEM_HEAD_DIM = 128
MEM_W = MEM_HEADS * MEM_HEAD_DIM
FFN_DIM = 2816
FFN_CONV = 3
EPS = 1e-6
SPLIT_SIZES = (DIL_QKV_W, DIL_QKV_W, DIL_QKV_W, GDN_QKV_W, GDN_W, GDN_HEADS, GDN_HEADS, D_MODEL, D_MODEL)
SPLIT_IDX = tuple(int(i) for i in np.cumsum(SPLIT_SIZES)[:-1])
N_IN = sum(SPLIT_SIZES)

kernel_name = "hybrid_dilated_deltanet_decoder_step"


def rmsnorm(x, gain):
    xf = x.astype(jnp.float32)
    y = xf * lax.rsqrt(jnp.mean(xf * xf, axis=-1, keepdims=True) + EPS)
    return (y * gain.astype(jnp.float32)).astype(x.dtype)


def l2norm(x):
    return x * lax.rsqrt(jnp.sum(x * x, axis=-1, keepdims=True) + EPS)


def rel_bucket(dist):
    exact = REL_BUCKETS // 2
    d = jnp.maximum(dist, 1).astype(jnp.float32)
    large = exact + (jnp.log(d / exact) / math.log(REL_MAX_DIST / exact) * (REL_BUCKETS - exact)).astype(jnp.int32)
    return jnp.where(dist < exact, dist, jnp.minimum(large, REL_BUCKETS - 1))


def dilation_biases(rel_bias):
    biases = []
    for g, (window, dil) in enumerate(DIL_GROUPS):
        dist = dil * jnp.arange(window // dil + 1, dtype=jnp.int32)
        tab = rel_bias[rel_bucket(dist)]
        biases.append(tab[:, g * DIL_HPG:(g + 1) * DIL_HPG].T.astype(jnp.float32))
    return biases


def dilated_group_attend(q, kv, qpos, base, dil, bias):
    nk = bias.shape[-1]
    kpos = qpos[:, None] - dil * jnp.arange(nk, dtype=jnp.int32)[None, :]
    valid = kpos >= 0
    idx = jnp.clip(kpos - base, 0, kv.shape[1] - 1)
    kvg = jnp.take(kv, idx, axis=1)
    s = jnp.einsum('bqhd,bqjhd->bhqj', q, kvg[:, :, :, 0]).astype(jnp.float32) * (DIL_HEAD_DIM ** -0.5)
    s = jnp.where(valid[None, None], s + bias[None, :, None, :], -jnp.inf)
    lse = jax.nn.logsumexp(s, axis=-1)
    p = jnp.exp(s - lse[..., None]).astype(kv.dtype)
    return jnp.einsum('bhqj,bqjhd->bqhd', p, kvg[:, :, :, 1]), lse


def dilated_mixture(q, kvs, qpos, bases, biases):
    outs, lses = [], []
    for g, (_, dil) in enumerate(DIL_GROUPS):
        o, lse = dilated_group_attend(q[:, :, g], kvs[g], qpos, bases[g], dil, biases[g])
        outs.append(o)
        lses.append(lse)
    wts = jax.nn.softmax(jnp.stack(lses), axis=0)
    return jnp.einsum('gbhq,gbqhd->bqhd', wts.astype(q.dtype), jnp.stack(outs))


def dilated_prompt(q, k, v, biases):
    B, S = q.shape[:2]
    kvs = [jnp.stack([k[:, :, g], v[:, :, g]], axis=2) for g in range(len(DIL_GROUPS))]
    bases = (0,) * len(DIL_GROUPS)

    def block(start):
        qb = lax.dynamic_slice_in_dim(q, start, DIL_Q_BLOCK, axis=1)
        qpos = start + jnp.arange(DIL_Q_BLOCK, dtype=jnp.int32)
        return dilated_mixture(qb, kvs, qpos, bases, biases)

    starts = jnp.arange(S // DIL_Q_BLOCK, dtype=jnp.int32) * DIL_Q_BLOCK
    o = lax.map(block, starts)
    o = jnp.moveaxis(o, 0, 1).reshape(B, S, DIL_HPG, DIL_HEAD_DIM)
    new = [kv[:, S - min(w, S):] for kv, (w, _) in zip(kvs, DIL_GROUPS)]
    return o, new


def dilated_sample(q, k, v, caches, biases):
    T = q.shape[1]
    qpos = PAST_LEN + jnp.arange(T, dtype=jnp.int32)
    kvs, bases, new = [], [], []
    for g, buf in enumerate(caches):
        L = buf.shape[1]
        kv = jnp.concatenate([buf, jnp.stack([k[:, :, g], v[:, :, g]], axis=2)], axis=1)
        kvs.append(kv)
        bases.append(PAST_LEN - L)
        new.append(kv[:, kv.shape[1] - L:])
    return dilated_mixture(q, kvs, qpos, bases, biases), new


def causal_dwconv(x, buf, w):
    T, K = x.shape[1], w.shape[0]
    xp = jnp.concatenate([buf.astype(x.dtype), x], axis=1)
    y = xp[:, :T] * w[0]
    for j in range(1, K):
        y = y + xp[:, j:j + T] * w[j]
    return y, xp[:, T:]


def gated_delta_chunked(q, k, v, beta, g, s0):
    B, T, H, Dk = k.shape
    Dv = v.shape[-1]
    C = min(GDN_CHUNK, T)
    pad = (-T) % C
    if pad:
        pw = ((0, 0), (0, pad), (0, 0))
        q, k, v = (jnp.pad(a, pw + ((0, 0),)) for a in (q, k, v))
        beta, g = jnp.pad(beta, pw), jnp.pad(g, pw)
    N = (T + pad) // C

    def blk(a):
        return jnp.moveaxis(a.reshape((B, N, C) + a.shape[2:]), 3, 1)

    qc, kc, vc, bc, gc = (blk(a) for a in (q, k, v, beta, g))
    G = jnp.cumsum(gc, axis=-1)
    ii = jnp.arange(C)
    incl = ii[:, None] >= ii[None, :]
    strict = ii[:, None] > ii[None, :]
    gamma = jnp.exp(jnp.where(incl, G[..., :, None] - G[..., None, :], -jnp.inf))
    kk = jnp.einsum('bhnck,bhnek->bhnce', kc, kc)
    m = jnp.where(strict, bc[..., :, None] * kk * gamma, 0.0)
    rhs = jnp.concatenate([vc * bc[..., None], kc * (bc * jnp.exp(G))[..., None]], axis=-1)
    sol = lax.linalg.triangular_solve(m, rhs, left_side=True, lower=True, unit_diagonal=True)
    u0, w = sol[..., :Dv], sol[..., Dv:]
    a_intra = jnp.einsum('bhnck,bhnek->bhnce', qc, kc) * gamma
    q_dec = qc * jnp.exp(G)[..., None]
    k_dec = kc * jnp.exp(G[..., -1:] - G)[..., None]
    decay_tot = jnp.exp(G[..., -1])
    xs = tuple(jnp.moveaxis(a, 2, 0) for a in (u0, w, a_intra, q_dec, k_dec, decay_tot))

    def step(s, inp):
        u0_n, w_n, a_n, qd_n, kd_n, dt_n = inp
        v_new = u0_n - jnp.einsum('bhck,bhkv->bhcv', w_n, s)
        o_n = jnp.einsum('bhck,bhkv->bhcv', qd_n, s) + jnp.einsum('bhce,bhev->bhcv', a_n, v_new)
        s = s * dt_n[..., None, None] + jnp.einsum('bhck,bhcv->bhkv', kd_n, v_new)
        return s, o_n

    s_fin, o = lax.scan(step, s0, xs)
    o = jnp.transpose(o, (1, 0, 3, 2, 4)).reshape(B, N * C, H, Dv)[:, :T]
    return o, s_fin


def gdn_branch(qkv, z, beta_raw, a_raw, conv_buf, s0, w_conv, a_log, dt_bias, norm_out):
    B, T, _ = qkv.shape
    hs = (B, T, GDN_HEADS, GDN_HEAD_DIM)
    qkv_c, conv_new = causal_dwconv(qkv, conv_buf, w_conv)
    q, k, v = jnp.split(jax.nn.silu(qkv_c.astype(jnp.float32)), 3, axis=-1)
    q = l2norm(q.reshape(hs)) * (GDN_HEAD_DIM ** -0.5)
    k = l2norm(k.reshape(hs))
    beta = jax.nn.sigmoid(beta_raw.astype(jnp.float32))
    g = -jnp.exp(a_log.astype(jnp.float32)) * jax.nn.softplus(a_raw.astype(jnp.float32) + dt_bias.astype(jnp.float32))
    o, s_new = gated_delta_chunked(q, k, v.reshape(hs), beta, g, s0.astype(jnp.float32))
    o = rmsnorm(o, norm_out) * jax.nn.silu(z.astype(jnp.float32).reshape(hs))
    return o.reshape(B, T, GDN_W).astype(qkv.dtype), conv_new, s_new.astype(s0.dtype)


def token_mix(u, dilated_fn, conv_buf, delta_state, w_in, w_conv_delta, a_log, dt_bias, norm_delta_out,
              w_branch_a, w_branch_b, w_out):
    B, T, _ = u.shape
    qa, ka, va, qkv_b, z_b, beta_raw, a_raw, gate_a, gate_b = jnp.split(u @ w_in, SPLIT_IDX, axis=-1)
    ashape = (B, T, len(DIL_GROUPS), DIL_HPG, DIL_HEAD_DIM)
    o_a, dil_state = dilated_fn(qa.reshape(ashape), ka.reshape(ashape), va.reshape(ashape))
    o_b, conv_new, delta_new = gdn_branch(qkv_b, z_b, beta_raw, a_raw, conv_buf, delta_state, w_conv_delta,
                                          a_log, dt_bias, norm_delta_out)
    merged = (jax.nn.sigmoid(gate_a) * (o_a.reshape(B, T, DIL_HPG * DIL_HEAD_DIM) @ w_branch_a)
              + jax.nn.sigmoid(gate_b) * (o_b @ w_branch_b))
    return merged @ w_out, dil_state, conv_new, delta_new


def memory_kv(mem, norm_mem_kv, w_mem_kv):
    B, M, _ = mem.shape
    k, v = jnp.split(rmsnorm(mem, norm_mem_kv) @ w_mem_kv, 2, axis=-1)
    return k.reshape(B, M, MEM_HEADS, MEM_HEAD_DIM), v.reshape(B, M, MEM_HEADS, MEM_HEAD_DIM)


def memory_attend(u, mem_k, mem_v, w_mem_q, w_mem_o):
    B, T, _ = u.shape
    q = (u @ w_mem_q).reshape(B, T, MEM_HEADS, MEM_HEAD_DIM)
    s = jnp.einsum('bthd,bmhd->bhtm', q, mem_k.astype(q.dtype)).astype(jnp.float32) * (MEM_HEAD_DIM ** -0.5)
    p = jax.nn.softmax(s, axis=-1).astype(q.dtype)
    o = jnp.einsum('bhtm,bmhd->bthd', p, mem_v.astype(q.dtype)).reshape(B, T, MEM_W)
    return o @ w_mem_o


def conv_ffn(u, conv_buf, w_up, w_conv, b_conv, w_down):
    gate, up = jnp.split(u @ w_up, 2, axis=-1)
    gate_c, conv_new = causal_dwconv(gate, conv_buf, w_conv)
    return (jax.nn.silu(gate_c + b_conv) * up) @ w_down, conv_new


def decoder_layer(h, dilated_fn, delta_conv_buf, delta_state, mem_k, mem_v, ffn_conv_buf,
                  norm_mix, w_in, w_conv_delta, a_log, dt_bias, norm_delta_out, w_branch_a, w_branch_b, w_out,
                  norm_mem_q, w_mem_q, w_mem_o, norm_ffn, w_ffn_up, w_ffn_conv, b_ffn_conv, w_ffn_down):
    mix, dil_state, dconv_new, delta_new = token_mix(rmsnorm(h, norm_mix), dilated_fn, delta_conv_buf, delta_state,
                                                     w_in, w_conv_delta, a_log, dt_bias, norm_delta_out,
                                                     w_branch_a, w_branch_b, w_out)
    h = h + mix
    h = h + memory_attend(rmsnorm(h, norm_mem_q), mem_k, mem_v, w_mem_q, w_mem_o)
    f, fconv_new = conv_ffn(rmsnorm(h, norm_ffn), ffn_conv_buf, w_ffn_up, w_ffn_conv, b_ffn_conv, w_ffn_down)
    return h + f, dil_state, dconv_new, delta_new, fconv_new


def setup_inputs(seed: int = 0) -> dict:
    key = jax.random.key(seed)
    ks = iter(jax.random.split(key, 48))
    nrm = lambda shape, scale: scale * jax.random.normal(next(ks), shape, jnp.float32)
    gain = lambda shape: 1.0 + 0.02 * jax.random.normal(next(ks), shape, jnp.float32)
    L = DEPTH
    dl = [min(w, PAST_LEN) for w, _ in DIL_GROUPS]
    a_log = jnp.log(jax.random.uniform(next(ks), (L, GDN_HEADS), jnp.float32, 1.0, 16.0))
    dt = jnp.exp(jax.random.uniform(next(ks), (L, GDN_HEADS), jnp.float32, math.log(1e-3), math.log(1e-1)))
    dt_bias = dt + jnp.log(-jnp.expm1(-dt))
    return {
        'x_prompt': nrm((BATCH, SEQ, D_MODEL), 1.0),
        'x_sample': nrm((DEC_BATCH, DEC_SEQ, D_MODEL), 1.0),
        'cache_dil0_kv': nrm((L, DEC_BATCH, dl[0], 2, DIL_HPG, DIL_HEAD_DIM), 1.0),
        'cache_dil1_kv': nrm((L, DEC_BATCH, dl[1], 2, DIL_HPG, DIL_HEAD_DIM), 1.0),
        'cache_dil2_kv': nrm((L, DEC_BATCH, dl[2], 2, DIL_HPG, DIL_HEAD_DIM), 1.0),
        'state_delta': nrm((L, DEC_BATCH, GDN_HEADS, GDN_HEAD_DIM, GDN_HEAD_DIM), 0.1),
        'state_delta_conv': nrm((L, DEC_BATCH, GDN_CONV - 1, GDN_QKV_W), 1.0),
        'cache_mem_k': nrm((L, DEC_BATCH, MEM_LEN, MEM_HEADS, MEM_HEAD_DIM), 1.0),
        'cache_mem_v': nrm((L, DEC_BATCH, MEM_LEN, MEM_HEADS, MEM_HEAD_DIM), 1.0),
        'state_ffn_conv': nrm((L, DEC_BATCH, FFN_CONV - 1, FFN_DIM), 1.0),
        'mem_prompt': nrm((BATCH, MEM_LEN, D_MODEL), 1.0),
        'rel_bias': nrm((REL_BUCKETS, DIL_N_HEADS), 0.2),
        'norm_mix': gain((L, D_MODEL)),
        'w_in': nrm((L, D_MODEL, N_IN), D_MODEL ** -0.5),
        'w_conv_delta': nrm((L, GDN_CONV, GDN_QKV_W), GDN_CONV ** -0.5),
        'a_log': a_log,
        'dt_bias': dt_bias,
        'norm_delta_out': gain((L, GDN_HEAD_DIM)),
        'w_branch_a': nrm((L, DIL_HPG * DIL_HEAD_DIM, D_MODEL), (DIL_HPG * DIL_HEAD_DIM) ** -0.5),
        'w_branch_b': nrm((L, GDN_W, D_MODEL), GDN_W ** -0.5),
        'w_out': nrm((L, D_MODEL, D_MODEL), D_MODEL ** -0.5),
        'norm_mem_q': gain((L, D_MODEL)),
        'norm_mem_kv': gain((L, D_MODEL)),
        'w_mem_q': nrm((L, D_MODEL, MEM_W), D_MODEL ** -0.5),
        'w_mem_kv': nrm((L, D_MODEL, 2 * MEM_W), D_MODEL ** -0.5),
        'w_mem_o': nrm((L, MEM_W, D_MODEL), MEM_W ** -0.5),
        'norm_ffn': gain((L, D_MODEL)),
        'w_ffn_up': nrm((L, D_MODEL, 2 * FFN_DIM), D_MODEL ** -0.5),
        'w_ffn_conv': nrm((L, FFN_CONV, FFN_DIM), FFN_CONV ** -0.5),
        'b_ffn_conv': nrm((L, FFN_DIM), 0.02),
        'w_ffn_down': nrm((L, FFN_DIM, D_MODEL), FFN_DIM ** -0.5),
        'norm_final': gain((D_MODEL,)),
    }


def reference(x_prompt, x_sample, cache_dil0_kv, cache_dil1_kv, cache_dil2_kv, state_delta, state_delta_conv,
              cache_mem_k, cache_mem_v, state_ffn_conv, mem_prompt, rel_bias, norm_mix, w_in, w_conv_delta,
              a_log, dt_bias, norm_delta_out, w_branch_a, w_branch_b, w_out, norm_mem_q, norm_mem_kv, w_mem_q,
              w_mem_kv, w_mem_o, norm_ffn, w_ffn_up, w_ffn_conv, b_ffn_conv, w_ffn_down, norm_final):
    biases = dilation_biases(rel_bias)
    B, dt = x_prompt.shape[0], x_prompt.dtype
    zero_dconv = jnp.zeros((B, GDN_CONV - 1, GDN_QKV_W), dt)
    zero_delta = jnp.zeros((B, GDN_HEADS, GDN_HEAD_DIM, GDN_HEAD_DIM), dt)
    zero_fconv = jnp.zeros((B, FFN_CONV - 1, FFN_DIM), dt)
    h_p, h_s = x_prompt, x_sample
    p_states, s_states = [], []
    for l in range(DEPTH):
        lw = (norm_mix[l], w_in[l], w_conv_delta[l], a_log[l], dt_bias[l], norm_delta_out[l], w_branch_a[l],
              w_branch_b[l], w_out[l], norm_mem_q[l], w_mem_q[l], w_mem_o[l], norm_ffn[l], w_ffn_up[l],
              w_ffn_conv[l], b_ffn_conv[l], w_ffn_down[l])
        mk_p, mv_p = memory_kv(mem_prompt, norm_mem_kv[l], w_mem_kv[l])
        h_p, dil_p, dconv_p, delta_p, fconv_p = decoder_layer(
            h_p, lambda q, k, v: dilated_prompt(q, k, v, biases), zero_dconv, zero_delta, mk_p, mv_p, zero_fconv, *lw)
        p_states.append((dil_p[0], dil_p[1], dil_p[2], delta_p, dconv_p, mk_p, mv_p, fconv_p))
        caches_l = (cache_dil0_kv[l], cache_dil1_kv[l], cache_dil2_kv[l])
        h_s, dil_s, dconv_s, delta_s, fconv_s = decoder_layer(
            h_s, lambda q, k, v, c=caches_l: dilated_sample(q, k, v, c, biases), state_delta_conv[l],
            state_delta[l], cache_mem_k[l], cache_mem_v[l], state_ffn_conv[l], *lw)
        s_states.append((dil_s[0], dil_s[1], dil_s[2], delta_s, dconv_s, fconv_s))
    y_prompt = rmsnorm(h_p, norm_final)
    y_sample = rmsnorm(h_s, norm_final)
    (dil0_p, dil1_p, dil2_p, delta_p, dconv_p, mem_k_p, mem_v_p, fconv_p) = [jnp.stack(t, axis=0) for t in zip(*p_states)]
    (dil0_s, dil1_s, dil2_s, delta_s, dconv_s, fconv_s) = [jnp.stack(t, axis=0) for t in zip(*s_states)]
    return (y_prompt, y_sample, dil0_p, dil1_p, dil2_p, delta_p, dconv_p, mem_k_p, mem_v_p, fconv_p,
            dil0_s, dil1_s, dil2_s, delta_s, dconv_s, fconv_s)
```

```python
import math
import os
from contextlib import ExitStack
import numpy as np
import concourse.bass as bass
import concourse.mybir as mybir
from concourse.bass_utils import run_bass_kernel_spmd

F32 = mybir.dt.float32
BF16 = mybir.dt.bfloat16
AF = mybir.ActivationFunctionType
ALU = mybir.AluOpType
AX = mybir.AxisListType

ENGS = ("pe", "act", "dve", "pool", "sp")
N_DMA_SLOTS = 40


class Op:
    __slots__ = ("eng", "fn", "reads", "writes", "dma", "idx", "waits", "inc", "slot", "slot_val", "cnt_val")

    def __init__(self, eng, fn, reads, writes, dma):
        self.eng, self.fn, self.reads, self.writes, self.dma = eng, fn, tuple(reads), tuple(writes), dma
        self.waits = []
        self.inc = False
        self.slot = None
        self.slot_val = 0
        self.cnt_val = 0


class Sched:
    def __init__(self, nc):
        self.nc = nc
        self.ops = []

    def add(self, eng, fn, reads=(), writes=(), dma=False):
        writes = list(writes) + [k for k in reads if isinstance(k, tuple) and k[0] == "ps" and k not in writes]
        op = Op(eng, fn, reads, writes, dma)
        op.idx = len(self.ops)
        self.ops.append(op)
        return op

    def pe(self, fn, r=(), w=()):
        return self.add("pe", fn, r, w)

    def act(self, fn, r=(), w=()):
        return self.add("act", fn, r, w)

    def dve(self, fn, r=(), w=()):
        return self.add("dve", fn, r, w)

    def pool(self, fn, r=(), w=()):
        return self.add("pool", fn, r, w)

    def on(self, eng, fn, r=(), w=()):
        return self.add(eng, fn, r, w)

    def dma(self, fn, r=(), w=(), q="sp"):
        return self.add(q, fn, r, w, dma=True)

    def finalize(self, stack):
        nc = self.nc
        ops = self.ops
        last_w = {}
        readers = {}
        deps_of = []
        for op in ops:
            deps = {}
            for k in op.reads:
                lw = last_w.get(k)
                if lw is not None:
                    deps[lw.idx] = "raw"
            for k in op.writes:
                lw = last_w.get(k)
                if lw is not None and lw.idx not in deps:
                    deps[lw.idx] = "waw"
                for rd in readers.get(k, ()):
                    if rd.idx not in deps:
                        deps[rd.idx] = "war"
            for k in op.reads:
                readers.setdefault(k, []).append(op)
            for k in op.writes:
                last_w[k] = op
                readers[k] = []
            need = []
            for di, kind in deps.items():
                d = ops[di]
                if d is op:
                    continue
                if d.dma or op.dma:
                    need.append(d)
                elif d.eng != op.eng:
                    need.append(d)
                elif kind == "raw" and op.eng != "pe":
                    need.append(d)
            deps_of.append(need)
        slot_last = [None] * N_DMA_SLOTS
        slot_cnt = [0] * N_DMA_SLOTS
        nd = 0
        for op in ops:
            if op.dma:
                s = nd % N_DMA_SLOTS
                nd += 1
                op.slot = s
                slot_cnt[s] += 16
                op.slot_val = slot_cnt[s]
                if slot_last[s] is not None:
                    deps_of[op.idx].append(slot_last[s])
                slot_last[s] = op
                op.inc = True
        for op in ops:
            for d in deps_of[op.idx]:
                d.inc = True
        cnt = {e: 0 for e in ENGS}
        for op in ops:
            if not op.dma and op.inc:
                cnt[op.eng] += 1
                op.cnt_val = cnt[op.eng]
        sems = {e: stack.enter_context(nc.semaphore("c_" + e)) for e in ENGS}
        dsems = [stack.enter_context(nc.semaphore("d_%d" % i)) for i in range(N_DMA_SLOTS)]
        waited = {e: {} for e in ENGS}
        nw = 0
        for op in ops:
            wl = {}
            for d in deps_of[op.idx]:
                if d.dma:
                    key, val = ("d", d.slot), d.slot_val
                else:
                    key, val = ("c", d.eng), d.cnt_val
                if val > wl.get(key, 0):
                    wl[key] = val
            wd = waited[op.eng]
            for key, val in wl.items():
                if wd.get(key, 0) >= val:
                    continue
                wd[key] = val
                sem = dsems[key[1]] if key[0] == "d" else sems[key[1]]
                op.waits.append((sem, val))
                nw += 1
        self.stats = dict(n_ops=len(ops), n_waits=nw, n_dma=nd,
                          per_eng={e: sum(1 for o in ops if o.eng == e) for e in ENGS})
        by_eng = {e: [o for o in ops if o.eng == e] for e in ENGS}
        final_slots = [(dsems[s], slot_cnt[s]) for s in range(N_DMA_SLOTS) if slot_cnt[s] > 0]

        def emit(e, eng):
            for op in by_eng[e]:
                for sem, val in op.waits:
                    eng.wait_ge(sem, val)
                ins = op.fn(eng)
                if op.inc:
                    if op.dma:
                        ins.then_inc(dsems[op.slot], 16)
                    else:
                        ins.then_inc(sems[e], 1)
            if e == "sp":
                for sem, val in final_slots:
                    eng.wait_ge(sem, val)

        block = stack.enter_context(nc.Block())

        @block.sync
        def _(eng):
            emit("sp", eng)

        @block.tensor
        def _(eng):
            emit("pe", eng)

        @block.scalar
        def _(eng):
            emit("act", eng)

        @block.vector
        def _(eng):
            emit("dve", eng)

        @block.gpsimd
        def _(eng):
            emit("pool", eng)


D = 1024
NP_ = 2048
NS_ = 16
NT_ = NP_ + NS_
NTILE = 16
N_IN = 8464
FFN = 2816
NEG = -30000.0
EPS = 1e-6
DILS = (1, 4, 16)
WINS = (128, 512, 2048)
C_QA, C_KA, C_VA, C_QKV, C_Z, C_BETA, C_A, C_GA, C_GB = 0, 768, 1536, 2304, 5376, 6400, 6408, 6416, 7440

CST = {}
_o = 0
for _n in ("ident", "U", "V", "negS", "selA", "selB", "sel3", "ones"):
    CST[_n] = _o
    _o += 128
CST["mbd4"] = _o
_o += 256
CST["mbdm"] = _o
_o += 512
CST_W = _o

PRM = {}
_o = 0
for _n, _w in (("gT", 32), ("wcd", 96), ("ndo", 1), ("wfc", 66), ("bfc", 22)):
    PRM[_n] = _o
    _o += _w
PRM_W = _o


def _rel_bucket_np(dist):
    d = np.maximum(dist, 1).astype(np.float32)
    large = 16 + (np.log(d / np.float32(16)) / np.float32(math.log(2048 / 16)) * np.float32(16)).astype(np.int32)
    return np.where(dist < 16, dist, np.minimum(large, 31))


def host_constants(rel_bias):
    c = np.zeros((128, CST_W), np.float32)
    t = np.arange(128)
    same = (t[:, None] // 64) == (t[None, :] // 64)
    c[:, CST["ident"]:CST["ident"] + 128] = np.eye(128)
    c[:, CST["U"]:CST["U"] + 128] = (t[:, None] <= t[None, :]) & same
    c[:, CST["V"]:CST["V"] + 128] = (t[:, None] > t[None, :]) & same
    c[:, CST["negS"]:CST["negS"] + 128] = np.where((t[None, :] > t[:, None]) & same, 0.0, NEG)
    c[63, CST["selA"]:CST["selA"] + 128] = 1
    c[127, CST["selB"]:CST["selB"] + 128] = 1
    c[3, CST["sel3"]:CST["sel3"] + 128] = 1
    c[:, CST["ones"]:CST["ones"] + 128] = 1
    for h in range(4):
        c[h, CST["mbd4"] + 64 * h:CST["mbd4"] + 64 * (h + 1)] = 1
        c[h, CST["mbdm"] + 128 * h:CST["mbdm"] + 128 * (h + 1)] = 1
    bias = []
    for g in range(3):
        idx = _rel_bucket_np(DILS[g] * np.arange(129, dtype=np.int32))
        bias.append(rel_bias[idx][:, 4 * g:4 * g + 4].T.astype(np.float32))
    tb = np.full((128, 12, 2, 128), NEG, np.float32)
    mp = t[:, None]
    m = t[None, :]
    for g in range(3):
        for h in range(4):
            dd = np.where(m >= mp, bias[g][h][np.clip(m - mp, 0, 128)], NEG)
            pv = np.where(mp >= m, bias[g][h][np.clip(m - mp + 128, 0, 128)], NEG)
            tb[:, 4 * g + h, 0, :] = dd
            tb[:, 4 * g + h, 1, :] = pv
    tbs = np.full((128, 3, 4, 4), NEG, np.float32)
    tbn = np.full((4, 3, 4, 4), NEG, np.float32)
    for g in range(3):
        for tq in range(4):
            for h in range(4):
                col = bias[g][h][128 - t]
                if g == 0:
                    col = np.where(t <= 127 - tq, col, NEG)
                tbs[:, g, tq, h] = col
                for tp in range(4):
                    if g == 0:
                        if tp <= tq:
                            tbn[tp, g, tq, h] = bias[0][h][tq - tp]
                    elif tp == tq:
                        tbn[tp, g, tq, h] = bias[g][h][0]
    return c, tb, tbs.reshape(128, 48), tbn.reshape(4, 48)


def build_program(stop_after=None, dbg=()):
    nc = bass.Bass("TRN2", target_bir_lowering=False)
    st = ExitStack()

    def din(name, shape, dt=F32):
        return nc.dram_tensor(name, list(shape), dt, kind="ExternalInput").ap()

    def dout(name, shape, dt=F32):
        return nc.dram_tensor(name, list(shape), dt, kind="ExternalOutput").ap()

    def dscr(name, shape, dt=F32):
        return nc.dram_tensor(name, list(shape), dt, kind="Internal").ap()

    x_p = din("x_p", [NP_, D])
    x_s = din("x_s", [NS_, D])
    cache = [din("cache%d" % g, [4, WINS[g], 512]) for g in range(3)]
    s0_in = din("s0", [4, 8, 128, 128])
    dconv_in = din("dconv_in", [12, 3072])
    cmk = din("cmk", [4, 256, 512])
    cmv = din("cmv", [4, 256, 512])
    fconv_in = din("fconv_in", [8, FFN])
    mem_p = din("mem_p", [256, D])
    w_in = din("w_in", [D, N_IN])
    w_ba = din("w_ba", [256, D])
    w_bb = din("w_bb", [D, D])
    w_out = din("w_out", [D, D])
    w_mq = din("w_mq", [D, 512])
    w_mkv = din("w_mkv", [D, D])
    w_mo = din("w_mo", [512, D])
    w_up = din("w_up", [D, 2 * FFN])
    w_dn = din("w_dn", [FFN, D])
    cst_d = din("cst", [128, CST_W])
    prm_d = din("prm", [128, PRM_W])
    tb_d = din("tb", [128, 12 * 2 * 128])
    tbs_d = din("tbs", [128, 48])
    tbn_d = din("tbn", [4, 48])
    alog_d = din("alog", [8])
    dtb_d = din("dtb", [8])
    gfin_d = din("gfin", [D])

    y_p = dout("y_p", [NP_, D])
    y_s = dout("y_s", [NS_, D])
    dil_p = [dout("dil%d_p" % g, [WINS[g], 512]) for g in range(3)]
    delta_p = dout("delta_p", [8, 128, 128])
    dconv_p = dout("dconv_p", [3, 3072])
    memk_p = dout("memk_p", [256, 512])
    memv_p = dout("memv_p", [256, 512])
    fconv_p = dout("fconv_p", [2, FFN])
    dil_s = [dout("dil%d_s" % g, [4, WINS[g], 512]) for g in range(3)]
    delta_s = dout("delta_s", [4, 8, 128, 128])
    dconv_s = dout("dconv_s", [4, 3, 3072])
    fconv_s = dout("fconv_s", [4, 2, FFN])

    hscr = dscr("hscr", [NT_, D])
    qscr = dscr("qscr", [NS_, 768])
    qmscr = dscr("qmscr", [NS_, 512])
    wbf = dscr("wbf", [D, 4096], BF16)
    oa_scr = dscr("oa_scr", [64, 4 * NT_], BF16)
    ob_scr = dscr("ob_scr", [128, 8 * NT_], BF16)
    dbg_out = {}

    with st:
        S = Sched(nc)
        sbt = lambda name, shape, dt: st.enter_context(nc.sbuf_tensor(name, shape, dt))
        nT = sbt("nT", [128, 8, NT_], BF16)
        cstt = sbt("cst_sb", [128, CST_W], F32)
        prm = sbt("prm_sb", [128, PRM_W], F32)
        cbf = sbt("cbf", [128, 3, 128], BF16)
        gfin = sbt("gfin_sb", [128, D], F32)
        ARENA_W = 41700
        arena = sbt("arena", [128, ARENA_W], F32)
        psum = st.enter_context(nc.psum_tensor("psum", [128, 8, 512], F32))

        class Arena:
            def __init__(self):
                self.off = 0

            def reset(self):
                self.off = 0

            def alloc(self, shape, dt, parts=128):
                elems = int(np.prod(shape[1:]))
                words = (elems * (2 if dt == BF16 else 4) + 3) // 4
                words += words % 2
                assert self.off + words <= ARENA_W, ("arena overflow", self.off, words)
                v = arena[0:shape[0], self.off:self.off + words]
                self.off += words
                if dt == BF16:
                    v = v.bitcast(BF16)
                v = v[:, 0:elems]
                if len(shape) == 3:
                    v = v.rearrange("p (a b) -> p a b", a=shape[1])
                elif len(shape) == 4:
                    v = v.rearrange("p (a b c) -> p a b c", a=shape[1], b=shape[2])
                elif len(shape) == 5:
                    v = v.rearrange("p (a b c d) -> p a b c d", a=shape[1], b=shape[2], c=shape[3])
                return v

        AR = Arena()
        oaT = AR.alloc([64, 4, NT_], BF16)
        base_mark = AR.off

        def barrier():
            keys = set()
            for op in S.ops:
                keys.update(op.reads)
                keys.update(op.writes)
            keys = sorted(keys, key=str)
            oc = CST["ones"]
            S.dve(lambda e: e.memset(cbf[0:1, 2, 0:2], 1.0), r=[], w=keys + ["_bar"])
            S.act(lambda e: e.copy(out=cstt[0:1, oc:oc + 2], in_=cstt[0:1, oc + 2:oc + 4]), r=["_bar"], w=["_bar2"])
            S.pool(lambda e: e.memset(cbf[32:33, 2, 0:2], 1.0), r=["_bar"], w=["_bar3"])
            S.dma(lambda e: e.dma_start(out=cstt[64:65, oc:oc + 2], in_=cstt[64:65, oc + 2:oc + 4]), r=["_bar"], w=["_bar4"])

        def C(name, w=128, rows=128):
            return cstt[0:rows, CST[name]:CST[name] + w]

        ident_f = C("ident")
        ident_b = cbf[:, 0, :]
        negS_b = cbf[:, 1, :]
        ones_b = cbf[:, 2, :]

        def bank(i):
            return psum[:, i, :]

        def bank_bf(i):
            return psum[:, i, :].bitcast(BF16)

        def MM(out, lhsT, rhs, r, w, start=True, stop=True, **kw):
            S.pe(lambda e: e.matmul(out, lhsT=lhsT, rhs=rhs, start=start, stop=stop, **kw), r, w)

        def TR(out, in_, ident, r, w):
            S.pe(lambda e: e.transpose(out=out, in_=in_, identity=ident), r, w)

        def ACT(out, in_, func, r, w, **kw):
            S.act(lambda e: e.activation(out=out, in_=in_, func=func, **kw), r, w)

        def CP(eng, out, in_, r, w):
            if eng == "act":
                S.act(lambda e: e.copy(out=out, in_=in_), r, w)
            else:
                S.on(eng, lambda e: e.tensor_copy(out=out, in_=in_), r, w)

        def TT(eng, out, in0, in1, op, r, w):
            S.on(eng, lambda e: e.tensor_tensor(out=out, in0=in0, in1=in1, op=op), r, w)

        def TS(eng, out, in0, s1, op0, r, w, s2=None, op1=None):
            if op1 is None:
                S.on(eng, lambda e: e.tensor_scalar(out=out, in0=in0, scalar1=s1, scalar2=None, op0=op0), r, w)
            else:
                S.on(eng, lambda e: e.tensor_scalar(out=out, in0=in0, scalar1=s1, scalar2=s2, op0=op0, op1=op1), r, w)

        def STT(out, in0, scalar, in1, op0, op1, r, w):
            S.dve(lambda e: e.scalar_tensor_tensor(out=out, in0=in0, scalar=scalar, in1=in1, op0=op0, op1=op1), r, w)

        def RED(out, in_, r, w):
            S.dve(lambda e: e.tensor_reduce(out=out, in_=in_, axis=AX.X, op=ALU.add), r, w)

        def DMA(out, in_, r, w, q="sp"):
            S.dma(lambda e: e.dma_start(out=out, in_=in_), r, w, q=q)

        def MSET(eng, ap, val, r, w):
            S.on(eng, lambda e: e.memset(ap, val), r, w)

        S.dma(lambda e: e.dma_start(out=cstt[:], in_=cst_d), w=["cst"])
        S.dma(lambda e: e.dma_start(out=prm[:], in_=prm_d), w=["prm"])
        S.dma(lambda e: e.dma_start(out=gfin[:], in_=gfin_d.partition_broadcast(128)), w=["gfin"])
        S.dve(lambda e: e.tensor_copy(out=cbf[:, 0, :], in_=C("ident")), r=["cst"], w=["cbf"])
        S.dve(lambda e: e.tensor_copy(out=cbf[:, 1, :], in_=C("negS")), r=["cst"], w=["cbf"])
        S.dve(lambda e: e.tensor_copy(out=cbf[:, 2, :], in_=C("ones")), r=["cst"], w=["cbf"])
        gT = prm[:, PRM["gT"]:PRM["gT"] + 32].rearrange("p (a b) -> p a b", a=4)

        def norm_tile_to_nT(i, xt, xkey, rows, gidx, tmp, pbank, tag, dst=None, dkey=None, defer=False):
            junk, ssq, rstd, ub = tmp
            c0 = i * 128
            if dst is None:
                dst = nT[:, :, c0:c0 + rows]
                dkey = ("nT", i)
            S.act(lambda e: e.activation(out=junk[0:rows, :], in_=xt[0:rows, :], func=AF.Square, accum_out=ssq[0:rows, :]),
                  r=[xkey], w=[tag + "junk", tag + "ssq"])
            S.act(lambda e: e.activation(out=rstd[0:rows, :], in_=ssq[0:rows, :], func=AF.Ln, scale=1.0 / D, bias=EPS),
                  r=[tag + "ssq"], w=[tag + "rstd"])
            S.act(lambda e: e.activation(out=rstd[0:rows, :], in_=rstd[0:rows, :], func=AF.Exp, scale=-0.5),
                  r=[tag + "rstd"], w=[tag + "rstd"])
            S.dve(lambda e: e.tensor_scalar(out=ub[0:rows, :], in0=xt[0:rows, :], scalar1=rstd[0:rows, 0:1], scalar2=None, op0=ALU.mult),
                  r=[xkey, tag + "rstd"], w=[tag + "ub"])
            def part2():
                pb = bank_bf(pbank)[:, 0:1024].rearrange("p (a b) -> p a b", a=8)
                for kc in range(8):
                    TR(pb[:, kc, 0:rows], ub[0:rows, kc * 128:(kc + 1) * 128], ident_b[0:rows, 0:rows], [tag + "ub", "cbf"], [("ps", pbank)])
                TT("dve", dst, pb[:, :, 0:rows], gT[:, gidx, :].unsqueeze(2).to_broadcast([128, 8, rows]), ALU.mult, [("ps", pbank), "prm"], [dkey])
            if defer:
                return part2
            part2()
            return rstd

        AR.off = base_mark
        xts = [AR.alloc([128, D], F32) for _ in range(4)]
        tmps = [(AR.alloc([128, D], BF16), AR.alloc([128, 1], F32), AR.alloc([128, 1], F32), AR.alloc([128, D], BF16)) for _ in range(4)]
        pend1 = None
        for i in range(NTILE + 1):
            rows = 128 if i < NTILE else NS_
            b = i % 4
            src = x_p[i * 128:(i + 1) * 128, :] if i < NTILE else x_s
            DMA(xts[b][0:rows, :], src, [], [("xt", b)])
            nxt1 = norm_tile_to_nT(i, xts[b], ("xt", b), rows, 0, tmps[b], b, "n1_%d_" % b, defer=True)
            if pend1 is not None:
                pend1()
            pend1 = nxt1
        pend1()

        if "nT" in dbg:
            dbg_out["nT"] = dout("dbg_nT", [128, 8 * NT_], BF16)
            S.dma(lambda e: e.dma_start(out=dbg_out["nT"], in_=nT[:].rearrange("p a b -> p (a b)")), r=[("nT", i) for i in range(17)], w=["dbg_nT"])
        if stop_after == 1:
            S.finalize(st)
            return nc, S, dbg_out

        barrier()
        AR.off = base_mark
        qTa = AR.alloc([128, 6, NT_], BF16)
        kTa = AR.alloc([128, 6, NT_], BF16)
        va = AR.alloc([128, 3, 16, 4, 66], BF16)
        qs_tok = AR.alloc([16, 768], F32)
        kvnew = [AR.alloc([4, 3, 512], F32) for _ in range(4)]
        kvnb = [AR.alloc([4, 3, 256], BF16) for _ in range(4)]
        tbs = AR.alloc([128, 48], F32)
        tbn = AR.alloc([4, 48], F32)
        a_mark = AR.off
        wqk = [AR.alloc([128, 8, 128], BF16) for _ in range(3)]
        wkv = [AR.alloc([128, 8, 512], BF16) for _ in range(2)]
        kvo = [AR.alloc([128, 512], F32) for _ in range(2)]
        DMA(tbs[:], tbs_d, [], ["tbs"])
        DMA(tbn[:], tbn_d, [], ["tbn"])
        for g in range(3):
            for s_ in range(4):
                DMA(dil_s[g][s_, 0:WINS[g] - 4, :], cache[g][s_, 4:WINS[g], :], [], [("dil_s_old", g, s_)])
        w_in_v = w_in.rearrange("(kc p) n -> p kc n", p=128)
        SKIP = os.environ.get('DBG_SKIP', '').split(',')
        if 'memset' not in SKIP:
            S.pool(lambda e: e.memset(va[:].rearrange("p a b c d -> p (a b c d)"), 1.0), w=["va_init"])
        TB = [(0, 512), (512, 512), (1024, 512), (1536, 512), (2048, NS_)]
        rot = 0
        nw_ = 0
        for (dst, wkey, scale, c0) in ((qTa, "wq", 0.125, C_QA), (kTa, "wk", None, C_KA)):
            for c in range(6):
                wb = nw_ % 3
                nw_ += 1
                wt = wqk[wb]
                S.dma(lambda e, wt=wt, c=c, c0=c0: e.dma_start(out=wt[:], in_=w_in_v[:, :, c0 + c * 128:c0 + (c + 1) * 128]), w=[("wqk", wb)], q="pool")
                if wkey == "wq":
                    for kc in range(8):
                        MM(bank(2)[0:16, 0:128], nT[:, kc, NP_:NT_], wt[:, kc, :], [("wqk", wb), ("nT", 16)], [("ps", 2)], start=(kc == 0), stop=(kc == 7))
                    ACT(qs_tok[0:16, c * 128:(c + 1) * 128], bank(2)[0:16, 0:128], AF.Copy, [("ps", 2)], ["qs_tok"], scale=0.125)
                for (t0, tn) in TB:
                    pb = rot % 2
                    rot += 1
                    for kc in range(8):
                        S.pe(lambda e, kc=kc, pb=pb, t0=t0, tn=tn, wt=wt: e.matmul(bank(pb)[:, 0:tn], lhsT=wt[:, kc, :], rhs=nT[:, kc, t0:t0 + tn], start=(kc == 0), stop=(kc == 7)),
                             r=[("wqk", wb)] + [("nT", j) for j in range(t0 // 128, (t0 + tn + 127) // 128)], w=[("ps", pb)])
                    if scale is None:
                        S.act(lambda e, pb=pb, c=c, t0=t0, tn=tn, dst=dst: e.copy(out=dst[:, c, t0:t0 + tn], in_=bank(pb)[:, 0:tn]),
                              r=[("ps", pb)], w=[(wkey + "T", c, t0)])
                    else:
                        S.act(lambda e, pb=pb, c=c, t0=t0, tn=tn, dst=dst, scale=scale: e.activation(out=dst[:, c, t0:t0 + tn], in_=bank(pb)[:, 0:tn], func=AF.Copy, scale=scale),
                              r=[("ps", pb)], w=[(wkey + "T", c, t0)])
        if "qk" in dbg:
            dbg_out["qTa"] = dout("dbg_qTa", [128, 6 * NT_], BF16)
            dbg_out["kTa"] = dout("dbg_kTa", [128, 6 * NT_], BF16)
            S.dma(lambda e: e.dma_start(out=dbg_out["qTa"], in_=qTa[:].rearrange("p a b -> p (a b)")), r=[("wqT", c, t0) for c in range(6) for (t0, _) in TB], w=["dbg_q"])
            S.dma(lambda e: e.dma_start(out=dbg_out["kTa"], in_=kTa[:].rearrange("p a b -> p (a b)")), r=[("wkT", c, t0) for c in range(6) for (t0, _) in TB], w=["dbg_k"])
        if stop_after == 2.1:
            S.finalize(st)
            return nc, S, dbg_out
        qk_keys = lambda name, c: [(name + "T", c, t0) for (t0, _) in TB]

        def tok_ap(g, j):
            d = DILS[g]
            nb = NP_ // (128 * d)
            r, b = j // nb, j % nb
            start = d * 128 * b + r
            return start, d, r, b

        rot = 0
        for g in [int(v) for v in os.environ.get('DBG_G', '0,1,2').split(',')]:
            S.dma(lambda e, g=g: e.dma_start(out=wkv[g % 2][:, :, 0:256], in_=w_in_v[:, :, C_KA + 256 * g:C_KA + 256 * (g + 1)]), w=[("wkv", g % 2)], q="pool")
            S.dma(lambda e, g=g: e.dma_start(out=wkv[g % 2][:, :, 256:512], in_=w_in_v[:, :, C_VA + 256 * g:C_VA + 256 * (g + 1)]), w=[("wkv", g % 2)], q="pool")
            for s_ in range(4):
                for kc in range(8):
                    MM(bank(0)[0:4, :], nT[:, kc, NP_ + 4 * s_:NP_ + 4 * s_ + 4], wkv[g % 2][:, kc, :], [("wkv", g % 2), ("nT", 16)], [("ps", 0)], start=(kc == 0), stop=(kc == 7))
                CP("act", kvnew[s_][0:4, g, :], bank(0)[0:4, :], [("ps", 0)], [("kvnew", s_, g)])
                CP("dve", kvnb[s_][0:4, g, :], kvnew[s_][0:4, g, 256:512], [("kvnew", s_, g)], [("kvnb", s_, g)])
                DMA(dil_s[g][s_, WINS[g] - 4:WINS[g], :], kvnew[s_][0:4, g, :], [("kvnew", s_, g)], [("dil_s_new", g, s_)])
            for j in range(16):
                start, d, r, b = tok_ap(g, j)
                pb = 2 + rot % 2
                ko = kvo[rot % 2]
                kok = ("kvo", rot % 2)
                rot += 1
                for kc in range(8):
                    S.pe(lambda e, kc=kc, pb=pb, g=g, start=start, d=d: e.matmul(bank(pb)[:], lhsT=nT[:, kc, start:start + 127 * d + 1:d], rhs=wkv[g % 2][:, kc, :], start=(kc == 0), stop=(kc == 7)),
                         r=[("wkv", g % 2)] + [("nT", jj) for jj in range(16)], w=[("ps", pb)])
                if 'va' not in SKIP:
                  S.on(os.environ.get('DBG_VAENG', 'dve'), lambda e, pb=pb, g=g, j=j: (e.tensor_copy if os.environ.get('DBG_VAENG', 'dve') == 'dve' else e.copy)(out=va[:, g, j, :, 0:64], in_=bank(pb)[:, 256:512].rearrange("p (h d) -> p h d", h=4)),
                        r=[("ps", pb), "va_init"], w=[("va", g, j)])
                if d * 128 * b >= NP_ - WINS[g] and 'out' not in SKIP:
                    S.act(lambda e, pb=pb, ko=ko: e.copy(out=ko[:], in_=bank(pb)[:]), r=[("ps", pb)], w=[kok])
                    row0 = d * 128 * b + r - (NP_ - WINS[g])
                    S.dma(lambda e, ko=ko, g=g, row0=row0, d=d: e.dma_start(out=dil_p[g][row0:row0 + 127 * d + 1:d, :], in_=ko[:]), r=[kok], w=[("dil_p", g, j)])

        if stop_after == 2.2:
            S.finalize(st)
            return nc, S, dbg_out
        barrier()
        AR.off = a_mark
        for h in range(8):
            for qi, cw in enumerate((C_QKV + 128 * h, C_QKV + 1024 + 128 * h, C_QKV + 2048 + 128 * h, C_Z + 128 * h)):
                DMA(wbf[:, 512 * h + 128 * qi:512 * h + 128 * (qi + 1)], w_in[:, cw:cw + 128], [], [("wbf", h)], q="pool")
        tbh = AR.alloc([128, 12, 2, 128], BF16)
        tbl = AR.alloc([128, 12, 2, 128], BF16)
        m_tb = AR.off
        tb = AR.alloc([128, 12, 2, 128], F32)
        DMA(tb[:].rearrange("p a b c -> p (a b c)"), tb_d, [], ["tb"])
        tbf_ = tb[:].rearrange("p a b c -> p (a b c)")
        tbhf_ = tbh[:].rearrange("p a b c -> p (a b c)")
        tblf_ = tbl[:].rearrange("p a b c -> p (a b c)")
        CP("act", tbhf_, tbf_, ["tb"], ["tbh"])
        TT("dve", tblf_, tbf_, tbhf_, ALU.subtract, ["tb", "tbh"], ["tbl"])
        tb2s = AR.alloc([128, 4, 128], F32)
        CP("act", tb2s[:], tb[:, 8:12, 0, :], ["tb"], ["tb2s"])
        AR.off = m_tb
        tb2 = AR.alloc([128, 4, 128], F32)
        sc2 = AR.alloc([128, 512], F32)
        CP("act", tb2[:], tb2s[:], ["tb2s", "tbl", "tbh"], ["tb2"])
        p_sb = [AR.alloc([128, 512], BF16) for _ in range(3)]
        accs = [AR.alloc([65, 512], F32, parts=65) for _ in range(2)]
        ncb = 0
        nbt = 0
        for c in range(4):
            for h in range(4):
                blocks = []
                for i in range(4 * c, 4 * c + 4):
                    for kind, kt in ((0, i), (1, i - 1)):
                        if kt >= 0:
                            blocks.append((0, kt, kind, 128 * i, 1, 128, 128 * (i - 4 * c), 1))
                for r in range(4):
                    for kind, kb in ((0, c), (1, c - 1)):
                        if kb >= 0:
                            blocks.append((1, r * 4 + kb, kind, 512 * c + r, 4, 128, r, 4))
                for r in range(16):
                    blocks.append((2, r, 2, 512 * c + r, 16, 32, r, 16))
                batches = []
                cur, ccols, cg = [], 0, None
                for blk in blocks:
                    if cur and (blk[0] != cg or ccols + blk[5] > 512):
                        batches.append(cur)
                        cur, ccols = [], 0
                    cg = blk[0]
                    cur.append((blk, ccols))
                    ccols += blk[5]
                batches.append(cur)
                accb = 4 + ncb % 2
                acck = ("ps", accb)
                nbat = len(batches)

                def emit_score(bidx, batch, c=c, h=h):
                    sbk = (6, 7, 0, 1)[bidx % 4]
                    pbi = bidx % 3
                    g = batch[0][0][0]
                    nb_ = len(batch)
                    tot = batch[-1][1] + batch[-1][0][5]
                    for k_, (blk, off) in enumerate(batch):
                        _, kt, kind, qs, qst, nq, os_, ost = blk
                        gh = 4 * g + h
                        ch, r0 = gh // 2, 64 * (gh % 2)
                        ks, kd, _, _ = tok_ap(g, kt)
                        MM(bank(sbk)[:, off:off + nq], kTa[r0:r0 + 64, ch, ks:ks + 127 * kd + 1:kd], qTa[r0:r0 + 64, ch, qs:qs + (nq - 1) * qst + 1:qst],
                           qk_keys("wk", ch) + qk_keys("wq", ch), [("ps", sbk)], start=(k_ == 0), stop=(g == 2 and k_ == nb_ - 1), skip_group_check=True)
                    if g == 2:
                        TT("dve", sc2[:, 0:tot].rearrange("p (r q) -> p r q", q=32), bank(sbk)[:, 0:tot].rearrange("p (r q) -> p r q", q=32),
                           tb2[:, h, 32 * c:32 * c + 32].unsqueeze(1).to_broadcast([128, nb_, 32]), ALU.add, [("ps", sbk), "tb2"], ["sc2"])
                        ACT(p_sb[pbi][:, 0:tot], sc2[:, 0:tot], AF.Exp, ["sc2"], [("p", pbi)])
                        return
                    gh = 4 * g + h
                    runs = []
                    k_ = 0
                    while k_ < nb_:
                        blk, off = batch[k_]
                        if blk[2] == 0 and k_ + 1 < nb_ and batch[k_ + 1][0][2] == 1:
                            runs.append((off, 0, 2))
                            k_ += 2
                        else:
                            runs.append((off, blk[2], 1))
                            k_ += 1
                    nmm = 2 * len(runs)
                    im = 0
                    for (tsrc, tkey) in ((tbh, "tbh"), (tbl, "tbl")):
                        for (off, k0_, nk_) in runs:
                            im += 1
                            if nk_ == 2:
                                MM(bank(sbk)[:, off:off + 256].rearrange("p (a b) -> p a b", a=2), ident_b, tsrc[:, gh, :, :], ["cbf", tkey], [("ps", sbk)],
                                   start=False, stop=(im == nmm), skip_group_check=True)
                            else:
                                MM(bank(sbk)[:, off:off + 128], ident_b, tsrc[:, gh, k0_, :], ["cbf", tkey], [("ps", sbk)], start=False, stop=(im == nmm), skip_group_check=True)
                    ACT(p_sb[pbi][:, 0:tot], bank(sbk)[:, 0:tot], AF.Exp, [("ps", sbk)], [("p", pbi)])

                def emit_pv(bidx, batch, first, last, h=h, accb=accb, acck=acck):
                    pbi = bidx % 3
                    for k_, (blk, off) in enumerate(batch):
                        g, kt, kind, qs, qst, nq, os_, ost = blk
                        MM(bank(accb)[0:65, os_:os_ + (nq - 1) * ost + 1:ost], va[:, g, kt, h, 0:65], p_sb[pbi][:, off:off + nq],
                           [("p", pbi), ("va", g, kt)], [acck], start=(first and k_ == 0), stop=(last and k_ == len(batch) - 1), skip_group_check=True)

                LOOK = 2
                for bi in range(min(LOOK, nbat)):
                    emit_score(nbt + bi, batches[bi])
                for bi in range(nbat):
                    if bi + LOOK < nbat:
                        emit_score(nbt + bi + LOOK, batches[bi + LOOK])
                    emit_pv(nbt + bi, batches[bi], bi == 0, bi == nbat - 1)
                nbt += nbat
                ab = ncb % 2
                ncb += 1
                S.act(lambda e, ab=ab, accb=accb: e.copy(out=accs[ab][0:65, :], in_=bank(accb)[0:65, :]), r=[acck], w=[("accs", ab)])
                if "accs" in dbg:
                    if "accs" not in dbg_out:
                        dbg_out["accs"] = dout("dbg_accs", [16, 65, 512], F32)
                    S.dma(lambda e, ab=ab, c=c, h=h: e.dma_start(out=dbg_out["accs"][4 * c + h], in_=accs[ab][0:65, :]), r=[("accs", ab)], w=[("dbg_accs", c, h)])
                S.act(lambda e, ab=ab: e.activation(out=accs[ab][64:65, :], in_=accs[ab][64:65, :], func=AF.Ln), r=[("accs", ab)], w=[("accs", ab)])
                S.act(lambda e, ab=ab: e.activation(out=accs[ab][64:65, :], in_=accs[ab][64:65, :], func=AF.Exp, scale=-1.0), r=[("accs", ab)], w=[("accs", ab)])
                bcb = 2 + ab
                S.pe(lambda e, ab=ab, bcb=bcb: e.matmul(bank(bcb)[0:64, :], lhsT=cstt[64:65, CST["ones"]:CST["ones"] + 64], rhs=accs[ab][64:65, :], start=True, stop=True),
                     r=[("accs", ab), "cst"], w=[("ps", bcb)])
                S.dve(lambda e, ab=ab, bcb=bcb, c=c, h=h: e.tensor_tensor(out=oaT[0:64, h, 512 * c:512 * (c + 1)], in0=accs[ab][0:64, :], in1=bank(bcb)[0:64, :], op=ALU.mult),
                      r=[("accs", ab), ("ps", bcb)], w=[("oaT", h, c)])


        DMA(qscr, qs_tok[0:16, :], ["qs_tok"], ["qscr"])
        qbc = [AR.alloc([128, 768], F32) for _ in range(2)]
        kvt = [AR.alloc([128, 512], BF16) for _ in range(6)]
        prod = [AR.alloc([128, 256], F32) for _ in range(2)]
        prodn = AR.alloc([4, 256], F32)
        scs = [AR.alloc([128, 4], F32) for _ in range(2)]
        scn = AR.alloc([4, 4], F32)
        Ps = [AR.alloc([128, 4], BF16) for _ in range(2)]
        Pn = [AR.alloc([4, 4], BF16) for _ in range(2)]
        om = AR.alloc([4, 256], F32)
        rcp = AR.alloc([4, 1], F32)
        for b_ in range(6):
            MSET("pool", kvt[b_][:], 0.0, ["tbl"], [("kvt", b_)])
        osb = bank(3)[0:64, 0:64].rearrange("p (h j) -> p h j", h=4)
        prodn2 = [prodn, AR.alloc([4, 256], F32)]
        scn2 = [scn, AR.alloc([4, 4], F32)]
        Ps6 = Ps + [AR.alloc([128, 4], BF16) for _ in range(4)]
        Pn6 = Pn + [AR.alloc([4, 4], BF16) for _ in range(4)]

        def a4_tok(j):
            s_, t_ = j // 4, j % 4
            tp = j % 2
            qb = qbc[tp]
            DMA(qb[:], qscr[j].partition_broadcast(128), ["qscr", "tbl"], [("qbc", tp)])
            ab_ = 4 + tp
            for g in range(3):
                d = DILS[g]
                kb = (3 * j + g) % 6
                nrow = 128 - t_ if g == 0 else 128
                DMA(kvt[kb][0:nrow, :], cache[g][s_, t_:t_ + (nrow - 1) * d + 1:d, :], [], [("kvt", kb)], q="pool")
                TT("dve", prod[tp][:], kvt[kb][:, 0:256], qb[:, 256 * g:256 * (g + 1)], ALU.mult, [("kvt", kb), ("qbc", tp)], [("prod", tp)])
                yield
                RED(scs[tp][:], prod[tp][:].rearrange("p (h d) -> p h d", h=4), [("prod", tp)], [("scs", tp)])
                yield
                TT("dve", scs[tp][:], scs[tp][:], tbs[:, 16 * g + 4 * t_:16 * g + 4 * t_ + 4], ALU.add, [("scs", tp), "tbs"], [("scs", tp)])
                yield
                ACT(Ps6[kb][:], scs[tp][:], AF.Exp, [("scs", tp)], [("Ps", kb)])
                yield
                MM(bank(ab_)[0:4, 0:256], Ps6[kb][:], kvt[kb][:, 256:512], [("Ps", kb), ("kvt", kb)], [("ps", ab_)], start=(g == 0), stop=False, skip_group_check=True)
                MM(bank(ab_)[0:4, 256:257], Ps6[kb][:], ones_b[:, 0:1], [("Ps", kb), "cbf"], [("ps", ab_)], start=False, stop=False, skip_group_check=True)
                yield
                TT("dve", prodn2[tp][0:4, :], kvnew[s_][0:4, g, 0:256], qb[0:4, 256 * g:256 * (g + 1)], ALU.mult, [("kvnew", s_, g), ("qbc", tp)], [("prodn", tp)])
                yield
                RED(scn2[tp][0:4, :], prodn2[tp][0:4, :].rearrange("p (h d) -> p h d", h=4), [("prodn", tp)], [("scn", tp)])
                yield
                TT("dve", scn2[tp][0:4, :], scn2[tp][0:4, :], tbn[0:4, 16 * g + 4 * t_:16 * g + 4 * t_ + 4], ALU.add, [("scn", tp), "tbn"], [("scn", tp)])
                yield
                ACT(Pn6[kb][0:4, :], scn2[tp][0:4, :], AF.Exp, [("scn", tp)], [("Pn", kb)])
                yield
                MM(bank(ab_)[0:4, 0:256], Pn6[kb][0:4, :], kvnb[s_][0:4, g, :], [("Pn", kb), ("kvnb", s_, g)], [("ps", ab_)], start=False, stop=False, skip_group_check=True)
                MM(bank(ab_)[0:4, 256:257], Pn6[kb][0:4, :], ones_b[0:4, 0:1], [("Pn", kb), "cbf"], [("ps", ab_)], start=False, stop=(g == 2), skip_group_check=True)
                yield
            S.dve(lambda e: e.reciprocal(out=rcp[0:4, :], in_=bank(ab_)[0:4, 256:257]), [("ps", ab_)], ["rcp"])
            STT(om[0:4, :], bank(ab_)[0:4, 0:256], rcp[0:4, 0:1], C("mbd4", 256, 4), ALU.mult, ALU.mult, [("ps", ab_), "rcp", "cst"], ["om"])
            for h in range(4):
                MM(osb[:, h, j:j + 1], om[0:4, 64 * h:64 * (h + 1)], C("ones", 1, 4), ["om", "cst"], [("ps", 3)], start=True, stop=True, skip_group_check=True)

        def _ls4(gens):
            gens = list(gens)
            while gens:
                for g0_ in list(gens):
                    try:
                        next(g0_)
                    except StopIteration:
                        gens.remove(g0_)

        for j in range(0, 16, 2):
            _ls4([a4_tok(j), a4_tok(j + 1)])
        CP("act", oaT[0:64, :, NP_:NT_], osb, [("ps", 3)], [("oaT", "s")])

        if "oaT" in dbg:
            dbg_out["oaT"] = dout("dbg_oaT", [64, 4 * NT_], BF16)
            S.dma(lambda e: e.dma_start(out=dbg_out["oaT"], in_=oaT[:].rearrange("p a b -> p (a b)")), r=[("oaT", h, c) for h in range(4) for c in range(4)] + [("oaT", "s")], w=["dbg_oaT"])
        if stop_after == 2:
            S.finalize(st)
            return nc, S, dbg_out

        DMA(oa_scr, oaT[:].rearrange("p a b -> p (a b)"), [("oaT", h, c) for h in range(4) for c in range(4)] + [("oaT", "s")], ["oa_scr"])
        barrier()
        AR.off = 0
        NTI = 20
        TILES = [(i, 128, i * 128) for i in range(16)] + [(16 + s_, 4, NP_ + 4 * s_) for s_ in range(4)]
        wba = AR.alloc([128, 8, 16], BF16)
        bb, gg, Gc, eG, eGlG, dtA, dtB, beG, eGdk = [AR.alloc([128, NTI, 8], F32) for _ in range(9)]
        dtb_bc = AR.alloc([128, 8], F32)
        negA = AR.alloc([128, 8], F32)
        t1, t2, t3 = [AR.alloc([128, 8], F32) for _ in range(3)]
        DMA(wba[:], w_in_v[:, :, C_BETA:C_BETA + 16], [], ["wba"], q="pool")
        DMA(dtb_bc[:], dtb_d.partition_broadcast(128), [], ["dtb"])
        DMA(negA[:], alog_d.partition_broadcast(128), [], ["negA"])
        ACT(negA[:], negA[:], AF.Exp, ["negA"], ["negA"])
        TS("dve", negA[:], negA[:], -1.0, ALU.mult, ["negA"], ["negA"])
        nTk = [("nT", j) for j in range(17)]
        tsc = [[AR.alloc([128, 8], F32) for _ in range(3)] for _ in range(4)]

        def scal_tile(ti, W, c0, par):
            b0 = 2 * par
            t1p, t2p, t3p = tsc[par]
            for kc in range(8):
                MM(bank(b0)[0:W, 0:16], nT[:, kc, c0:c0 + W], wba[:, kc, :], ["wba"] + nTk, [("ps", b0)], start=(kc == 0), stop=(kc == 7))
            yield
            TT("dve", t1p[0:W, :], bank(b0)[0:W, 8:16], dtb_bc[0:W, :], ALU.add, [("ps", b0), "dtb"], [("t1", par)])
            yield
            ACT(t1p[0:W, :], t1p[0:W, :], AF.Exp, [("t1", par)], [("t1", par)])
            yield
            ACT(t1p[0:W, :], t1p[0:W, :], AF.Ln, [("t1", par)], [("t1", par)], bias=1.0)
            yield
            TT("dve", gg[0:W, ti, :], t1p[0:W, :], negA[0:W, :], ALU.mult, [("t1", par), "negA"], [("gg", ti)])
            yield
            ACT(t2p[0:W, :], bank(b0)[0:W, 0:8], AF.Exp, [("ps", b0)], [("t2", par)], scale=-1.0)
            yield
            ACT(t2p[0:W, :], t2p[0:W, :], AF.Ln, [("t2", par)], [("t2", par)], bias=1.0)
            yield
            ACT(bb[0:W, ti, :], t2p[0:W, :], AF.Exp, [("t2", par)], [("bb", ti)], scale=-1.0)
            yield
            MM(bank(b0 + 1)[0:W, 0:8], C("U")[0:W, 0:W], gg[0:W, ti, :], [("gg", ti), "cst"], [("ps", b0 + 1)])
            yield
            CP("act", Gc[0:W, ti, :], bank(b0 + 1)[0:W, 0:8], [("ps", b0 + 1)], [("Gc", ti)])
            yield
            ACT(eG[0:W, ti, :], bank(b0 + 1)[0:W, 0:8], AF.Exp, [("ps", b0 + 1)], [("eG", ti)])
            yield
            TT("dve", beG[0:W, ti, :], eG[0:W, ti, :], bb[0:W, ti, :], ALU.mult, [("eG", ti), ("bb", ti)], [("beG", ti)])
            yield
            TS("dve", eGdk[0:W, ti, :], eG[0:W, ti, :], 128 ** -0.5, ALU.mult, [("eG", ti)], [("eGdk", ti)])
            yield
            if W == 128:
                MM(bank(b0 + 1)[:, 8:16], C("selA"), Gc[:, ti, :], [("Gc", ti), "cst"], [("ps", b0 + 1)])
                yield
                MM(bank(b0 + 1)[:, 16:24], C("selB"), Gc[:, ti, :], [("Gc", ti), "cst"], [("ps", b0 + 1)])
                yield
                ACT(dtA[:, ti, :], bank(b0 + 1)[:, 8:16], AF.Exp, [("ps", b0 + 1)], [("dtA", ti)])
                yield
                ACT(dtB[:, ti, :], bank(b0 + 1)[:, 16:24], AF.Exp, [("ps", b0 + 1)], [("dtB", ti)])
                yield
                TT("dve", t3p[0:64, :], bank(b0 + 1)[0:64, 8:16], Gc[0:64, ti, :], ALU.subtract, [("ps", b0 + 1), ("Gc", ti)], [("t3", par)])
                yield
                TT("dve", t3p[64:128, :], bank(b0 + 1)[64:128, 16:24], Gc[64:128, ti, :], ALU.subtract, [("ps", b0 + 1), ("Gc", ti)], [("t3", par)])
                yield
                ACT(eGlG[:, ti, :], t3p[:, :], AF.Exp, [("t3", par)], [("eGlG", ti)])
                yield
            else:
                MM(bank(b0 + 1)[:, 8:16], C("sel3")[0:W, :], Gc[0:W, ti, :], [("Gc", ti), "cst"], [("ps", b0 + 1)])
                yield
                ACT(dtA[:, ti, :], bank(b0 + 1)[:, 8:16], AF.Exp, [("ps", b0 + 1)], [("dtA", ti)])
                yield
                TT("dve", t3p[0:W, :], bank(b0 + 1)[0:W, 8:16], Gc[0:W, ti, :], ALU.subtract, [("ps", b0 + 1), ("Gc", ti)], [("t3", par)])
                yield
                ACT(eGlG[0:W, ti, :], t3p[0:W, :], AF.Exp, [("t3", par)], [("eGlG", ti)])
                yield

        def _ls(gens):
            gens = list(gens)
            while gens:
                for g0_ in list(gens):
                    try:
                        next(g0_)
                    except StopIteration:
                        gens.remove(g0_)

        for q4 in range(0, len(TILES), 4):
            _ls([scal_tile(TILES[q4 + k][0], TILES[q4 + k][1], TILES[q4 + k][2], k) for k in range(min(4, len(TILES) - q4))])

        U4 = AR.alloc([128, 4, 128], F32)
        I4 = AR.alloc([128, 4, 128], F32)
        I4b = AR.alloc([128, 4, 128], BF16)
        negS4 = AR.alloc([128, 4, 128], BF16)
        for hl in range(4):
            CP("dve", U4[:, hl, :], C("U"), ["cst"], ["c4"])
            CP("dve", I4[:, hl, :], C("ident"), ["cst"], ["c4"])
            CP("dve", I4b[:, hl, :], C("ident"), ["cst"], ["c4"])
            CP("dve", negS4[:, hl, :], C("negS"), ["cst"], ["c4"])
        carry = AR.alloc([128, 8, 3, 3], F32)
        preS = AR.alloc([128, 3, 4, 7], F32)
        caccS = AR.alloc([128, 4, 4], F32)
        BLKS = [(0, 768), (768, 768), (1536, 512)]
        BW = 768 + 16
        qkvb = AR.alloc([128, 8, 3, BW], BF16)
        szb = AR.alloc([128, 8, BW], BF16)
        oTt = [AR.alloc([128, 8, 128], F32) for _ in range(2)]
        obt = [AR.alloc([128, 8, 128], BF16) for _ in range(2)]
        sqn = AR.alloc([128, 512], BF16)
        rsn = AR.alloc([128, 512], F32)
        tnn = AR.alloc([128, 512], F32)
        dco = tnn[0:19, 0:384]
        dcbh = rsn[0:12, 0:384].rearrange("p (q c) -> p q c", q=3)
        Sf = AR.alloc([128, 8, 128], F32)
        Sb = AR.alloc([128, 8, 128], BF16)

        class PBuf:
            pass

        GS = []
        tile_mark = AR.off
        for g_ in range(2):
            B = PBuf()
            B.squg = AR.alloc([128, 512], F32)
            B.ssq = AR.alloc([128, 8], F32)
            B.rqk = AR.alloc([128, 4, 2], F32)
            B.sc = AR.alloc([128, 5, 4], F32)
            B.Gm = AR.alloc([128, 4, 128], F32)
            B.Idg = AR.alloc([128, 4, 128], BF16)
            B.sck = AR.alloc([128, 4, 2, 128], BF16)
            B.Irk = AR.alloc([128, 4, 128], F32)
            B.P1 = AR.alloc([128, 4, 128], BF16)
            B.PT = [AR.alloc([128, 4, 128], BF16) for _ in range(2)]
            B.Y = [AR.alloc([128, 4, 128], BF16) for _ in range(2)]
            B.hand = []
            for _ in range(2):
                H_ = PBuf()
                H_.kbg = AR.alloc([128, 4, 128], BF16)
                H_.vb = AR.alloc([128, 4, 128], BF16)
                H_.P0 = AR.alloc([128, 4, 128], BF16)
                B.hand.append(H_)
            B.si = []
            for _ in range(3):
                I_ = PBuf()
                I_.u0 = AR.alloc([128, 4, 128], BF16)
                I_.wT = AR.alloc([128, 4, 128], BF16)
                I_.aT = AR.alloc([128, 4, 128], BF16)
                I_.qd = AR.alloc([128, 4, 128], BF16)
                I_.kdec = AR.alloc([128, 4, 128], BF16)
                I_.vn = AR.alloc([128, 4, 128], BF16)
                B.si.append(I_)
            GS.append(B)
        tile_end = AR.off
        AR.off = tile_mark
        Wh = [AR.alloc([128, 8, 512], BF16) for _ in range(2)]
        pre2 = [AR.alloc([128, 3, 3 + 768], F32) for _ in range(2)]
        cacc3 = [AR.alloc([128, 768], F32) for _ in range(3)]
        AR.off = max(AR.off, tile_end)
        MSET("pool", carry[:].rearrange("p a b c -> p (a b c)"), 0.0, [], ["carry"])
        w_in_dc = dconv_in.rearrange("r (q c) -> r q c", q=3)
        dconv_p_v = dconv_p.rearrange("r (q c) -> r q c", q=3)
        wcd0 = PRM["wcd"]
        ndo_col = prm[:, PRM["ndo"]:PRM["ndo"] + 1]
        DK = 128 ** -0.5
        wbf_v = wbf.rearrange("(kc p) n -> p kc n", p=128)

        def bc(ap2, W, n):
            return ap2.unsqueeze(2).to_broadcast([W, 4, n])

        def bank2(i):
            return psum[:, i:i + 2, :].rearrange("p a b -> p (a b)")

        def gdn_prep(g_, ti, W, c0, sib, ha, part):
            B = GS[g_]
            I_ = B.si[sib]
            H = B.hand[ha]
            KH = lambda n: (n, g_, "h", ha)
            bk = lambda b: (b + 3 * g_) % 6
            K_ = lambda n: (n, g_)
            KI = lambda n: (n, g_, sib)
            hs = slice(4 * g_, 4 * g_ + 4)
            qk_key = ("qkvb", g_)
            k_ = qkvb[:, hs, 1, c0:c0 + W]
            q_ = qkvb[:, hs, 0, c0:c0 + W]
            colv = lambda arr: arr[0:W, ti, hs]
            if part == "A":
                sqv = B.squg[:, :].bitcast(BF16)[:, 0:1024].rearrange("p (h q w) -> p h q w", h=4, q=2)
                ACT(sqv[:, :, :, 0:W], qkvb[:, hs, 0:2, c0:c0 + W], AF.Square, [qk_key], [K_("squg")])
                for hl in range(4):
                    for qq in range(2):
                        MM(bank(bk(0))[0:W, 2 * hl + qq:2 * hl + qq + 1], sqv[:, hl, qq, 0:W], ones_b[:, 0:1], [K_("squg"), "cbf"], [("ps", bk(0))])
                ACT(B.ssq[0:W, :], bank(bk(0))[0:W, 0:8], AF.Ln, [("ps", bk(0))], [K_("ssq")], bias=EPS)
                ACT(B.rqk[0:W, :, :].rearrange("p h q -> p (h q)"), B.ssq[0:W, :], AF.Exp, [K_("ssq")], [K_("rqk")], scale=-0.5)
                rk = B.rqk[0:W, :, 1]
                rq = B.rqk[0:W, :, 0]
                yield
                TT("pool", B.sc[0:W, 0, :], rk, colv(bb), ALU.mult, [K_("rqk"), ("bb", ti)], [K_("sc")])
                TS("pool", B.sc[0:W, 1, :], rq, DK, ALU.mult, [K_("rqk")], [K_("sc")])
                TT("pool", B.sc[0:W, 2, :], rq, colv(eGdk), ALU.mult, [K_("rqk"), ("eGdk", ti)], [K_("sc")])
                TT("pool", B.sc[0:W, 3, :], rk, colv(beG), ALU.mult, [K_("rqk"), ("beG", ti)], [K_("sc")])
                TT("pool", B.sc[0:W, 4, :], rk, colv(eGlG), ALU.mult, [K_("rqk"), ("eGlG", ti)], [K_("sc")])
                yield
                pbf = bank_bf(bk(1))[:, 0:1024].rearrange("p (h q d) -> p h q d", h=4, q=2)
                for hl in range(4):
                    TR(pbf[0:W, hl, 0, :], qkvb[:, 4 * g_ + hl, 1, c0:c0 + W], ident_b, [qk_key, "cbf"], [("ps", bk(1))])
                    TR(pbf[0:W, hl, 1, :], qkvb[:, 4 * g_ + hl, 2, c0:c0 + W], ident_b, [qk_key, "cbf"], [("ps", bk(1))])
                TT("dve", H.kbg[0:W, :, :], pbf[0:W, :, 0, :], bc(B.sc[0:W, 3, :], W, 128), ALU.mult, [("ps", bk(1)), K_("sc")], [KH("kbg")])
                TT("dve", I_.kdec[0:W, :, :], pbf[0:W, :, 0, :], bc(B.sc[0:W, 4, :], W, 128), ALU.mult, [("ps", bk(1)), K_("sc")], [KI("kdec")])
                TT("dve", H.vb[0:W, :, :], pbf[0:W, :, 1, :], bc(colv(bb), W, 128), ALU.mult, [("ps", bk(1)), ("bb", ti)], [KH("vb")])
                yield
                Ug = B.squg[0:W, :].rearrange("p (h c) -> p h c", h=4)[:, :, 0:W]
                TT("pool", Ug, U4[0:W, :, 0:W], bc(colv(gg), W, W), ALU.mult, ["c4", ("gg", ti), K_("squg")], [K_("squg")])
                Dv = bank(bk(2))[0:W, :].rearrange("p (h c) -> p h c", h=4)[:, :, 0:W]
                MM(Dv, C("V")[0:W, 0:W], Ug, [K_("squg"), "cst"], [("ps", bk(2))], start=True, stop=False)
                MM(Dv, ident_b[0:W, 0:W], negS4[0:W, :, 0:W], ["cbf", "c4"], [("ps", bk(2))], start=False, stop=True)
                ACT(B.Gm[0:W, :, 0:W], Dv, AF.Exp, [("ps", bk(2))], [K_("Gm")])
                TT("dve", B.Gm[0:W, :, 0:W], B.Gm[0:W, :, 0:W], bc(rk, W, W), ALU.mult, [K_("Gm"), K_("rqk")], [K_("Gm")])
                yield
                dsts = (B.sck[:, :, 0, 0:W], B.sck[:, :, 1, 0:W], I_.qd[:, :, 0:W])
                srcs = (k_, q_, q_)
                dkeys = (K_("sck"), K_("sck"), KI("qd"))
                for j in range(3):
                    yield
                    TT("pool", B.Idg[0:W, :, 0:W], I4b[0:W, :, 0:W], bc(B.sc[0:W, j, :], W, W), ALU.mult, ["c4", K_("sc")], [K_("Idg")])
                    rb = bk(3 + (j % 2))
                    Rv = bank(rb)[:, :].rearrange("p (h c) -> p h c", h=4)[:, :, 0:W]
                    MM(Rv, ones_b[0:W, :], B.Idg[0:W, :, 0:W], [K_("Idg"), "cbf"], [("ps", rb)])
                    TT("dve", dsts[j], Rv, srcs[j], ALU.mult, [("ps", rb), qk_key], [dkeys[j]])
                yield
                KKv = bank2(bk(4))[0:W, :].rearrange("p (h q c) -> p h q c", h=4, q=2)[:, :, :, 0:W]
                for hl in range(4):
                    MM(KKv[:, hl, :, :], qkvb[:, 4 * g_ + hl, 1, c0:c0 + W], B.sck[:, hl, :, 0:W], [qk_key, K_("sck")], [("ps", bk(4)), ("ps", bk(4) + 1)])
                TT("dve", H.P0[0:W, :, 0:W], KKv[:, :, 0, :], B.Gm[0:W, :, 0:W], ALU.mult, [("ps", bk(4)), ("ps", bk(4) + 1), K_("Gm")], [KH("P0")])
                TT("pool", B.Irk[0:W, :, 0:W], I4[0:W, :, 0:W], bc(rk, W, W), ALU.mult, ["c4", K_("rqk")], [K_("Irk")])
                TT("pool", B.Irk[0:W, :, 0:W], B.Irk[0:W, :, 0:W], B.Gm[0:W, :, 0:W], ALU.add, [K_("Irk"), K_("Gm")], [K_("Irk")])
                TT("dve", I_.aT[0:W, :, 0:W], KKv[:, :, 1, :], B.Irk[0:W, :, 0:W], ALU.mult, [("ps", bk(4)), ("ps", bk(4) + 1), K_("Irk")], [KI("aT")])
            else:
                Pl = [H.P0, B.P1]
                pkey = [KH("P0"), K_("P1")]
                ptv = bank_bf(bk(2))[:, 0:512].rearrange("p (h c) -> p h c", h=4)
                for hl in range(4):
                    TR(ptv[0:W, hl, 0:W], H.P0[0:W, hl, 0:W], ident_b[0:W, 0:W], [KH("P0"), "cbf"], [("ps", bk(2))])
                CP("act", B.PT[0][0:W, :, 0:W], ptv[0:W, :, 0:W], [("ps", bk(2))], [K_("PT0")])
                TT("pool", B.Y[0][0:W, :, 0:W], I4[0:W, :, 0:W], H.P0[0:W, :, 0:W], ALU.subtract, ["c4", KH("P0")], [K_("Y0")])
                n_it = 5 if W == 128 else 1
                v4 = lambda b_: bank(bk(b_))[0:W, :].rearrange("p (h c) -> p h c", h=4)[:, :, 0:W]
                for it in range(n_it):
                    yield
                    last = it == n_it - 1
                    a, b2 = it % 2, 1 - it % 2
                    for hl in range(4):
                        MM(v4(3)[:, hl, :], Pl[a][0:W, hl, 0:W], B.PT[a][0:W, hl, 0:W], [pkey[a], K_("PT%d" % a)], [("ps", bk(3))])
                    CP("act", B.PT[b2][0:W, :, 0:W], v4(3), [("ps", bk(3))], [K_("PT%d" % b2)])
                    yield
                    if not last:
                        for hl in range(4):
                            MM(v4(0)[:, hl, :], B.PT[a][0:W, hl, 0:W], Pl[a][0:W, hl, 0:W], [pkey[a], K_("PT%d" % a)], [("ps", bk(0))])
                        CP("act", Pl[b2][0:W, :, 0:W], v4(0), [("ps", bk(0))], [pkey[b2]])
                    yield
                    for hl in range(4):
                        MM(v4(2)[:, hl, :], B.PT[b2][0:W, hl, 0:W], B.Y[a][0:W, hl, 0:W], [K_("PT%d" % b2), K_("Y%d" % a)], [("ps", bk(2))])
                    TT("dve", B.Y[b2][0:W, :, 0:W], B.Y[a][0:W, :, 0:W], v4(2), ALU.add, [K_("Y%d" % a), ("ps", bk(2))], [K_("Y%d" % b2)])
                TTb = B.Y[n_it % 2]
                tk = K_("Y%d" % (n_it % 2))
                yield
                u0v = bank(bk(4))[0:W, :].rearrange("p (h d) -> p h d", h=4)
                for hl in range(4):
                    MM(u0v[:, hl, :], TTb[0:W, hl, 0:W], H.vb[0:W, hl, :], [tk, KH("vb")], [("ps", bk(4))])
                CP("act", I_.u0[0:W, :, :], u0v, [("ps", bk(4))], [KI("u0")])
                wTv = bank(bk(5))[:, :].rearrange("p (h c) -> p h c", h=4)[:, :, 0:W]
                for hl in range(4):
                    MM(wTv[:, hl, :], H.kbg[0:W, hl, :], TTb[0:W, hl, 0:W], [tk, KH("kbg")], [("ps", bk(5))])
                CP("dve" if g_ == 0 else "act", I_.wT[:, :, 0:W], wTv, [("ps", bk(5))], [KI("wT")])

        def gdn_seq(g_, ti, W, sib, r0, C_, dtarr, dtkey, ob):
            B = GS[g_]
            I_ = B.si[sib]
            KI = lambda n: (n, g_, sib)
            hs = slice(4 * g_, 4 * g_ + 4)
            Sk = ("S", g_)
            rs = slice(r0, r0 + C_)
            wsv = bank(6)[:, :].rearrange("p (h d) -> p h d", h=4)
            for hl in range(4):
                MM(wsv[rs, hl, :], I_.wT[:, hl, rs], Sb[:, 4 * g_ + hl, :], [KI("wT"), ("Sb", g_)], [("ps", 6)])
            TT("dve", I_.vn[rs, :, :], I_.u0[rs, :, :], wsv[rs, :, :], ALU.subtract, [KI("u0"), ("ps", 6)], [KI("vn")])
            yield
            TT("pool", Sf[:, hs, :], Sf[:, hs, :], bc(dtarr[:, ti, hs], 128, 128), ALU.mult, [("Sf", g_), dtkey], [("Sf", g_)])
            ov = bank(7)[:, :].rearrange("p (h c) -> p h c", h=4)
            for hl in range(4):
                MM(ov[:, hl, rs], Sb[:, 4 * g_ + hl, :], I_.qd[:, hl, rs], [("Sb", g_), KI("qd")], [("ps", 7)], start=True, stop=False)
                MM(ov[:, hl, rs], I_.vn[rs, hl, :], I_.aT[rs, hl, rs], [KI("vn"), KI("aT")], [("ps", 7)], start=False, stop=True)
            CP("act", oTt[ob][:, hs, rs], ov[:, :, rs], [("ps", 7)], [("oTt", ob, g_)])
            yield
            snv = bank(6)[:, :].rearrange("p (h d) -> p h d", h=4)
            for hl in range(4):
                MM(snv[:, hl, :], I_.kdec[rs, hl, :], I_.vn[rs, hl, :], [KI("kdec"), KI("vn")], [("ps", 6)])
            TT("dve", Sf[:, hs, :], Sf[:, hs, :], snv, ALU.add, [("Sf", g_), ("ps", 6)], [("Sf", g_)])
            CP("act", Sb[:, hs, :], Sf[:, hs, :], [("Sf", g_)], [("Sb", g_)])

        def gdn_norm(W, c0, gcol0, ob):
            for g_ in range(2):
                hs = slice(4 * g_, 4 * g_ + 4)
                sq_ = sqn[:, 0:4 * W].rearrange("p (h c) -> p h c", h=4)
                rs_ = rsn[:, 0:4 * W].rearrange("p (h c) -> p h c", h=4)
                tn_ = tnn[:, 0:4 * W].rearrange("p (h c) -> p h c", h=4)
                ACT(sq_, oTt[ob][:, hs, 0:W], AF.Square, [("oTt", ob, g_)], ["sqn"])
                yield
                msv = bank(0)[:, 0:4 * W].rearrange("p (h c) -> p h c", h=4)
                MM(msv, ones_b, sq_, ["sqn", "cbf"], [("ps", 0)])
                ACT(rs_, msv, AF.Ln, [("ps", 0)], ["rsn"], scale=1.0 / 128, bias=EPS)
                yield
                ACT(rs_, rs_, AF.Exp, ["rsn"], ["rsn"], scale=-0.5)
                yield
                STT(tn_, oTt[ob][:, hs, 0:W], ndo_col, rs_, ALU.mult, ALU.mult, [("oTt", ob, g_), "rsn", "prm"], ["tnn"])
                TT("dve", obt[ob][:, hs, 0:W], tn_, szb[:, hs, c0:c0 + W], ALU.mult, ["tnn", "szb"], [("obt", ob)])
                yield
            DMA(ob_scr_v[:, :, gcol0:gcol0 + W], obt[ob][:, :, 0:W], [("obt", ob)], [("ob_scr", gcol0)])

        ob_scr_v = ob_scr.rearrange("p (h t) -> p h t", h=8)

        def lockstep(gens):
            gens = list(gens)
            while gens:
                for gg_ in list(gens):
                    try:
                        next(gg_)
                    except StopIteration:
                        gens.remove(gg_)
        MSET("pool", Sf[:].rearrange("p a b -> p (a b)"), 0.0, [], [("Sf", 0), ("Sf", 1)])
        MSET("pool", Sb[:].rearrange("p a b -> p (a b)"), 0.0, [], [("Sb", 0), ("Sb", 1)])
        rot = 0
        nob = 0
        for b_, (t0, BT) in enumerate(BLKS):
            lastb = b_ == len(BLKS) - 1
            nk = [("nT", j) for j in range(t0 // 128, (t0 + BT) // 128)]
            subs = [(o, min(512, BT - o)) for o in range(0, BT, 512)]
            for h in range(8):
                W_ = Wh[h % 2]
                preh = pre2[h % 2]
                wk_ = ("Wh", h % 2)
                gk = ("qkvb", h // 4)
                DMA(W_[:], wbf_v[:, :, 512 * h:512 * (h + 1)], [("wbf", h)], [wk_])
                for qi in range(4):
                    for (so, sn) in subs:
                        pb = rot % 8
                        rot += 1
                        for kc in range(8):
                            MM(bank(pb)[:, 0:sn], W_[:, kc, 128 * qi:128 * (qi + 1)], nT[:, kc, t0 + so:t0 + so + sn], [wk_] + nk, [("ps", pb)], start=(kc == 0), stop=(kc == 7))
                        if qi == 3:
                            ACT(szb[:, h, so:so + sn], bank(pb)[:, 0:sn], AF.Silu, [("ps", pb)], ["szb"])
                        else:
                            CP("act" if (rot % 2) else "dve", preh[:, qi, 3 + so:3 + so + sn], bank(pb)[:, 0:sn], [("ps", pb)], [("pre", h % 2, qi)])
                if lastb:
                    for qi in range(4):
                        for kc in range(8):
                            MM(bank(5)[:, 16 * qi:16 * (qi + 1)], W_[:, kc, 128 * qi:128 * (qi + 1)], nT[:, kc, NP_:NT_], [wk_, ("nT", 16)], [("ps", 5)], start=(kc == 0), stop=(kc == 7))
                    CP("dve", preS[:, :, :, 3:7], bank(5)[:, 0:48].rearrange("p (q s t) -> p q s t", q=3, s=4), [("ps", 5)], [("preS", "new")])
                    ACT(szb[:, h, BT:BT + 16], bank(5)[:, 48:64], AF.Silu, [("ps", 5)], ["szb"])
                    for kc in range(8):
                        MM(bank(4)[0:19, 0:384], nT[:, kc, NP_ - 3:NT_], W_[:, kc, 0:384], [wk_, ("nT", 15), ("nT", 16)], [("ps", 4)], start=(kc == 0), stop=(kc == 7))
                    CP("act", dco[0:19, :], bank(4)[0:19, 0:384], [("ps", 4)], ["tnn"])
                    dco_v = dco[:, :].rearrange("r (q c) -> r q c", q=3)
                    DMA(dconv_p_v[:, :, 128 * h:128 * (h + 1)], dco_v[0:3, :, :], ["tnn"], [("dconv_p", h)])
                    for s_ in range(4):
                        DMA(dconv_s[s_].rearrange("r (q c) -> r q c", q=3)[:, :, 128 * h:128 * (h + 1)], dco_v[3 + 4 * s_ + 1:3 + 4 * s_ + 4, :, :], ["tnn"], [("dconv_s", h, s_)])
                    DMA(dcbh[0:12, :, :], w_in_dc[:, :, 128 * h:128 * (h + 1)], [], ["rsn"])
                    for qi in range(3):
                        TR(bank(4)[:, 384 + 12 * qi:384 + 12 * (qi + 1)], dcbh[0:12, qi, :], ident_f[0:12, 0:12], ["rsn", "cst"], [("ps", 4)])
                    CP("dve", preS[:, :, :, 0:3], bank(4)[:, 384:420].rearrange("p (q s i) -> p q s i", q=3, s=4), [("ps", 4)], [("preS", "buf")])
                CP("pool", preh[:, :, 0:3], carry[:, h, :, :], ["carry"], [("pre", h % 2, "c")])
                wc = lambda j, qi: prm[:, wcd0 + 4 * (8 * qi + h) + j:wcd0 + 4 * (8 * qi + h) + j + 1]
                for qi in range(3):
                    pk = [("pre", h % 2, qi), ("pre", h % 2, "c"), "prm"]
                    ACT(cacc3[qi][:, 0:BT], preh[:, qi, 3:3 + BT], AF.Copy, pk, [("cacc", qi)], scale=wc(3, qi))
                for qi in range(3):
                    pk = [("pre", h % 2, qi), ("pre", h % 2, "c"), "prm"]
                    for j in (2, 1, 0):
                        STT(cacc3[qi][:, 0:BT], preh[:, qi, j:j + BT], wc(j, qi), cacc3[qi][:, 0:BT], ALU.mult, ALU.add, pk + [("cacc", qi)], [("cacc", qi)])
                for qi in range(3):
                    ACT(qkvb[:, h, qi, 0:BT], cacc3[qi][:, 0:BT], AF.Silu, [("cacc", qi)], [gk])
                if lastb:
                    for qi in range(3):
                        ACT(caccS[:], preS[:, qi, :, 3:7], AF.Copy, [("preS", "new"), "prm"], ["caccS"], scale=wc(3, qi))
                        for j in (2, 1, 0):
                            STT(caccS[:], preS[:, qi, :, j:j + 4], wc(j, qi), caccS[:], ALU.mult, ALU.add, [("preS", "new"), ("preS", "buf"), "prm", "caccS"], ["caccS"])
                        ACT(qkvb[:, h, qi, BT:BT + 16].rearrange("p (s t) -> p s t", s=4), caccS[:], AF.Silu, ["caccS"], [gk])
                CP("pool", carry[:, h, :, :], preh[:, :, BT:BT + 3], [("pre", h % 2, 0), ("pre", h % 2, 1), ("pre", h % 2, 2)], ["carry"])
            tl_list = [(t0 // 128 + tl, 128, 128 * tl, t0 + 128 * tl) for tl in range(BT // 128)]
            if lastb:
                tl_list += [(16 + s_, 4, BT + 4 * s_, NP_ + 4 * s_) for s_ in range(4)]
            barrier()
            NTL = len(tl_list)
            pA = lambda n_: [gdn_prep(g_, tl_list[n_][0], tl_list[n_][1], tl_list[n_][2], n_ % 3, n_ % 2, "A") for g_ in range(2)]
            pB = lambda n_: [gdn_prep(g_, tl_list[n_][0], tl_list[n_][1], tl_list[n_][2], n_ % 3, n_ % 2, "B") for g_ in range(2)]
            lockstep(pA(0))
            lockstep((pA(1) if NTL > 1 else []) + pB(0))
            pend_norm = None
            for n_, (ti, W, c0, gcol) in enumerate(tl_list):
                sib = n_ % 3
                ob = nob % 2
                nob += 1
                gens = []
                if n_ + 2 < NTL:
                    gens += pA(n_ + 2)
                if n_ + 1 < NTL:
                    gens += pB(n_ + 1)
                if pend_norm is not None:
                    gens += [pend_norm]
                if W == 128:
                    def seq2(g_, ti=ti, W=W, sib=sib, ob=ob):
                        yield from gdn_seq(g_, ti, W, sib, 0, 64, dtA, ("dtA", ti), ob)
                        yield from gdn_seq(g_, ti, W, sib, 64, 64, dtB, ("dtB", ti), ob)
                    gens = [seq2(0), seq2(1)] + gens
                    lockstep(gens)
                    if ti == 15:
                        DMA(delta_p.rearrange("h k v -> k h v"), Sf[:], [("Sf", 0), ("Sf", 1)], ["delta_p"])
                else:
                    s_ = ti - 16
                    DMA(Sf[:], s0_in[s_].rearrange("h k v -> k h v"), ["delta_p"] + [("delta_s", j) for j in range(s_)], [("Sf", 0), ("Sf", 1)])
                    for g_ in range(2):
                        CP("act", Sb[:, 4 * g_:4 * g_ + 4, :], Sf[:, 4 * g_:4 * g_ + 4, :], [("Sf", g_)], [("Sb", g_)])
                    gens = [gdn_seq(g_, ti, W, sib, 0, 4, dtA, ("dtA", ti), ob) for g_ in range(2)] + gens
                    lockstep(gens)
                    DMA(delta_s[s_].rearrange("h k v -> k h v"), Sf[:], [("Sf", 0), ("Sf", 1)], [("delta_s", s_)])
                pend_norm = gdn_norm(W, c0, gcol, ob)
            lockstep([pend_norm])
            if not lastb:
                barrier()

        if "obT" in dbg:
            dbg_out["obT"] = dout("dbg_obT", [128, 8 * NT_], BF16)
            DMA(dbg_out["obT"], ob_scr, [("ob_scr", gc) for gc in list(range(0, NP_, 128)) + [NP_ + 4 * s_ for s_ in range(4)]], ["dbg_obT"])
        if stop_after == 3:
            S.finalize(st)
            return nc, S, dbg_out

        barrier()
        AR.off = 0
        oaT = AR.alloc([64, 4, NT_], BF16)
        obT = AR.alloc([128, 8, NT_], BF16)
        ob_all = [("ob_scr", gc) for gc in list(range(0, NP_, 128)) + [NP_ + 4 * s_ for s_ in range(4)]]
        oa_scr_v = oa_scr.rearrange("p (a b) -> p a b", a=4)
        ob_scr_v3 = ob_scr.rearrange("p (a b) -> p a b", a=8)
        for bi3, (t0, tn) in enumerate(TB):
            DMA(oaT[:, :, t0:t0 + tn], oa_scr_v[:, :, t0:t0 + tn], ["oa_scr"], [("oaT3", bi3)])
            DMA(obT[:, :, t0:t0 + tn], ob_scr_v3[:, :, t0:t0 + tn], ob_all, [("obT3", bi3)])
        mergedT = AR.alloc([128, 8, NT_], BF16)
        wout = AR.alloc([128, 8, D], BF16)
        wg3 = [[AR.alloc([128, 8, 128], BF16) for _ in range(3)] for _ in range(2)]
        wa3 = [AR.alloc([64, 4, 128], BF16) for _ in range(2)]
        sg3 = [AR.alloc([128, 512], F32) for _ in range(4)]
        m3 = [AR.alloc([128, 512], F32) for _ in range(4)]
        xt3 = [AR.alloc([128, D], F32) for _ in range(2)]
        h3t = [AR.alloc([128, D], F32) for _ in range(2)]
        tmp3 = [(AR.alloc([128, D], BF16), AR.alloc([128, 1], F32), AR.alloc([128, 1], F32), AR.alloc([128, D], BF16)) for _ in range(2)]
        DMA(wout[:], w_out.rearrange("(kc p) n -> p kc n", p=128), [], ["wout"], q="pool")
        w_bb_v = w_bb.rearrange("(kc p) n -> p kc n", p=128)
        w_ba_v = w_ba.rearrange("(h d) n -> d h n", d=64)
        for f in range(8):
            wb = f % 2
            DMA(wg3[wb][0][:], w_in_v[:, :, C_GA + 128 * f:C_GA + 128 * (f + 1)], [], [("wg3", wb, 0)], q="pool")
            DMA(wg3[wb][1][:], w_in_v[:, :, C_GB + 128 * f:C_GB + 128 * (f + 1)], [], [("wg3", wb, 1)], q="pool")
            DMA(wg3[wb][2][:], w_bb_v[:, :, 128 * f:128 * (f + 1)], [], [("wg3", wb, 2)], q="pool")
            DMA(wa3[wb][:], w_ba_v[:, :, 128 * f:128 * (f + 1)], [], [("wa3", wb)], q="pool")
            for bi, (t0, tn) in enumerate(TB):
                nk = [("nT", j) for j in range(t0 // 128, (t0 + tn + 127) // 128)]
                p0 = 4 * ((f * 5 + bi) % 2)
                s0 = 2 * ((f * 5 + bi) % 2)
                for kc in range(8):
                    MM(bank(p0 + 0)[:, 0:tn], wg3[wb][0][:, kc, :], nT[:, kc, t0:t0 + tn], [("wg3", wb, 0)] + nk, [("ps", p0 + 0)], start=(kc == 0), stop=(kc == 7))
                for kc in range(8):
                    MM(bank(p0 + 1)[:, 0:tn], wg3[wb][1][:, kc, :], nT[:, kc, t0:t0 + tn], [("wg3", wb, 1)] + nk, [("ps", p0 + 1)], start=(kc == 0), stop=(kc == 7))
                for hh in range(4):
                    MM(bank(p0 + 2)[:, 0:tn], wa3[wb][0:64, hh, :], oaT[0:64, hh, t0:t0 + tn], [("wa3", wb), ("oaT3", bi)], [("ps", p0 + 2)], start=(hh == 0), stop=(hh == 3))
                for kc in range(8):
                    MM(bank(p0 + 3)[:, 0:tn], wg3[wb][2][:, kc, :], obT[:, kc, t0:t0 + tn], [("wg3", wb, 2), ("obT3", bi)], [("ps", p0 + 3)], start=(kc == 0), stop=(kc == 7))
                ACT(sg3[s0 + 0][:, 0:tn], bank(p0 + 0)[:, 0:tn], AF.Sigmoid, [("ps", p0 + 0)], [("sg3", s0 + 0)])
                ACT(sg3[s0 + 1][:, 0:tn], bank(p0 + 1)[:, 0:tn], AF.Sigmoid, [("ps", p0 + 1)], [("sg3", s0 + 1)])
                TT("dve", m3[s0 + 0][:, 0:tn], sg3[s0 + 0][:, 0:tn], bank(p0 + 2)[:, 0:tn], ALU.mult, [("sg3", s0 + 0), ("ps", p0 + 2)], [("m3", s0 + 0)])
                TT("dve", m3[s0 + 1][:, 0:tn], sg3[s0 + 1][:, 0:tn], bank(p0 + 3)[:, 0:tn], ALU.mult, [("sg3", s0 + 1), ("ps", p0 + 3)], [("m3", s0 + 1)])
                TT("dve", mergedT[:, f, t0:t0 + tn], m3[s0 + 0][:, 0:tn], m3[s0 + 1][:, 0:tn], ALU.add, [("m3", s0 + 0), ("m3", s0 + 1)], [("mergedT", f, bi)])
        if "merged" in dbg:
            dbg_out["merged"] = dout("dbg_merged", [128, 8 * NT_], BF16)
            S.dma(lambda e: e.dma_start(out=dbg_out["merged"], in_=mergedT[:].rearrange("p a b -> p (a b)")), r=[("mergedT", f, bi) for f in range(8) for bi in range(5)], w=["dbg_merged"])
        mK = lambda bi: [("mergedT", f, bi) for f in range(8)]
        pend2 = None
        for i in range(NTILE + 1):
            rows = 128 if i < NTILE else NS_
            c0 = i * 128
            b = i % 2
            src = x_p[i * 128:(i + 1) * 128, :] if i < NTILE else x_s
            DMA(xt3[b][0:rows, :], src, [], [("xt3", b)])
            for half in range(2):
                pb = (4 if i % 2 == 0 else 2) + half
                for kc in range(8):
                    MM(bank(pb)[0:rows, :], mergedT[:, kc, c0:c0 + rows], wout[:, kc, 512 * half:512 * (half + 1)], ["wout"] + mK(min(i // 4, 4)), [("ps", pb)], start=(kc == 0), stop=(kc == 7))
                TT("dve", h3t[b][0:rows, 512 * half:512 * (half + 1)], xt3[b][0:rows, 512 * half:512 * (half + 1)], bank(pb)[0:rows, :], ALU.add, [("xt3", b), ("ps", pb)], [("h3t", b)])
            DMA(hscr[c0:c0 + rows, :], h3t[b][0:rows, :], [("h3t", b)], [("hscr", i)])
            nxt2 = norm_tile_to_nT(i, h3t[b], ("h3t", b), rows, 1, tmp3[b], 6 + b, "n2_%d_" % b, defer=True)
            if pend2 is not None:
                pend2()
            pend2 = nxt2
        pend2()
        if stop_after == 4:
            if "h1" in dbg:
                pass
            S.finalize(st)
            return nc, S, dbg_out

        barrier()
        AR.off = 0
        memT = AR.alloc([128, 8, 256], BF16)
        wmkv = AR.alloc([128, 8, D], BF16)
        wmq = AR.alloc([128, 8, 512], BF16)
        wmo = AR.alloc([128, 4, D], BF16)
        kTm = AR.alloc([128, 4, 256], BF16)
        vmem = AR.alloc([128, 2, 512], BF16)
        qmT = AR.alloc([128, 4, NT_], BF16)
        omT = AR.alloc([128, 4, NT_], BF16)
        qms = AR.alloc([16, 512], F32)
        xt4 = [AR.alloc([128, D], F32) for _ in range(2)]
        h4t = [AR.alloc([128, D], F32) for _ in range(2)]
        tmp4 = [(AR.alloc([128, D], BF16), AR.alloc([128, 1], F32), AR.alloc([128, 1], F32), AR.alloc([128, D], BF16)) for _ in range(2)]
        pT4 = [AR.alloc([128, 512], BF16) for _ in range(4)]
        rc4 = [AR.alloc([128, 512], F32) for _ in range(2)]
        kvo4 = [AR.alloc([128, 512], F32) for _ in range(2)]
        DMA(wmkv[:], w_mkv.rearrange("(kc p) n -> p kc n", p=128), [], ["wmkv"], q="pool")
        DMA(wmq[:], w_mq.rearrange("(kc p) n -> p kc n", p=128), [], ["wmq"], q="pool")
        DMA(wmo[:], w_mo.rearrange("(kc p) n -> p kc n", p=128), [], ["wmo"], q="pool")
        MQ = 128 ** -0.5
        for m_ in range(2):
            DMA(xt4[m_][:], mem_p[128 * m_:128 * (m_ + 1), :], [], [("xt4", m_)])
            norm_tile_to_nT(m_, xt4[m_], ("xt4", m_), 128, 2, tmp4[m_], m_, "nm_%d_" % m_, dst=memT[:, :, 128 * m_:128 * (m_ + 1)], dkey=("memT", m_))
        for m_ in range(2):
            for half in range(2):
                pb = 2 + half
                for kc in range(8):
                    MM(bank(pb)[:], memT[:, kc, 128 * m_:128 * (m_ + 1)], wmkv[:, kc, 512 * half:512 * (half + 1)], ["wmkv", ("memT", m_)], [("ps", pb)], start=(kc == 0), stop=(kc == 7))
                ko = kvo4[half]
                CP("act", ko[:], bank(pb)[:], [("ps", pb)], [("kvo4", half)])
                DMA((memk_p if half == 0 else memv_p)[128 * m_:128 * (m_ + 1), :], ko[:], [("kvo4", half)], [("memkv_p", m_, half)])
                if half == 1:
                    CP("dve", vmem[:, m_, :], bank(pb)[:], [("ps", pb)], [("vmem", m_)])
        for hh in range(4):
            for kc in range(8):
                MM(bank(4)[:, 0:256], wmkv[:, kc, 128 * hh:128 * (hh + 1)], memT[:, kc, :], ["wmkv", ("memT", 0), ("memT", 1)], [("ps", 4)], start=(kc == 0), stop=(kc == 7))
            CP("act", kTm[:, hh, :], bank(4)[:, 0:256], [("ps", 4)], [("kTm", hh)])
        rot = 0
        for hh in range(4):
            for bi, (t0, tn) in enumerate(TB):
                pb = rot % 2
                rot += 1
                nk = [("nT", j) for j in range(t0 // 128, (t0 + tn + 127) // 128)]
                for kc in range(8):
                    MM(bank(pb)[:, 0:tn], wmq[:, kc, 128 * hh:128 * (hh + 1)], nT[:, kc, t0:t0 + tn], ["wmq"] + nk, [("ps", pb)], start=(kc == 0), stop=(kc == 7))
                ACT(qmT[:, hh, t0:t0 + tn], bank(pb)[:, 0:tn], AF.Copy, [("ps", pb)], [("qmT", hh, bi)], scale=MQ)
        for kc in range(8):
            MM(bank(5)[0:16, :], nT[:, kc, NP_:NT_], wmq[:, kc, :], ["wmq", ("nT", 16)], [("ps", 5)], start=(kc == 0), stop=(kc == 7))
        ACT(qms[0:16, :], bank(5)[0:16, :], AF.Copy, [("ps", 5)], ["qms"], scale=MQ)
        DMA(qmscr, qms[0:16, :], ["qms"], ["qmscr"])
        for bi, (t0, tn) in enumerate(TB[:4]):
            for hh in range(4):
                q0 = 4 * ((bi * 4 + hh) % 2)
                pq = 2 * ((bi * 4 + hh) % 2)
                for m_ in range(2):
                    MM(bank(q0 + m_)[:, 0:tn], kTm[:, hh, 128 * m_:128 * (m_ + 1)], qmT[:, hh, t0:t0 + tn], [("kTm", hh), ("qmT", hh, bi)], [("ps", q0 + m_)])
                    ACT(pT4[pq + m_][:, 0:tn], bank(q0 + m_)[:, 0:tn], AF.Exp, [("ps", q0 + m_)], [("pT4", pq + m_)])
                for m_ in range(2):
                    MM(bank(q0 + 2)[:, 0:tn], vmem[:, m_, 128 * hh:128 * (hh + 1)], pT4[pq + m_][:, 0:tn], [("vmem", m_), ("pT4", pq + m_)], [("ps", q0 + 2)], start=(m_ == 0), stop=(m_ == 1))
                for m_ in range(2):
                    MM(bank(q0 + 3)[:, 0:tn], ones_b, pT4[pq + m_][:, 0:tn], ["cbf", ("pT4", pq + m_)], [("ps", q0 + 3)], start=(m_ == 0), stop=(m_ == 1))
                rb = (bi * 4 + hh) % 2
                ACT(rc4[rb][:, 0:tn], bank(q0 + 3)[:, 0:tn], AF.Ln, [("ps", q0 + 3)], [("rc4", rb)])
                ACT(rc4[rb][:, 0:tn], rc4[rb][:, 0:tn], AF.Exp, [("rc4", rb)], [("rc4", rb)], scale=-1.0)
                TT("dve", omT[:, hh, t0:t0 + tn], rc4[rb][:, 0:tn], bank(q0 + 2)[:, 0:tn], ALU.mult, [("rc4", rb), ("ps", q0 + 2)], [("omT", hh, bi)])
        Kt = [AR.alloc([128, 512], F32) for _ in range(4)]
        Vt = [AR.alloc([128, 512], BF16) for _ in range(4)]
        qb4 = [AR.alloc([128, 512], F32) for _ in range(2)]
        pr4 = [AR.alloc([128, 512], F32) for _ in range(2)]
        sc4 = [AR.alloc([128, 4], F32) for _ in range(2)]
        P4 = [AR.alloc([128, 4], BF16) for _ in range(4)]
        om4 = AR.alloc([4, 512], F32)
        rcp4 = AR.alloc([4, 1], F32)
        osb4 = bank(7)[:, 0:64].rearrange("p (h j) -> p h j", h=4)

        def d_load(s_):
            for m_ in range(2):
                kv = 2 * (s_ % 2) + m_
                DMA(Kt[kv][:], cmk[s_, 128 * m_:128 * (m_ + 1), :], [], [("Kt", kv)])
                DMA(Vt[kv][:], cmv[s_, 128 * m_:128 * (m_ + 1), :], [], [("Vt", kv)], q="pool")

        def d_s1(j):
            s_ = j // 4
            qb = qb4[j % 2]
            DMA(qb[:], qmscr[j].partition_broadcast(128), ["qmscr"], [("qb4", j % 2)])
            for m_ in range(2):
                kv = 2 * (s_ % 2) + m_
                b_ = (2 * j + m_) % 4
                TT("dve", pr4[m_][:], Kt[kv][:], qb[:], ALU.mult, [("Kt", kv), ("qb4", j % 2)], [("pr4", m_)])
                RED(sc4[m_][:], pr4[m_][:].rearrange("p (h d) -> p h d", h=4), [("pr4", m_)], [("sc4", m_)])
                ACT(P4[b_][:], sc4[m_][:], AF.Exp, [("sc4", m_)], [("P4", b_)])

        def d_s2(j):
            s_ = j // 4
            for m_ in range(2):
                kv = 2 * (s_ % 2) + m_
                b_ = (2 * j + m_) % 4
                MM(bank(6)[0:4, :], P4[b_][:], Vt[kv][:], [("P4", b_), ("Vt", kv)], [("ps", 6)], start=(m_ == 0), stop=(m_ == 1))
                MM(bank(5)[0:4, 0:1], P4[b_][:], ones_b[:, 0:1], [("P4", b_), "cbf"], [("ps", 5)], start=(m_ == 0), stop=(m_ == 1))
            S.dve(lambda e: e.reciprocal(out=rcp4[0:4, :], in_=bank(5)[0:4, 0:1]), [("ps", 5)], ["rcp4"])
            STT(om4[0:4, :], bank(6)[0:4, :], rcp4[0:4, 0:1], C("mbdm", 512, 4), ALU.mult, ALU.mult, [("ps", 6), "rcp4", "cst"], ["om4"])
            for hh in range(4):
                MM(osb4[:, hh, j:j + 1], om4[0:4, 128 * hh:128 * (hh + 1)], C("ones", 1, 4), ["om4", "cst"], [("ps", 7)], skip_group_check=True)

        d_load(0)
        d_load(1)
        d_s1(0)
        for j in range(16):
            if j + 1 < 16:
                d_s1(j + 1)
            d_s2(j)
            if (j + 1) % 4 == 0 and (j + 1) // 4 + 1 < 4:
                d_load((j + 1) // 4 + 1)
        CP("act", omT[:, :, NP_:NT_], osb4, [("ps", 7)], [("omT", "s")])
        if "omT" in dbg:
            dbg_out["omT"] = dout("dbg_omT", [128, 4 * NT_], BF16)
            S.dma(lambda e: e.dma_start(out=dbg_out["omT"], in_=omT[:].rearrange("p a b -> p (a b)")), r=[("omT", hh, bi) for hh in range(4) for bi in range(4)] + [("omT", "s")], w=["dbg_omT"])
        pend4 = None
        for i in range(NTILE + 1):
            rows = 128 if i < NTILE else NS_
            c0 = i * 128
            b = i % 2
            DMA(xt4[b][0:rows, :], hscr[c0:c0 + rows, :], [("hscr", i)], [("xt4", b)])
            omK = [("omT", hh, i // 4) for hh in range(4)] if i < NTILE else [("omT", "s")]
            for half in range(2):
                pb = (2 if i % 2 == 0 else 4) + half
                for kc in range(4):
                    MM(bank(pb)[0:rows, :], omT[:, kc, c0:c0 + rows], wmo[:, kc, 512 * half:512 * (half + 1)], ["wmo"] + omK, [("ps", pb)], start=(kc == 0), stop=(kc == 3))
                TT("dve", h4t[b][0:rows, 512 * half:512 * (half + 1)], xt4[b][0:rows, 512 * half:512 * (half + 1)], bank(pb)[0:rows, :], ALU.add, [("xt4", b), ("ps", pb)], [("h4t", b)])
            DMA(hscr[c0:c0 + rows, :], h4t[b][0:rows, :], [("h4t", b)], [("hscr", i)])
            nxt4 = norm_tile_to_nT(i, h4t[b], ("h4t", b), rows, 3, tmp4[b], b, "n3_%d_" % b, defer=True)
            if pend4 is not None:
                pend4()
            pend4 = nxt4
        pend4()
        if stop_after == 5:
            S.finalize(st)
            return nc, S, dbg_out

        barrier()
        AR.off = 0
        wdn = AR.alloc([128, 22, D], BF16)
        hid = AR.alloc([128, 22, 512], BF16)
        hidS = AR.alloc([128, 22, 16], BF16)
        NWGU = 6
        wgu = [[AR.alloc([128, 8, 128], BF16) for _ in range(2)] for _ in range(NWGU)]
        gpre = [AR.alloc([128, 514], F32) for _ in range(4)]
        gacc = [AR.alloc([128, 512], F32) for _ in range(4)]
        gpreS = AR.alloc([128, 4, 6], F32)
        gaccS = AR.alloc([128, 4, 4], F32)
        carry = AR.alloc([128, 22, 2], F32)
        fcb = AR.alloc([128, 22, 8], F32)
        fci = AR.alloc([8, FFN], F32)
        fco = AR.alloc([18, FFN], F32)
        xt5 = [AR.alloc([128, D], F32) for _ in range(2)]
        h5t = [AR.alloc([128, D], F32) for _ in range(2)]
        y5 = [AR.alloc([128, D], F32) for _ in range(2)]
        junk5 = AR.alloc([128, D], BF16)
        st5 = [AR.alloc([128, 2], F32) for _ in range(2)]
        DMA(wdn[:], w_dn.rearrange("(c p) n -> p c n", p=128), [], ["wdn"], q="pool")
        DMA(fci[0:8, :], fconv_in, [], ["fci"])
        MSET("pool", carry[:].rearrange("p a b -> p (a b)"), 0.0, [], ["carry"])
        for c in range(22):
            TR(bank(7)[:, 8 * c:8 * (c + 1)], fci[0:8, 128 * c:128 * (c + 1)], ident_f[0:8, 0:8], ["fci", "cst"], [("ps", 7)])
        CP("act", fcb[:].rearrange("p a b -> p (a b)"), bank(7)[:, 0:176], [("ps", 7)], ["fcb"])
        w_up_v = w_up.rearrange("(kc p) n -> p kc n", p=128)
        wfc0, bfc0 = PRM["wfc"], PRM["bfc"]
        nw5 = 0
        rot = 0
        pendB = []
        for bi, (t0, tn) in enumerate(TB[:4]):
            nk = [("nT", j) for j in range(t0 // 128, (t0 + tn) // 128)]
            lastb = bi == 3
            for c in range(22):
                wb = nw5 % NWGU
                nw5 += 1
                wg_, wu_ = wgu[wb]
                DMA(wg_[:], w_up_v[:, :, 128 * c:128 * (c + 1)], [], [("wgu", wb, 0)], q="pool")
                DMA(wu_[:], w_up_v[:, :, FFN + 128 * c:FFN + 128 * (c + 1)], [], [("wgu", wb, 1)], q="pool")
                gb = rot % 4
                pg, pu = rot % 3, 3 + rot % 3
                rot += 1
                for kc in range(8):
                    MM(bank(pg)[:, 0:tn], wg_[:, kc, :], nT[:, kc, t0:t0 + tn], [("wgu", wb, 0)] + nk, [("ps", pg)], start=(kc == 0), stop=(kc == 7))
                for kc in range(8):
                    MM(bank(pu)[:, 0:tn], wu_[:, kc, :], nT[:, kc, t0:t0 + tn], [("wgu", wb, 1)] + nk, [("ps", pu)], start=(kc == 0), stop=(kc == 7))
                w_ = lambda j, c=c: prm[:, wfc0 + 3 * c + j:wfc0 + 3 * c + j + 1]
                bcol = prm[:, bfc0 + c:bfc0 + c + 1]
                gp, ga = gpre[gb], gacc[gb]
                CP("act", gp[:, 0:2], carry[:, c, :], ["carry"], [("gpre", gb)])
                CP("act", gp[:, 2:2 + tn], bank(pg)[:, 0:tn], [("ps", pg)], [("gpre", gb)])
                ACT(ga[:, 0:tn], gp[:, 2:2 + tn], AF.Identity, [("gpre", gb), "prm"], [("gacc", gb)], scale=w_(2), bias=bcol)
                STT(ga[:, 0:tn], gp[:, 1:1 + tn], w_(1), ga[:, 0:tn], ALU.mult, ALU.add, [("gpre", gb), ("gacc", gb), "prm"], [("gacc", gb)])
                STT(ga[:, 0:tn], gp[:, 0:tn], w_(0), ga[:, 0:tn], ALU.mult, ALU.add, [("gpre", gb), ("gacc", gb), "prm"], [("gacc", gb)])
                CP("act", carry[:, c, :], gp[:, tn:tn + 2], [("gpre", gb)], ["carry"])
                def stageB(ga=ga, gb=gb, pu=pu, c=c, tn=tn):
                    ACT(ga[:, 0:tn], ga[:, 0:tn], AF.Silu, [("gacc", gb)], [("gacc", gb)])
                    TT("dve", hid[:, c, 0:tn], ga[:, 0:tn], bank(pu)[:, 0:tn], ALU.mult, [("gacc", gb), ("ps", pu)], [("hid", c)])
                for fB in pendB:
                    fB()
                pendB = [stageB]
                if lastb:
                    for kc in range(8):
                        MM(bank(6)[:, 0:16], wg_[:, kc, :], nT[:, kc, NP_:NT_], [("wgu", wb, 0), ("nT", 16)], [("ps", 6)], start=(kc == 0), stop=(kc == 7))
                    for kc in range(8):
                        MM(bank(6)[:, 16:32], wu_[:, kc, :], nT[:, kc, NP_:NT_], [("wgu", wb, 1), ("nT", 16)], [("ps", 6)], start=(kc == 0), stop=(kc == 7))
                    for kc in range(8):
                        MM(bank(6)[0:18, 32:160], nT[:, kc, NP_ - 2:NT_], wg_[:, kc, :], [("wgu", wb, 0), ("nT", 15), ("nT", 16)], [("ps", 6)], start=(kc == 0), stop=(kc == 7))
                    CP("act", fco[0:18, 128 * c:128 * (c + 1)], bank(6)[0:18, 32:160], [("ps", 6)], ["fco"])
                    CP("dve", gpreS[:, :, 0:2], fcb[:, c, :].rearrange("p (s i) -> p s i", s=4), ["fcb"], ["gpreS"])
                    CP("dve", gpreS[:, :, 2:6], bank(6)[:, 0:16].rearrange("p (s t) -> p s t", s=4), [("ps", 6)], ["gpreS"])
                    ACT(gaccS[:], gpreS[:, :, 2:6], AF.Identity, ["gpreS", "prm"], ["gaccS"], scale=w_(2), bias=bcol)
                    STT(gaccS[:], gpreS[:, :, 1:5], w_(1), gaccS[:], ALU.mult, ALU.add, ["gpreS", "gaccS", "prm"], ["gaccS"])
                    STT(gaccS[:], gpreS[:, :, 0:4], w_(0), gaccS[:], ALU.mult, ALU.add, ["gpreS", "gaccS", "prm"], ["gaccS"])
                    ACT(gaccS[:], gaccS[:], AF.Silu, ["gaccS"], ["gaccS"])
                    TT("dve", hidS[:, c, :].rearrange("p (s t) -> p s t", s=4), gaccS[:], bank(6)[:, 16:32].rearrange("p (s t) -> p s t", s=4), ALU.mult, ["gaccS", ("ps", 6)], [("hidS", c)])
            for fB in pendB:
                fB()
            pendB = []
            tiles5 = [(4 * bi + k, 128, hid, 128 * k, [("hid", c) for c in range(22)]) for k in range(4)]
            if lastb:
                tiles5.append((16, NS_, hidS, 0, [("hidS", c) for c in range(22)]))
            for (i, rows, hsrc, hc0, hk) in tiles5:
                b = i % 2
                c0 = i * 128
                DMA(xt5[b][0:rows, :], hscr[c0:c0 + rows, :], [("hscr", i)], [("xt5", b)])
                for half in range(2):
                    pb = 4 + half
                    for c in range(22):
                        MM(bank(pb)[0:rows, :], hsrc[:, c, hc0:hc0 + rows], wdn[:, c, 512 * half:512 * (half + 1)], ["wdn"] + hk, [("ps", pb)], start=(c == 0), stop=(c == 21))
                    TT("dve", h5t[b][0:rows, 512 * half:512 * (half + 1)], xt5[b][0:rows, 512 * half:512 * (half + 1)], bank(pb)[0:rows, :], ALU.add, [("xt5", b), ("ps", pb)], [("h5t", b)])
                S.act(lambda e, b=b, rows=rows: e.activation(out=junk5[0:rows, :], in_=h5t[b][0:rows, :], func=AF.Square, accum_out=st5[b][0:rows, 0:1]), [("h5t", b)], ["junk5", ("st5", b)])
                ACT(st5[b][0:rows, 1:2], st5[b][0:rows, 0:1], AF.Ln, [("st5", b)], [("st5", b)], scale=1.0 / D, bias=EPS)
                ACT(st5[b][0:rows, 1:2], st5[b][0:rows, 1:2], AF.Exp, [("st5", b)], [("st5", b)], scale=-0.5)
                STT(y5[b][0:rows, :], h5t[b][0:rows, :], st5[b][0:rows, 1:2], gfin[0:rows, :], ALU.mult, ALU.mult, [("h5t", b), ("st5", b), "gfin"], [("y5", b)])
                DMA((y_p[c0:c0 + rows, :] if i < NTILE else y_s), y5[b][0:rows, :], [("y5", b)], [("y", i)])
        DMA(fconv_p, fco[0:2, :], ["fco"], ["fconv_p"])
        for s_ in range(4):
            DMA(fconv_s[s_], fco[2 + 4 * s_ + 2:2 + 4 * s_ + 4, :], ["fco"], [("fconv_s", s_)])
        S.finalize(st)
    return nc, S, dbg_out


def prep_core_inputs(inp, core, shared=None):
    f = lambda a: np.ascontiguousarray(a, dtype=np.float32)
    if shared is None:
        shared = {}
        cst, tb, tbs, tbn = host_constants(np.asarray(inp["rel_bias"], np.float32))
        prm = np.zeros((128, PRM_W), np.float32)
        gl = [inp["norm_mix"][0], inp["norm_mem_q"][0], inp["norm_mem_kv"][0], inp["norm_ffn"][0]]
        for i, gvec in enumerate(gl):
            prm[:, PRM["gT"] + 8 * i:PRM["gT"] + 8 * (i + 1)] = np.asarray(gvec).reshape(8, 128).T
        prm[:, PRM["wcd"]:PRM["wcd"] + 96] = np.asarray(inp["w_conv_delta"][0]).T.reshape(24, 128, 4).transpose(1, 0, 2).reshape(128, 96)
        prm[:, PRM["ndo"]] = np.asarray(inp["norm_delta_out"][0])
        prm[:, PRM["wfc"]:PRM["wfc"] + 66] = np.asarray(inp["w_ffn_conv"][0]).T.reshape(22, 128, 3).transpose(1, 0, 2).reshape(128, 66)
        prm[:, PRM["bfc"]:PRM["bfc"] + 22] = np.asarray(inp["b_ffn_conv"][0]).reshape(22, 128).T
        shared.update(
            w_in=f(inp["w_in"][0]), w_ba=f(inp["w_branch_a"][0]), w_bb=f(inp["w_branch_b"][0]), w_out=f(inp["w_out"][0]),
            w_mq=f(inp["w_mem_q"][0]), w_mkv=f(inp["w_mem_kv"][0]), w_mo=f(inp["w_mem_o"][0]), w_up=f(inp["w_ffn_up"][0]),
            w_dn=f(inp["w_ffn_down"][0]), cst=cst, prm=prm, tb=tb.reshape(128, -1), tbs=tbs, tbn=tbn,
            alog=f(inp["a_log"][0]), dtb=f(inp["dt_bias"][0]), gfin=f(inp["norm_final"]))
    sl = slice(4 * core, 4 * core + 4)
    m = dict(shared)
    m.update(
        x_p=f(inp["x_prompt"][core]), x_s=f(inp["x_sample"][sl]).reshape(16, D),
        cache0=f(inp["cache_dil0_kv"][0, sl]).reshape(4, 128, 512), cache1=f(inp["cache_dil1_kv"][0, sl]).reshape(4, 512, 512),
        cache2=f(inp["cache_dil2_kv"][0, sl]).reshape(4, 2048, 512), s0=f(inp["state_delta"][0, sl]),
        dconv_in=f(inp["state_delta_conv"][0, sl]).reshape(12, 3072), cmk=f(inp["cache_mem_k"][0, sl]).reshape(4, 256, 512),
        cmv=f(inp["cache_mem_v"][0, sl]).reshape(4, 256, 512), fconv_in=f(inp["state_ffn_conv"][0, sl]).reshape(8, FFN),
        mem_p=f(inp["mem_prompt"][core]))
    return m, shared


_PROG = None


def kernel(**inp):
    global _PROG
    inp = {k: np.asarray(v) for k, v in inp.items()}
    if _PROG is None:
        _PROG = build_program()[0]
    nc = _PROG
    in_maps = []
    shared = None
    for core in range(8):
        m, shared = prep_core_inputs(inp, core, shared)
        in_maps.append(m)
    res = run_bass_kernel_spmd(nc, in_maps, core_ids=list(range(8)))
    R = res.results
    st = lambda name: np.stack([np.asarray(r[name], np.float32) for r in R], 0)
    y_prompt = st("y_p")
    y_sample = st("y_s").reshape(32, 4, D)
    outs = [y_prompt, y_sample]
    for g in range(3):
        outs.append(st("dil%d_p" % g).reshape(1, 8, WINS[g], 2, 4, 64))
    outs.append(st("delta_p").reshape(1, 8, 8, 128, 128))
    outs.append(st("dconv_p").reshape(1, 8, 3, 3072))
    outs.append(st("memk_p").reshape(1, 8, 256, 4, 128))
    outs.append(st("memv_p").reshape(1, 8, 256, 4, 128))
    outs.append(st("fconv_p").reshape(1, 8, 2, FFN))
    for g in range(3):
        outs.append(st("dil%d_s" % g).reshape(1, 32, WINS[g], 2, 4, 64))
    outs.append(st("delta_s").reshape(1, 32, 8, 128, 128))
    outs.append(st("dconv_s").reshape(1, 32, 3, 3072))
    outs.append(st("fconv_s").reshape(1, 32, 2, FFN))
    return tuple(outs)
```

```python
import math
import os
from contextlib import ExitStack
import numpy as np
import concourse.bass as bass
import concourse.mybir as mybir
from concourse.bass_utils import run_bass_kernel_spmd

F32 = mybir.dt.float32
BF16 = mybir.dt.bfloat16
AF = mybir.ActivationFunctionType
ALU = mybir.AluOpType
AX = mybir.AxisListType

ENGS = ("pe", "act", "dve", "pool", "sp")
N_DMA_SLOTS = 40


class Op:
    __slots__ = ("eng", "fn", "reads", "writes", "dma", "idx", "waits", "inc", "slot", "slot_val", "cnt_val")

    def __init__(self, eng, fn, reads, writes, dma):
        self.eng, self.fn, self.reads, self.writes, self.dma = eng, fn, tuple(reads), tuple(writes), dma
        self.waits = []
        self.inc = False
        self.slot = None
        self.slot_val = 0
        self.cnt_val = 0


class Sched:
    def __init__(self, nc):
        self.nc = nc
        self.ops = []

    def add(self, eng, fn, reads=(), writes=(), dma=False):
        writes = list(writes) + [k for k in reads if isinstance(k, tuple) and k[0] == "ps" and k not in writes]
        op = Op(eng, fn, reads, writes, dma)
        op.idx = len(self.ops)
        self.ops.append(op)
        return op

    def pe(self, fn, r=(), w=()):
        return self.add("pe", fn, r, w)

    def act(self, fn, r=(), w=()):
        return self.add("act", fn, r, w)

    def dve(self, fn, r=(), w=()):
        return self.add("dve", fn, r, w)

    def pool(self, fn, r=(), w=()):
        return self.add("pool", fn, r, w)

    def on(self, eng, fn, r=(), w=()):
        return self.add(eng, fn, r, w)

    def dma(self, fn, r=(), w=(), q="sp"):
        return self.add(q, fn, r, w, dma=True)

    def finalize(self, stack):
        nc = self.nc
        ops = self.ops
        last_w = {}
        readers = {}
        deps_of = []
        for op in ops:
            deps = {}
            for k in op.reads:
                lw = last_w.get(k)
                if lw is not None:
                    deps[lw.idx] = "raw"
            for k in op.writes:
                lw = last_w.get(k)
                if lw is not None and lw.idx not in deps:
                    deps[lw.idx] = "waw"
                for rd in readers.get(k, ()):
                    if rd.idx not in deps:
                        deps[rd.idx] = "war"
            for k in op.reads:
                readers.setdefault(k, []).append(op)
            for k in op.writes:
                last_w[k] = op
                readers[k] = []
            need = []
            for di, kind in deps.items():
                d = ops[di]
                if d is op:
                    continue
                if d.dma or op.dma:
                    need.append(d)
                elif d.eng != op.eng:
                    need.append(d)
                elif kind == "raw" and op.eng != "pe":
                    need.append(d)
            deps_of.append(need)
        slot_last = [None] * N_DMA_SLOTS
        slot_cnt = [0] * N_DMA_SLOTS
        nd = 0
        for op in ops:
            if op.dma:
                s = nd % N_DMA_SLOTS
                nd += 1
                op.slot = s
                slot_cnt[s] += 16
                op.slot_val = slot_cnt[s]
                if slot_last[s] is not None:
                    deps_of[op.idx].append(slot_last[s])
                slot_last[s] = op
                op.inc = True
        for op in ops:
            for d in deps_of[op.idx]:
                d.inc = True
        cnt = {e: 0 for e in ENGS}
        for op in ops:
            if not op.dma and op.inc:
                cnt[op.eng] += 1
                op.cnt_val = cnt[op.eng]
        sems = {e: stack.enter_context(nc.semaphore("c_" + e)) for e in ENGS}
        dsems = [stack.enter_context(nc.semaphore("d_%d" % i)) for i in range(N_DMA_SLOTS)]
        waited = {e: {} for e in ENGS}
        nw = 0
        for op in ops:
            wl = {}
            for d in deps_of[op.idx]:
                if d.dma:
                    key, val = ("d", d.slot), d.slot_val
                else:
                    key, val = ("c", d.eng), d.cnt_val
                if val > wl.get(key, 0):
                    wl[key] = val
            wd = waited[op.eng]
            for key, val in wl.items():
                if wd.get(key, 0) >= val:
                    continue
                wd[key] = val
                sem = dsems[key[1]] if key[0] == "d" else sems[key[1]]
                op.waits.append((sem, val))
                nw += 1
        self.stats = dict(n_ops=len(ops), n_waits=nw, n_dma=nd,
                          per_eng={e: sum(1 for o in ops if o.eng == e) for e in ENGS})
        by_eng = {e: [o for o in ops if o.eng == e] for e in ENGS}
        final_slots = [(dsems[s], slot_cnt[s]) for s in range(N_DMA_SLOTS) if slot_cnt[s] > 0]

        def emit(e, eng):
            for op in by_eng[e]:
                for sem, val in op.waits:
                    eng.wait_ge(sem, val)
                ins = op.fn(eng)
                if op.inc:
                    if op.dma:
                        ins.then_inc(dsems[op.slot], 16)
                    else:
                        ins.then_inc(sems[e], 1)
            if e == "sp":
                for sem, val in final_slots:
                    eng.wait_ge(sem, val)

        block = stack.enter_context(nc.Block())

        @block.sync
        def _(eng):
            emit("sp", eng)

        @block.tensor
        def _(eng):
            emit("pe", eng)

        @block.scalar
        def _(eng):
            emit("act", eng)

        @block.vector
        def _(eng):
            emit("dve", eng)

        @block.gpsimd
        def _(eng):
            emit("pool", eng)


D = 1024
NP_ = 2048
NS_ = 16
NT_ = NP_ + NS_
NTILE = 16
N_IN = 8464
FFN = 2816
NEG = -30000.0
EPS = 1e-6
DILS = (1, 4, 16)
WINS = (128, 512, 2048)
C_QA, C_KA, C_VA, C_QKV, C_Z, C_BETA, C_A, C_GA, C_GB = 0, 768, 1536, 2304, 5376, 6400, 6408, 6416, 7440

CST = {}
_o = 0
for _n in ("ident", "U", "V", "negS", "selA", "selB", "sel3", "ones"):
    CST[_n] = _o
    _o += 128
CST["mbd4"] = _o
_o += 256
CST["mbdm"] = _o
_o += 512
CST_W = _o

PRM = {}
_o = 0
for _n, _w in (("gT", 32), ("wcd", 96), ("ndo", 1), ("wfc", 66), ("bfc", 22)):
    PRM[_n] = _o
    _o += _w
PRM_W = _o


def _rel_bucket_np(dist):
    d = np.maximum(dist, 1).astype(np.float32)
    large = 16 + (np.log(d / np.float32(16)) / np.float32(math.log(2048 / 16)) * np.float32(16)).astype(np.int32)
    return np.where(dist < 16, dist, np.minimum(large, 31))


def host_constants(rel_bias):
    c = np.zeros((128, CST_W), np.float32)
    t = np.arange(128)
    same = (t[:, None] // 64) == (t[None, :] // 64)
    c[:, CST["ident"]:CST["ident"] + 128] = np.eye(128)
    c[:, CST["U"]:CST["U"] + 128] = (t[:, None] <= t[None, :]) & same
    c[:, CST["V"]:CST["V"] + 128] = (t[:, None] > t[None, :]) & same
    c[:, CST["negS"]:CST["negS"] + 128] = np.where((t[None, :] > t[:, None]) & same, 0.0, NEG)
    c[63, CST["selA"]:CST["selA"] + 128] = 1
    c[127, CST["selB"]:CST["selB"] + 128] = 1
    c[3, CST["sel3"]:CST["sel3"] + 128] = 1
    c[:, CST["ones"]:CST["ones"] + 128] = 1
    for h in range(4):
        c[h, CST["mbd4"] + 64 * h:CST["mbd4"] + 64 * (h + 1)] = 1
        c[h, CST["mbdm"] + 128 * h:CST["mbdm"] + 128 * (h + 1)] = 1
    bias = []
    for g in range(3):
        idx = _rel_bucket_np(DILS[g] * np.arange(129, dtype=np.int32))
        bias.append(rel_bias[idx][:, 4 * g:4 * g + 4].T.astype(np.float32))
    tb = np.full((128, 12, 2, 128), NEG, np.float32)
    mp = t[:, None]
    m = t[None, :]
    for g in range(3):
        for h in range(4):
            dd = np.where(m >= mp, bias[g][h][np.clip(m - mp, 0, 128)], NEG)
            pv = np.where(mp >= m, bias[g][h][np.clip(m - mp + 128, 0, 128)], NEG)
            tb[:, 4 * g + h, 0, :] = dd
            tb[:, 4 * g + h, 1, :] = pv
    tbs = np.full((128, 3, 4, 4), NEG, np.float32)
    tbn = np.full((4, 3, 4, 4), NEG, np.float32)
    for g in range(3):
        for tq in range(4):
            for h in range(4):
                col = bias[g][h][128 - t]
                if g == 0:
                    col = np.where(t <= 127 - tq, col, NEG)
                tbs[:, g, tq, h] = col
                for tp in range(4):
                    if g == 0:
                        if tp <= tq:
                            tbn[tp, g, tq, h] = bias[0][h][tq - tp]
                    elif tp == tq:
                        tbn[tp, g, tq, h] = bias[g][h][0]
    return c, tb, tbs.reshape(128, 48), tbn.reshape(4, 48)


def build_program(stop_after=None, dbg=()):
    nc = bass.Bass("TRN2", target_bir_lowering=False)
    st = ExitStack()

    def din(name, shape, dt=F32):
        return nc.dram_tensor(name, list(shape), dt, kind="ExternalInput").ap()

    def dout(name, shape, dt=F32):
        return nc.dram_tensor(name, list(shape), dt, kind="ExternalOutput").ap()

    def dscr(name, shape, dt=F32):
        return nc.dram_tensor(name, list(shape), dt, kind="Internal").ap()

    x_p = din("x_p", [NP_, D])
    x_s = din("x_s", [NS_, D])
    cache = [din("cache%d" % g, [4, WINS[g], 512]) for g in range(3)]
    s0_in = din("s0", [4, 8, 128, 128])
    dconv_in = din("dconv_in", [12, 3072])
    cmk = din("cmk", [4, 256, 512])
    cmv = din("cmv", [4, 256, 512])
    fconv_in = din("fconv_in", [8, FFN])
    mem_p = din("mem_p", [256, D])
    w_in = din("w_in", [D, N_IN])
    w_ba = din("w_ba", [256, D])
    w_bb = din("w_bb", [D, D])
    w_out = din("w_out", [D, D])
    w_mq = din("w_mq", [D, 512])
    w_mkv = din("w_mkv", [D, D])
    w_mo = din("w_mo", [512, D])
    w_up = din("w_up", [D, 2 * FFN])
    w_dn = din("w_dn", [FFN, D])
    cst_d = din("cst", [128, CST_W])
    prm_d = din("prm", [128, PRM_W])
    tb_d = din("tb", [128, 12 * 2 * 128])
    tbs_d = din("tbs", [128, 48])
    tbn_d = din("tbn", [4, 48])
    alog_d = din("alog", [8])
    dtb_d = din("dtb", [8])
    gfin_d = din("gfin", [D])

    y_p = dout("y_p", [NP_, D])
    y_s = dout("y_s", [NS_, D])
    dil_p = [dout("dil%d_p" % g, [WINS[g], 512]) for g in range(3)]
    delta_p = dout("delta_p", [8, 128, 128])
    dconv_p = dout("dconv_p", [3, 3072])
    memk_p = dout("memk_p", [256, 512])
    memv_p = dout("memv_p", [256, 512])
    fconv_p = dout("fconv_p", [2, FFN])
    dil_s = [dout("dil%d_s" % g, [4, WINS[g], 512]) for g in range(3)]
    delta_s = dout("delta_s", [4, 8, 128, 128])
    dconv_s = dout("dconv_s", [4, 3, 3072])
    fconv_s = dout("fconv_s", [4, 2, FFN])

    hscr = dscr("hscr", [NT_, D])
    qscr = dscr("qscr", [NS_, 768])
    qmscr = dscr("qmscr", [NS_, 512])
    wbf = dscr("wbf", [D, 4096], BF16)
    oa_scr = dscr("oa_scr", [64, 4 * NT_], BF16)
    ob_scr = dscr("ob_scr", [128, 8 * NT_], BF16)
    dbg_out = {}

    with st:
        S = Sched(nc)
        sbt = lambda name, shape, dt: st.enter_context(nc.sbuf_tensor(name, shape, dt))
        nT = sbt("nT", [128, 8, NT_], BF16)
        cstt = sbt("cst_sb", [128, CST_W], F32)
        prm = sbt("prm_sb", [128, PRM_W], F32)
        cbf = sbt("cbf", [128, 3, 128], BF16)
        gfin = sbt("gfin_sb", [128, D], F32)
        ARENA_W = 41700
        arena = sbt("arena", [128, ARENA_W], F32)
        psum = st.enter_context(nc.psum_tensor("psum", [128, 8, 512], F32))

        class Arena:
            def __init__(self):
                self.off = 0

            def reset(self):
                self.off = 0

            def alloc(self, shape, dt, parts=128):
                elems = int(np.prod(shape[1:]))
                words = (elems * (2 if dt == BF16 else 4) + 3) // 4
                words += words % 2
                assert self.off + words <= ARENA_W, ("arena overflow", self.off, words)
                v = arena[0:shape[0], self.off:self.off + words]
                self.off += words
                if dt == BF16:
                    v = v.bitcast(BF16)
                v = v[:, 0:elems]
                if len(shape) == 3:
                    v = v.rearrange("p (a b) -> p a b", a=shape[1])
                elif len(shape) == 4:
                    v = v.rearrange("p (a b c) -> p a b c", a=shape[1], b=shape[2])
                elif len(shape) == 5:
                    v = v.rearrange("p (a b c d) -> p a b c d", a=shape[1], b=shape[2], c=shape[3])
                return v

        AR = Arena()
        oaT = AR.alloc([64, 4, NT_], BF16)
        base_mark = AR.off

        def barrier():
            keys = set()
            for op in S.ops:
                keys.update(op.reads)
                keys.update(op.writes)
            keys = sorted(keys, key=str)
            oc = CST["ones"]
            S.dve(lambda e: e.memset(cbf[0:1, 2, 0:2], 1.0), r=[], w=keys + ["_bar"])
            S.act(lambda e: e.copy(out=cstt[0:1, oc:oc + 2], in_=cstt[0:1, oc + 2:oc + 4]), r=["_bar"], w=["_bar2"])
            S.pool(lambda e: e.memset(cbf[32:33, 2, 0:2], 1.0), r=["_bar"], w=["_bar3"])
            S.dma(lambda e: e.dma_start(out=cstt[64:65, oc:oc + 2], in_=cstt[64:65, oc + 2:oc + 4]), r=["_bar"], w=["_bar4"])

        def C(name, w=128, rows=128):
            return cstt[0:rows, CST[name]:CST[name] + w]

        ident_f = C("ident")
        ident_b = cbf[:, 0, :]
        negS_b = cbf[:, 1, :]
        ones_b = cbf[:, 2, :]

        def bank(i):
            return psum[:, i, :]

        def bank_bf(i):
            return psum[:, i, :].bitcast(BF16)

        def MM(out, lhsT, rhs, r, w, start=True, stop=True, **kw):
            S.pe(lambda e: e.matmul(out, lhsT=lhsT, rhs=rhs, start=start, stop=stop, **kw), r, w)

        def TR(out, in_, ident, r, w):
            S.pe(lambda e: e.transpose(out=out, in_=in_, identity=ident), r, w)

        def ACT(out, in_, func, r, w, **kw):
            S.act(lambda e: e.activation(out=out, in_=in_, func=func, **kw), r, w)

        def CP(eng, out, in_, r, w):
            if eng == "act":
                S.act(lambda e: e.copy(out=out, in_=in_), r, w)
            else:
                S.on(eng, lambda e: e.tensor_copy(out=out, in_=in_), r, w)

        def TT(eng, out, in0, in1, op, r, w):
            S.on(eng, lambda e: e.tensor_tensor(out=out, in0=in0, in1=in1, op=op), r, w)

        def TS(eng, out, in0, s1, op0, r, w, s2=None, op1=None):
            if op1 is None:
                S.on(eng, lambda e: e.tensor_scalar(out=out, in0=in0, scalar1=s1, scalar2=None, op0=op0), r, w)
            else:
                S.on(eng, lambda e: e.tensor_scalar(out=out, in0=in0, scalar1=s1, scalar2=s2, op0=op0, op1=op1), r, w)

        def STT(out, in0, scalar, in1, op0, op1, r, w):
            S.dve(lambda e: e.scalar_tensor_tensor(out=out, in0=in0, scalar=scalar, in1=in1, op0=op0, op1=op1), r, w)

        def RED(out, in_, r, w):
            S.dve(lambda e: e.tensor_reduce(out=out, in_=in_, axis=AX.X, op=ALU.add), r, w)

        def DMA(out, in_, r, w, q="sp"):
            S.dma(lambda e: e.dma_start(out=out, in_=in_), r, w, q=q)

        def MSET(eng, ap, val, r, w):
            S.on(eng, lambda e: e.memset(ap, val), r, w)

        S.dma(lambda e: e.dma_start(out=cstt[:], in_=cst_d), w=["cst"])
        S.dma(lambda e: e.dma_start(out=prm[:], in_=prm_d), w=["prm"])
        S.dma(lambda e: e.dma_start(out=gfin[:], in_=gfin_d.partition_broadcast(128)), w=["gfin"])
        S.dve(lambda e: e.tensor_copy(out=cbf[:, 0, :], in_=C("ident")), r=["cst"], w=["cbf"])
        S.dve(lambda e: e.tensor_copy(out=cbf[:, 1, :], in_=C("negS")), r=["cst"], w=["cbf"])
        S.dve(lambda e: e.tensor_copy(out=cbf[:, 2, :], in_=C("ones")), r=["cst"], w=["cbf"])
        gT = prm[:, PRM["gT"]:PRM["gT"] + 32].rearrange("p (a b) -> p a b", a=4)

        def norm_tile_to_nT(i, xt, xkey, rows, gidx, tmp, pbank, tag, dst=None, dkey=None, defer=False):
            junk, ssq, rstd, ub = tmp
            c0 = i * 128
            if dst is None:
                dst = nT[:, :, c0:c0 + rows]
                dkey = ("nT", i)
            S.act(lambda e: e.activation(out=junk[0:rows, :], in_=xt[0:rows, :], func=AF.Square, accum_out=ssq[0:rows, :]),
                  r=[xkey], w=[tag + "junk", tag + "ssq"])
            S.act(lambda e: e.activation(out=rstd[0:rows, :], in_=ssq[0:rows, :], func=AF.Ln, scale=1.0 / D, bias=EPS),
                  r=[tag + "ssq"], w=[tag + "rstd"])
            S.act(lambda e: e.activation(out=rstd[0:rows, :], in_=rstd[0:rows, :], func=AF.Exp, scale=-0.5),
                  r=[tag + "rstd"], w=[tag + "rstd"])
            S.dve(lambda e: e.tensor_scalar(out=ub[0:rows, :], in0=xt[0:rows, :], scalar1=rstd[0:rows, 0:1], scalar2=None, op0=ALU.mult),
                  r=[xkey, tag + "rstd"], w=[tag + "ub"])
            def part2():
                pb = bank_bf(pbank)[:, 0:1024].rearrange("p (a b) -> p a b", a=8)
                for kc in range(8):
                    TR(pb[:, kc, 0:rows], ub[0:rows, kc * 128:(kc + 1) * 128], ident_b[0:rows, 0:rows], [tag + "ub", "cbf"], [("ps", pbank)])
                TT("dve", dst, pb[:, :, 0:rows], gT[:, gidx, :].unsqueeze(2).to_broadcast([128, 8, rows]), ALU.mult, [("ps", pbank), "prm"], [dkey])
            if defer:
                return part2
            part2()
            return rstd

        AR.off = base_mark
        xts = [AR.alloc([128, D], F32) for _ in range(4)]
        tmps = [(AR.alloc([128, D], BF16), AR.alloc([128, 1], F32), AR.alloc([128, 1], F32), AR.alloc([128, D], BF16)) for _ in range(4)]
        pend1 = None
        for i in range(NTILE + 1):
            rows = 128 if i < NTILE else NS_
            b = i % 4
            src = x_p[i * 128:(i + 1) * 128, :] if i < NTILE else x_s
            DMA(xts[b][0:rows, :], src, [], [("xt", b)])
            nxt1 = norm_tile_to_nT(i, xts[b], ("xt", b), rows, 0, tmps[b], b, "n1_%d_" % b, defer=True)
            if pend1 is not None:
                pend1()
            pend1 = nxt1
        pend1()

        if "nT" in dbg:
            dbg_out["nT"] = dout("dbg_nT", [128, 8 * NT_], BF16)
            S.dma(lambda e: e.dma_start(out=dbg_out["nT"], in_=nT[:].rearrange("p a b -> p (a b)")), r=[("nT", i) for i in range(17)], w=["dbg_nT"])
        if stop_after == 1:
            S.finalize(st)
            return nc, S, dbg_out

        barrier()
        AR.off = base_mark
        qTa = AR.alloc([128, 6, NT_], BF16)
        kTa = AR.alloc([128, 6, NT_], BF16)
        va = AR.alloc([128, 3, 16, 4, 66], BF16)
        qs_tok = AR.alloc([16, 768], F32)
        kvnew = [AR.alloc([4, 3, 512], F32) for _ in range(4)]
        kvnb = [AR.alloc([4, 3, 256], BF16) for _ in range(4)]
        tbs = AR.alloc([128, 48], F32)
        tbn = AR.alloc([4, 48], F32)
        a_mark = AR.off
        wqk = [AR.alloc([128, 8, 128], BF16) for _ in range(3)]
        wkv = [AR.alloc([128, 8, 512], BF16) for _ in range(2)]
        kvo = [AR.alloc([128, 512], F32) for _ in range(2)]
        DMA(tbs[:], tbs_d, [], ["tbs"])
        DMA(tbn[:], tbn_d, [], ["tbn"])
        for g in range(3):
            for s_ in range(4):
                DMA(dil_s[g][s_, 0:WINS[g] - 4, :], cache[g][s_, 4:WINS[g], :], [], [("dil_s_old", g, s_)])
        w_in_v = w_in.rearrange("(kc p) n -> p kc n", p=128)
        SKIP = os.environ.get('DBG_SKIP', '').split(',')
        if 'memset' not in SKIP:
            S.pool(lambda e: e.memset(va[:].rearrange("p a b c d -> p (a b c d)"), 1.0), w=["va_init"])
        TB = [(0, 512), (512, 512), (1024, 512), (1536, 512), (2048, NS_)]
        rot = 0
        nw_ = 0
        for (dst, wkey, scale, c0) in ((qTa, "wq", 0.125, C_QA), (kTa, "wk", None, C_KA)):
            for c in range(6):
                wb = nw_ % 3
                nw_ += 1
                wt = wqk[wb]
                S.dma(lambda e, wt=wt, c=c, c0=c0: e.dma_start(out=wt[:], in_=w_in_v[:, :, c0 + c * 128:c0 + (c + 1) * 128]), w=[("wqk", wb)], q="pool")
                if wkey == "wq":
                    for kc in range(8):
                        MM(bank(2)[0:16, 0:128], nT[:, kc, NP_:NT_], wt[:, kc, :], [("wqk", wb), ("nT", 16)], [("ps", 2)], start=(kc == 0), stop=(kc == 7))
                    ACT(qs_tok[0:16, c * 128:(c + 1) * 128], bank(2)[0:16, 0:128], AF.Copy, [("ps", 2)], ["qs_tok"], scale=0.125)
                for (t0, tn) in TB:
                    pb = rot % 2
                    rot += 1
                    for kc in range(8):
                        S.pe(lambda e, kc=kc, pb=pb, t0=t0, tn=tn, wt=wt: e.matmul(bank(pb)[:, 0:tn], lhsT=wt[:, kc, :], rhs=nT[:, kc, t0:t0 + tn], start=(kc == 0), stop=(kc == 7)),
                             r=[("wqk", wb)] + [("nT", j) for j in range(t0 // 128, (t0 + tn + 127) // 128)], w=[("ps", pb)])
                    if scale is None:
                        S.act(lambda e, pb=pb, c=c, t0=t0, tn=tn, dst=dst: e.copy(out=dst[:, c, t0:t0 + tn], in_=bank(pb)[:, 0:tn]),
                              r=[("ps", pb)], w=[(wkey + "T", c, t0)])
                    else:
                        S.act(lambda e, pb=pb, c=c, t0=t0, tn=tn, dst=dst, scale=scale: e.activation(out=dst[:, c, t0:t0 + tn], in_=bank(pb)[:, 0:tn], func=AF.Copy, scale=scale),
                              r=[("ps", pb)], w=[(wkey + "T", c, t0)])
        if "qk" in dbg:
            dbg_out["qTa"] = dout("dbg_qTa", [128, 6 * NT_], BF16)
            dbg_out["kTa"] = dout("dbg_kTa", [128, 6 * NT_], BF16)
            S.dma(lambda e: e.dma_start(out=dbg_out["qTa"], in_=qTa[:].rearrange("p a b -> p (a b)")), r=[("wqT", c, t0) for c in range(6) for (t0, _) in TB], w=["dbg_q"])
            S.dma(lambda e: e.dma_start(out=dbg_out["kTa"], in_=kTa[:].rearrange("p a b -> p (a b)")), r=[("wkT", c, t0) for c in range(6) for (t0, _) in TB], w=["dbg_k"])
        if stop_after == 2.1:
            S.finalize(st)
            return nc, S, dbg_out
        qk_keys = lambda name, c: [(name + "T", c, t0) for (t0, _) in TB]

        def tok_ap(g, j):
            d = DILS[g]
            nb = NP_ // (128 * d)
            r, b = j // nb, j % nb
            start = d * 128 * b + r
            return start, d, r, b

        rot = 0
        for g in [int(v) for v in os.environ.get('DBG_G', '0,1,2').split(',')]:
            S.dma(lambda e, g=g: e.dma_start(out=wkv[g % 2][:, :, 0:256], in_=w_in_v[:, :, C_KA + 256 * g:C_KA + 256 * (g + 1)]), w=[("wkv", g % 2)], q="pool")
            S.dma(lambda e, g=g: e.dma_start(out=wkv[g % 2][:, :, 256:512], in_=w_in_v[:, :, C_VA + 256 * g:C_VA + 256 * (g + 1)]), w=[("wkv", g % 2)], q="pool")
            for s_ in range(4):
                for kc in range(8):
                    MM(bank(0)[0:4, :], nT[:, kc, NP_ + 4 * s_:NP_ + 4 * s_ + 4], wkv[g % 2][:, kc, :], [("wkv", g % 2), ("nT", 16)], [("ps", 0)], start=(kc == 0), stop=(kc == 7))
                CP("act", kvnew[s_][0:4, g, :], bank(0)[0:4, :], [("ps", 0)], [("kvnew", s_, g)])
                CP("dve", kvnb[s_][0:4, g, :], kvnew[s_][0:4, g, 256:512], [("kvnew", s_, g)], [("kvnb", s_, g)])
                DMA(dil_s[g][s_, WINS[g] - 4:WINS[g], :], kvnew[s_][0:4, g, :], [("kvnew", s_, g)], [("dil_s_new", g, s_)])
            for j in range(16):
                start, d, r, b = tok_ap(g, j)
                pb = 2 + rot % 2
                ko = kvo[rot % 2]
                kok = ("kvo", rot % 2)
                rot += 1
                for kc in range(8):
                    S.pe(lambda e, kc=kc, pb=pb, g=g, start=start, d=d: e.matmul(bank(pb)[:], lhsT=nT[:, kc, start:start + 127 * d + 1:d], rhs=wkv[g % 2][:, kc, :], start=(kc == 0), stop=(kc == 7)),
                         r=[("wkv", g % 2)] + [("nT", jj) for jj in range(16)], w=[("ps", pb)])
                if 'va' not in SKIP:
                  S.on(os.environ.get('DBG_VAENG', 'dve'), lambda e, pb=pb, g=g, j=j: (e.tensor_copy if os.environ.get('DBG_VAENG', 'dve') == 'dve' else e.copy)(out=va[:, g, j, :, 0:64], in_=bank(pb)[:, 256:512].rearrange("p (h d) -> p h d", h=4)),
                        r=[("ps", pb), "va_init"], w=[("va", g, j)])
                if d * 128 * b >= NP_ - WINS[g] and 'out' not in SKIP:
                    S.act(lambda e, pb=pb, ko=ko: e.copy(out=ko[:], in_=bank(pb)[:]), r=[("ps", pb)], w=[kok])
                    row0 = d * 128 * b + r - (NP_ - WINS[g])
                    S.dma(lambda e, ko=ko, g=g, row0=row0, d=d: e.dma_start(out=dil_p[g][row0:row0 + 127 * d + 1:d, :], in_=ko[:]), r=[kok], w=[("dil_p", g, j)])

        if stop_after == 2.2:
            S.finalize(st)
            return nc, S, dbg_out
        barrier()
        AR.off = a_mark
        for h in range(8):
            for qi, cw in enumerate((C_QKV + 128 * h, C_QKV + 1024 + 128 * h, C_QKV + 2048 + 128 * h, C_Z + 128 * h)):
                DMA(wbf[:, 512 * h + 128 * qi:512 * h + 128 * (qi + 1)], w_in[:, cw:cw + 128], [], [("wbf", h)], q="pool")
        tbh = AR.alloc([128, 12, 2, 128], BF16)
        tbl = AR.alloc([128, 12, 2, 128], BF16)
        m_tb = AR.off
        tb = AR.alloc([128, 12, 2, 128], F32)
        DMA(tb[:].rearrange("p a b c -> p (a b c)"), tb_d, [], ["tb"])
        tbf_ = tb[:].rearrange("p a b c -> p (a b c)")
        tbhf_ = tbh[:].rearrange("p a b c -> p (a b c)")
        tblf_ = tbl[:].rearrange("p a b c -> p (a b c)")
        CP("act", tbhf_, tbf_, ["tb"], ["tbh"])
        TT("dve", tblf_, tbf_, tbhf_, ALU.subtract, ["tb", "tbh"], ["tbl"])
        tb2s = AR.alloc([128, 4, 128], F32)
        CP("act", tb2s[:], tb[:, 8:12, 0, :], ["tb"], ["tb2s"])
        AR.off = m_tb
        tb2 = AR.alloc([128, 4, 128], F32)
        sc2 = AR.alloc([128, 512], F32)
        CP("act", tb2[:], tb2s[:], ["tb2s", "tbl", "tbh"], ["tb2"])
        p_sb = [AR.alloc([128, 512], BF16) for _ in range(3)]
        accs = [AR.alloc([65, 512], F32, parts=65) for _ in range(2)]
        ncb = 0
        nbt = 0
        for c in range(4):
            for h in range(4):
                blocks = []
                for i in range(4 * c, 4 * c + 4):
                    for kind, kt in ((0, i), (1, i - 1)):
                        if kt >= 0:
                            blocks.append((0, kt, kind, 128 * i, 1, 128, 128 * (i - 4 * c), 1))
                for r in range(4):
                    for kind, kb in ((0, c), (1, c - 1)):
                        if kb >= 0:
                            blocks.append((1, r * 4 + kb, kind, 512 * c + r, 4, 128, r, 4))
                for r in range(16):
                    blocks.append((2, r, 2, 512 * c + r, 16, 32, r, 16))
                batches = []
                cur, ccols, cg = [], 0, None
                for blk in blocks:
                    if cur and (blk[0] != cg or ccols + blk[5] > 512):
                        batches.append(cur)
                        cur, ccols = [], 0
                    cg = blk[0]
                    cur.append((blk, ccols))
                    ccols += blk[5]
                batches.append(cur)
                accb = 4 + ncb % 2
                acck = ("ps", accb)
                nbat = len(batches)

                def emit_score(bidx, batch, c=c, h=h):
                    sbk = (6, 7, 0, 1)[bidx % 4]
                    pbi = bidx % 3
                    g = batch[0][0][0]
                    nb_ = len(batch)
                    tot = batch[-1][1] + batch[-1][0][5]
                    for k_, (blk, off) in enumerate(batch):
                        _, kt, kind, qs, qst, nq, os_, ost = blk
                        gh = 4 * g + h
                        ch, r0 = gh // 2, 64 * (gh % 2)
                        ks, kd, _, _ = tok_ap(g, kt)
                        MM(bank(sbk)[:, off:off + nq], kTa[r0:r0 + 64, ch, ks:ks + 127 * kd + 1:kd], qTa[r0:r0 + 64, ch, qs:qs + (nq - 1) * qst + 1:qst],
                           qk_keys("wk", ch) + qk_keys("wq", ch), [("ps", sbk)], start=(k_ == 0), stop=(g == 2 and k_ == nb_ - 1), skip_group_check=True)
                    if g == 2:
                        TT("dve", sc2[:, 0:tot].rearrange("p (r q) -> p r q", q=32), bank(sbk)[:, 0:tot].rearrange("p (r q) -> p r q", q=32),
                           tb2[:, h, 32 * c:32 * c + 32].unsqueeze(1).to_broadcast([128, nb_, 32]), ALU.add, [("ps", sbk), "tb2"], ["sc2"])
                        ACT(p_sb[pbi][:, 0:tot], sc2[:, 0:tot], AF.Exp, ["sc2"], [("p", pbi)])
                        return
                    gh = 4 * g + h
                    runs = []
                    k_ = 0
                    while k_ < nb_:
                        blk, off = batch[k_]
                        if blk[2] == 0 and k_ + 1 < nb_ and batch[k_ + 1][0][2] == 1:
                            runs.append((off, 0, 2))
                            k_ += 2
                        else:
                            runs.append((off, blk[2], 1))
                            k_ += 1
                    nmm = 2 * len(runs)
                    im = 0
                    for (tsrc, tkey) in ((tbh, "tbh"), (tbl, "tbl")):
                        for (off, k0_, nk_) in runs:
                            im += 1
                            if nk_ == 2:
                                MM(bank(sbk)[:, off:off + 256].rearrange("p (a b) -> p a b", a=2), ident_b, tsrc[:, gh, :, :], ["cbf", tkey], [("ps", sbk)],
                                   start=False, stop=(im == nmm), skip_group_check=True)
                            else:
                                MM(bank(sbk)[:, off:off + 128], ident_b, tsrc[:, gh, k0_, :], ["cbf", tkey], [("ps", sbk)], start=False, stop=(im == nmm), skip_group_check=True)
                    ACT(p_sb[pbi][:, 0:tot], bank(sbk)[:, 0:tot], AF.Exp, [("ps", sbk)], [("p", pbi)])

                def emit_pv(bidx, batch, first, last, h=h, accb=accb, acck=acck):
                    pbi = bidx % 3
                    for k_, (blk, off) in enumerate(batch):
                        g, kt, kind, qs, qst, nq, os_, ost = blk
                        MM(bank(accb)[0:65, os_:os_ + (nq - 1) * ost + 1:ost], va[:, g, kt, h, 0:65], p_sb[pbi][:, off:off + nq],
                           [("p", pbi), ("va", g, kt)], [acck], start=(first and k_ == 0), stop=(last and k_ == len(batch) - 1), skip_group_check=True)

                LOOK = 2
                for bi in range(min(LOOK, nbat)):
                    emit_score(nbt + bi, batches[bi])
                for bi in range(nbat):
                    if bi + LOOK < nbat:
                        emit_score(nbt + bi + LOOK, batches[bi + LOOK])
                    emit_pv(nbt + bi, batches[bi], bi == 0, bi == nbat - 1)
                nbt += nbat
                ab = ncb % 2
                ncb += 1
                S.act(lambda e, ab=ab, accb=accb: e.copy(out=accs[ab][0:65, :], in_=bank(accb)[0:65, :]), r=[acck], w=[("accs", ab)])
                if "accs" in dbg:
                    if "accs" not in dbg_out:
                        dbg_out["accs"] = dout("dbg_accs", [16, 65, 512], F32)
                    S.dma(lambda e, ab=ab, c=c, h=h: e.dma_start(out=dbg_out["accs"][4 * c + h], in_=accs[ab][0:65, :]), r=[("accs", ab)], w=[("dbg_accs", c, h)])
                S.act(lambda e, ab=ab: e.activation(out=accs[ab][64:65, :], in_=accs[ab][64:65, :], func=AF.Ln), r=[("accs", ab)], w=[("accs", ab)])
                S.act(lambda e, ab=ab: e.activation(out=accs[ab][64:65, :], in_=accs[ab][64:65, :], func=AF.Exp, scale=-1.0), r=[("accs", ab)], w=[("accs", ab)])
                bcb = 2 + ab
                S.pe(lambda e, ab=ab, bcb=bcb: e.matmul(bank(bcb)[0:64, :], lhsT=cstt[64:65, CST["ones"]:CST["ones"] + 64], rhs=accs[ab][64:65, :], start=True, stop=True),
                     r=[("accs", ab), "cst"], w=[("ps", bcb)])
                S.dve(lambda e, ab=ab, bcb=bcb, c=c, h=h: e.tensor_tensor(out=oaT[0:64, h, 512 * c:512 * (c + 1)], in0=accs[ab][0:64, :], in1=bank(bcb)[0:64, :], op=ALU.mult),
                      r=[("accs", ab), ("ps", bcb)], w=[("oaT", h, c)])


        DMA(qscr, qs_tok[0:16, :], ["qs_tok"], ["qscr"])
        qbc = [AR.alloc([128, 768], F32) for _ in range(2)]
        kvt = [AR.alloc([128, 512], BF16) for _ in range(6)]
        prod = [AR.alloc([128, 256], F32) for _ in range(2)]
        prodn = AR.alloc([4, 256], F32)
        scs = [AR.alloc([128, 4], F32) for _ in range(2)]
        scn = AR.alloc([4, 4], F32)
        Ps = [AR.alloc([128, 4], BF16) for _ in range(2)]
        Pn = [AR.alloc([4, 4], BF16) for _ in range(2)]
        om = AR.alloc([4, 256], F32)
        rcp = AR.alloc([4, 1], F32)
        for b_ in range(6):
            MSET("pool", kvt[b_][:], 0.0, ["tbl"], [("kvt", b_)])
        osb = bank(3)[0:64, 0:64].rearrange("p (h j) -> p h j", h=4)
        prodn2 = [prodn, AR.alloc([4, 256], F32)]
        scn2 = [scn, AR.alloc([4, 4], F32)]
        Ps6 = Ps + [AR.alloc([128, 4], BF16) for _ in range(4)]
        Pn6 = Pn + [AR.alloc([4, 4], BF16) for _ in range(4)]

        def a4_tok(j):
            s_, t_ = j // 4, j % 4
            tp = j % 2
            qb = qbc[tp]
            DMA(qb[:], qscr[j].partition_broadcast(128), ["qscr", "tbl"], [("qbc", tp)])
            ab_ = 4 + tp
            for g in range(3):
                d = DILS[g]
                kb = (3 * j + g) % 6
                nrow = 128 - t_ if g == 0 else 128
                DMA(kvt[kb][0:nrow, :], cache[g][s_, t_:t_ + (nrow - 1) * d + 1:d, :], [], [("kvt", kb)], q="pool")
                TT("dve", prod[tp][:], kvt[kb][:, 0:256], qb[:, 256 * g:256 * (g + 1)], ALU.mult, [("kvt", kb), ("qbc", tp)], [("prod", tp)])
                yield
                RED(scs[tp][:], prod[tp][:].rearrange("p (h d) -> p h d", h=4), [("prod", tp)], [("scs", tp)])
                yield
                TT("dve", scs[tp][:], scs[tp][:], tbs[:, 16 * g + 4 * t_:16 * g + 4 * t_ + 4], ALU.add, [("scs", tp), "tbs"], [("scs", tp)])
                yield
                ACT(Ps6[kb][:], scs[tp][:], AF.Exp, [("scs", tp)], [("Ps", kb)])
                yield
                MM(bank(ab_)[0:4, 0:256], Ps6[kb][:], kvt[kb][:, 256:512], [("Ps", kb), ("kvt", kb)], [("ps", ab_)], start=(g == 0), stop=False, skip_group_check=True)
                MM(bank(ab_)[0:4, 256:257], Ps6[kb][:], ones_b[:, 0:1], [("Ps", kb), "cbf"], [("ps", ab_)], start=False, stop=False, skip_group_check=True)
                yield
                TT("dve", prodn2[tp][0:4, :], kvnew[s_][0:4, g, 0:256], qb[0:4, 256 * g:256 * (g + 1)], ALU.mult, [("kvnew", s_, g), ("qbc", tp)], [("prodn", tp)])
                yield
                RED(scn2[tp][0:4, :], prodn2[tp][0:4, :].rearrange("p (h d) -> p h d", h=4), [("prodn", tp)], [("scn", tp)])
                yield
                TT("dve", scn2[tp][0:4, :], scn2[tp][0:4, :], tbn[0:4, 16 * g + 4 * t_:16 * g + 4 * t_ + 4], ALU.add, [("scn", tp), "tbn"], [("scn", tp)])
                yield
                ACT(Pn6[kb][0:4, :], scn2[tp][0:4, :], AF.Exp, [("scn", tp)], [("Pn", kb)])
                yield
                MM(bank(ab_)[0:4, 0:256], Pn6[kb][0:4, :], kvnb[s_][0:4, g, :], [("Pn", kb), ("kvnb", s_, g)], [("ps", ab_)], start=False, stop=False, skip_group_check=True)
                MM(bank(ab_)[0:4, 256:257], Pn6[kb][0:4, :], ones_b[0:4, 0:1], [("Pn", kb), "cbf"], [("ps", ab_)], start=False, stop=(g == 2), skip_group_check=True)
                yield
            S.dve(lambda e: e.reciprocal(out=rcp[0:4, :], in_=bank(ab_)[0:4, 256:257]), [("ps", ab_)], ["rcp"])
            STT(om[0:4, :], bank(ab_)[0:4, 0:256], rcp[0:4, 0:1], C("mbd4", 256, 4), ALU.mult, ALU.mult, [("ps", ab_), "rcp", "cst"], ["om"])
            for h in range(4):
                MM(osb[:, h, j:j + 1], om[0:4, 64 * h:64 * (h + 1)], C("ones", 1, 4), ["om", "cst"], [("ps", 3)], start=True, stop=True, skip_group_check=True)

        def _ls4(gens):
            gens = list(gens)
            while gens:
                for g0_ in list(gens):
                    try:
                        next(g0_)
                    except StopIteration:
                        gens.remove(g0_)

        for j in range(0, 16, 2):
            _ls4([a4_tok(j), a4_tok(j + 1)])
        CP("act", oaT[0:64, :, NP_:NT_], osb, [("ps", 3)], [("oaT", "s")])

        if "oaT" in dbg:
            dbg_out["oaT"] = dout("dbg_oaT", [64, 4 * NT_], BF16)
            S.dma(lambda e: e.dma_start(out=dbg_out["oaT"], in_=oaT[:].rearrange("p a b -> p (a b)")), r=[("oaT", h, c) for h in range(4) for c in range(4)] + [("oaT", "s")], w=["dbg_oaT"])
        if stop_after == 2:
            S.finalize(st)
            return nc, S, dbg_out

        DMA(oa_scr, oaT[:].rearrange("p a b -> p (a b)"), [("oaT", h, c) for h in range(4) for c in range(4)] + [("oaT", "s")], ["oa_scr"])
        barrier()
        AR.off = 0
        NTI = 20
        TILES = [(i, 128, i * 128) for i in range(16)] + [(16 + s_, 4, NP_ + 4 * s_) for s_ in range(4)]
        wba = AR.alloc([128, 8, 16], BF16)
        bb, gg, Gc, eG, eGlG, dtA, dtB, beG, eGdk = [AR.alloc([128, NTI, 8], F32) for _ in range(9)]
        dtb_bc = AR.alloc([128, 8], F32)
        negA = AR.alloc([128, 8], F32)
        t1, t2, t3 = [AR.alloc([128, 8], F32) for _ in range(3)]
        DMA(wba[:], w_in_v[:, :, C_BETA:C_BETA + 16], [], ["wba"], q="pool")
        DMA(dtb_bc[:], dtb_d.partition_broadcast(128), [], ["dtb"])
        DMA(negA[:], alog_d.partition_broadcast(128), [], ["negA"])
        ACT(negA[:], negA[:], AF.Exp, ["negA"], ["negA"])
        TS("dve", negA[:], negA[:], -1.0, ALU.mult, ["negA"], ["negA"])
        nTk = [("nT", j) for j in range(17)]
        tsc = [[AR.alloc([128, 8], F32) for _ in range(3)] for _ in range(4)]

        def scal_tile(ti, W, c0, par):
            b0 = 2 * par
            t1p, t2p, t3p = tsc[par]
            for kc in range(8):
                MM(bank(b0)[0:W, 0:16], nT[:, kc, c0:c0 + W], wba[:, kc, :], ["wba"] + nTk, [("ps", b0)], start=(kc == 0), stop=(kc == 7))
            yield
            TT("dve", t1p[0:W, :], bank(b0)[0:W, 8:16], dtb_bc[0:W, :], ALU.add, [("ps", b0), "dtb"], [("t1", par)])
            yield
            ACT(t1p[0:W, :], t1p[0:W, :], AF.Exp, [("t1", par)], [("t1", par)])
            yield
            ACT(t1p[0:W, :], t1p[0:W, :], AF.Ln, [("t1", par)], [("t1", par)], bias=1.0)
            yield
            TT("dve", gg[0:W, ti, :], t1p[0:W, :], negA[0:W, :], ALU.mult, [("t1", par), "negA"], [("gg", ti)])
            yield
            ACT(t2p[0:W, :], bank(b0)[0:W, 0:8], AF.Exp, [("ps", b0)], [("t2", par)], scale=-1.0)
            yield
            ACT(t2p[0:W, :], t2p[0:W, :], AF.Ln, [("t2", par)], [("t2", par)], bias=1.0)
            yield
            ACT(bb[0:W, ti, :], t2p[0:W, :], AF.Exp, [("t2", par)], [("bb", ti)], scale=-1.0)
            yield
            MM(bank(b0 + 1)[0:W, 0:8], C("U")[0:W, 0:W], gg[0:W, ti, :], [("gg", ti), "cst"], [("ps", b0 + 1)])
            yield
            CP("act", Gc[0:W, ti, :], bank(b0 + 1)[0:W, 0:8], [("ps", b0 + 1)], [("Gc", ti)])
            yield
            ACT(eG[0:W, ti, :], bank(b0 + 1)[0:W, 0:8], AF.Exp, [("ps", b0 + 1)], [("eG", ti)])
            yield
            TT("dve", beG[0:W, ti, :], eG[0:W, ti, :], bb[0:W, ti, :], ALU.mult, [("eG", ti), ("bb", ti)], [("beG", ti)])
            yield
            TS("dve", eGdk[0:W, ti, :], eG[0:W, ti, :], 128 ** -0.5, ALU.mult, [("eG", ti)], [("eGdk", ti)])
            yield
            if W == 128:
                MM(bank(b0 + 1)[:, 8:16], C("selA"), Gc[:, ti, :], [("Gc", ti), "cst"], [("ps", b0 + 1)])
                yield
                MM(bank(b0 + 1)[:, 16:24], C("selB"), Gc[:, ti, :], [("Gc", ti), "cst"], [("ps", b0 + 1)])
                yield
                ACT(dtA[:, ti, :], bank(b0 + 1)[:, 8:16], AF.Exp, [("ps", b0 + 1)], [("dtA", ti)])
                yield
                ACT(dtB[:, ti, :], bank(b0 + 1)[:, 16:24], AF.Exp, [("ps", b0 + 1)], [("dtB", ti)])
                yield
                TT("dve", t3p[0:64, :], bank(b0 + 1)[0:64, 8:16], Gc[0:64, ti, :], ALU.subtract, [("ps", b0 + 1), ("Gc", ti)], [("t3", par)])
                yield
                TT("dve", t3p[64:128, :], bank(b0 + 1)[64:128, 16:24], Gc[64:128, ti, :], ALU.subtract, [("ps", b0 + 1), ("Gc", ti)], [("t3", par)])
                yield
                ACT(eGlG[:, ti, :], t3p[:, :], AF.Exp, [("t3", par)], [("eGlG", ti)])
                yield
            else:
                MM(bank(b0 + 1)[:, 8:16], C("sel3")[0:W, :], Gc[0:W, ti, :], [("Gc", ti), "cst"], [("ps", b0 + 1)])
                yield
                ACT(dtA[:, ti, :], bank(b0 + 1)[:, 8:16], AF.Exp, [("ps", b0 + 1)], [("dtA", ti)])
                yield
                TT("dve", t3p[0:W, :], bank(b0 + 1)[0:W, 8:16], Gc[0:W, ti, :], ALU.subtract, [("ps", b0 + 1), ("Gc", ti)], [("t3", par)])
                yield
                ACT(eGlG[0:W, ti, :], t3p[0:W, :], AF.Exp, [("t3", par)], [("eGlG", ti)])
                yield

        def _ls(gens):
            gens = list(gens)
            while gens:
                for g0_ in list(gens):
                    try:
                        next(g0_)
                    except StopIteration:
                        gens.remove(g0_)

        for q4 in range(0, len(TILES), 4):
            _ls([scal_tile(TILES[q4 + k][0], TILES[q4 + k][1], TILES[q4 + k][2], k) for k in range(min(4, len(TILES) - q4))])

        U4 = AR.alloc([128, 4, 128], F32)
        I4 = AR.alloc([128, 4, 128], F32)
        I4b = AR.alloc([128, 4, 128], BF16)
        negS4 = AR.alloc([128, 4, 128], BF16)
        for hl in range(4):
            CP("dve", U4[:, hl, :], C("U"), ["cst"], ["c4"])
            CP("dve", I4[:, hl, :], C("ident"), ["cst"], ["c4"])
            CP("dve", I4b[:, hl, :], C("ident"), ["cst"], ["c4"])
            CP("dve", negS4[:, hl, :], C("negS"), ["cst"], ["c4"])
        carry = AR.alloc([128, 8, 3, 3], F32)
        preS = AR.alloc([128, 3, 4, 7], F32)
        caccS = AR.alloc([128, 4, 4], F32)
        BLKS = [(0, 768), (768, 768), (1536, 512)]
        BW = 768 + 16
        qkvb = AR.alloc([128, 8, 3, BW], BF16)
        szb = AR.alloc([128, 8, BW], BF16)
        oTt = [AR.alloc([128, 8, 128], F32) for _ in range(2)]
        obt = [AR.alloc([128, 8, 128], BF16) for _ in range(2)]
        sqn = AR.alloc([128, 512], BF16)
        rsn = AR.alloc([128, 512], F32)
        tnn = AR.alloc([128, 512], F32)
        dco = tnn[0:19, 0:384]
        dcbh = rsn[0:12, 0:384].rearrange("p (q c) -> p q c", q=3)
        Sf = AR.alloc([128, 8, 128], F32)
        Sb = AR.alloc([128, 8, 128], BF16)

        class PBuf:
            pass

        GS = []
        tile_mark = AR.off
        for g_ in range(2):
            B = PBuf()
            B.squg = AR.alloc([128, 512], F32)
            B.ssq = AR.alloc([128, 8], F32)
            B.rqk = AR.alloc([128, 4, 2], F32)
            B.sc = AR.alloc([128, 5, 4], F32)
            B.Gm = AR.alloc([128, 4, 128], F32)
            B.Idg = AR.alloc([128, 4, 128], BF16)
            B.sck = AR.alloc([128, 4, 2, 128], BF16)
            B.Irk = AR.alloc([128, 4, 128], F32)
            B.P1 = AR.alloc([128, 4, 128], BF16)
            B.PT = [AR.alloc([128, 4, 128], BF16) for _ in range(2)]
            B.Y = [AR.alloc([128, 4, 128], BF16) for _ in range(2)]
            B.hand = []
            for _ in range(2):
                H_ = PBuf()
                H_.kbg = AR.alloc([128, 4, 128], BF16)
                H_.vb = AR.alloc([128, 4, 128], BF16)
                H_.P0 = AR.alloc([128, 4, 128], BF16)
                B.hand.append(H_)
            B.si = []
            for _ in range(3):
                I_ = PBuf()
                I_.u0 = AR.alloc([128, 4, 128], BF16)
                I_.wT = AR.alloc([128, 4, 128], BF16)
                I_.aT = AR.alloc([128, 4, 128], BF16)
                I_.qd = AR.alloc([128, 4, 128], BF16)
                I_.kdec = AR.alloc([128, 4, 128], BF16)
                I_.vn = AR.alloc([128, 4, 128], BF16)
                B.si.append(I_)
            GS.append(B)
        tile_end = AR.off
        AR.off = tile_mark
        Wh = [AR.alloc([128, 8, 512], BF16) for _ in range(2)]
        pre2 = [AR.alloc([128, 3, 3 + 768], F32) for _ in range(2)]
        cacc3 = [AR.alloc([128, 768], F32) for _ in range(3)]
        AR.off = max(AR.off, tile_end)
        MSET("pool", carry[:].rearrange("p a b c -> p (a b c)"), 0.0, [], ["carry"])
        w_in_dc = dconv_in.rearrange("r (q c) -> r q c", q=3)
        dconv_p_v = dconv_p.rearrange("r (q c) -> r q c", q=3)
        wcd0 = PRM["wcd"]
        ndo_col = prm[:, PRM["ndo"]:PRM["ndo"] + 1]
        DK = 128 ** -0.5
        wbf_v = wbf.rearrange("(kc p) n -> p kc n", p=128)

        def bc(ap2, W, n):
            return ap2.unsqueeze(2).to_broadcast([W, 4, n])

        def bank2(i):
            return psum[:, i:i + 2, :].rearrange("p a b -> p (a b)")

        def gdn_prep(g_, ti, W, c0, sib, ha, part):
            B = GS[g_]
            I_ = B.si[sib]
            H = B.hand[ha]
            KH = lambda n: (n, g_, "h", ha)
            bk = lambda b: (b + 3 * g_) % 6
            K_ = lambda n: (n, g_)
            KI = lambda n: (n, g_, sib)
            hs = slice(4 * g_, 4 * g_ + 4)
            qk_key = ("qkvb", g_)
            k_ = qkvb[:, hs, 1, c0:c0 + W]
            q_ = qkvb[:, hs, 0, c0:c0 + W]
            colv = lambda arr: arr[0:W, ti, hs]
            if part == "A":
                sqv = B.squg[:, :].bitcast(BF16)[:, 0:1024].rearrange("p (h q w) -> p h q w", h=4, q=2)
                ACT(sqv[:, :, :, 0:W], qkvb[:, hs, 0:2, c0:c0 + W], AF.Square, [qk_key], [K_("squg")])
                for hl in range(4):
                    for qq in range(2):
                        MM(bank(bk(0))[0:W, 2 * hl + qq:2 * hl + qq + 1], sqv[:, hl, qq, 0:W], ones_b[:, 0:1], [K_("squg"), "cbf"], [("ps", bk(0))])
                ACT(B.ssq[0:W, :], bank(bk(0))[0:W, 0:8], AF.Ln, [("ps", bk(0))], [K_("ssq")], bias=EPS)
                ACT(B.rqk[0:W, :, :].rearrange("p h q -> p (h q)"), B.ssq[0:W, :], AF.Exp, [K_("ssq")], [K_("rqk")], scale=-0.5)
                rk = B.rqk[0:W, :, 1]
                rq = B.rqk[0:W, :, 0]
                yield
                TT("pool", B.sc[0:W, 0, :], rk, colv(bb), ALU.mult, [K_("rqk"), ("bb", ti)], [K_("sc")])
                TS("pool", B.sc[0:W, 1, :], rq, DK, ALU.mult, [K_("rqk")], [K_("sc")])
                TT("pool", B.sc[0:W, 2, :], rq, colv(eGdk), ALU.mult, [K_("rqk"), ("eGdk", ti)], [K_("sc")])
                TT("pool", B.sc[0:W, 3, :], rk, colv(beG), ALU.mult, [K_("rqk"), ("beG", ti)], [K_("sc")])
                TT("pool", B.sc[0:W, 4, :], rk, colv(eGlG), ALU.mult, [K_("rqk"), ("eGlG", ti)], [K_("sc")])
                yield
                pbf = bank_bf(bk(1))[:, 0:1024].rearrange("p (h q d) -> p h q d", h=4, q=2)
                for hl in range(4):
                    TR(pbf[0:W, hl, 0, :], qkvb[:, 4 * g_ + hl, 1, c0:c0 + W], ident_b, [qk_key, "cbf"], [("ps", bk(1))])
                    TR(pbf[0:W, hl, 1, :], qkvb[:, 4 * g_ + hl, 2, c0:c0 + W], ident_b, [qk_key, "cbf"], [("ps", bk(1))])
                TT("dve", H.kbg[0:W, :, :], pbf[0:W, :, 0, :], bc(B.sc[0:W, 3, :], W, 128), ALU.mult, [("ps", bk(1)), K_("sc")], [KH("kbg")])
                TT("dve", I_.kdec[0:W, :, :], pbf[0:W, :, 0, :], bc(B.sc[0:W, 4, :], W, 128), ALU.mult, [("ps", bk(1)), K_("sc")], [KI("kdec")])
                TT("dve", H.vb[0:W, :, :], pbf[0:W, :, 1, :], bc(colv(bb), W, 128), ALU.mult, [("ps", bk(1)), ("bb", ti)], [KH("vb")])
                yield
                Ug = B.squg[0:W, :].rearrange("p (h c) -> p h c", h=4)[:, :, 0:W]
                TT("pool", Ug, U4[0:W, :, 0:W], bc(colv(gg), W, W), ALU.mult, ["c4", ("gg", ti), K_("squg")], [K_("squg")])
                Dv = bank(bk(2))[0:W, :].rearrange("p (h c) -> p h c", h=4)[:, :, 0:W]
                MM(Dv, C("V")[0:W, 0:W], Ug, [K_("squg"), "cst"], [("ps", bk(2))], start=True, stop=False)
                MM(Dv, ident_b[0:W, 0:W], negS4[0:W, :, 0:W], ["cbf", "c4"], [("ps", bk(2))], start=False, stop=True)
                ACT(B.Gm[0:W, :, 0:W], Dv, AF.Exp, [("ps", bk(2))], [K_("Gm")])
                TT("dve", B.Gm[0:W, :, 0:W], B.Gm[0:W, :, 0:W], bc(rk, W, W), ALU.mult, [K_("Gm"), K_("rqk")], [K_("Gm")])
                yield
                dsts = (B.sck[:, :, 0, 0:W], B.sck[:, :, 1, 0:W], I_.qd[:, :, 0:W])
                srcs = (k_, q_, q_)
                dkeys = (K_("sck"), K_("sck"), KI("qd"))
                for j in range(3):
                    yield
                    TT("pool", B.Idg[0:W, :, 0:W], I4b[0:W, :, 0:W], bc(B.sc[0:W, j, :], W, W), ALU.mult, ["c4", K_("sc")], [K_("Idg")])
                    rb = bk(3 + (j % 2))
                    Rv = bank(rb)[:, :].rearrange("p (h c) -> p h c", h=4)[:, :, 0:W]
                    MM(Rv, ones_b[0:W, :], B.Idg[0:W, :, 0:W], [K_("Idg"), "cbf"], [("ps", rb)])
                    TT("dve", dsts[j], Rv, srcs[j], ALU.mult, [("ps", rb), qk_key], [dkeys[j]])
                yield
                KKv = bank2(bk(4))[0:W, :].rearrange("p (h q c) -> p h q c", h=4, q=2)[:, :, :, 0:W]
                for hl in range(4):
                    MM(KKv[:, hl, :, :], qkvb[:, 4 * g_ + hl, 1, c0:c0 + W], B.sck[:, hl, :, 0:W], [qk_key, K_("sck")], [("ps", bk(4)), ("ps", bk(4) + 1)])
                TT("dve", H.P0[0:W, :, 0:W], KKv[:, :, 0, :], B.Gm[0:W, :, 0:W], ALU.mult, [("ps", bk(4)), ("ps", bk(4) + 1), K_("Gm")], [KH("P0")])
                TT("pool", B.Irk[0:W, :, 0:W], I4[0:W, :, 0:W], bc(rk, W, W), ALU.mult, ["c4", K_("rqk")], [K_("Irk")])
                TT("pool", B.Irk[0:W, :, 0:W], B.Irk[0:W, :, 0:W], B.Gm[0:W, :, 0:W], ALU.add, [K_("Irk"), K_("Gm")], [K_("Irk")])
                TT("dve", I_.aT[0:W, :, 0:W], KKv[:, :, 1, :], B.Irk[0:W, :, 0:W], ALU.mult, [("ps", bk(4)), ("ps", bk(4) + 1), K_("Irk")], [KI("aT")])
            else:
                Pl = [H.P0, B.P1]
                pkey = [KH("P0"), K_("P1")]
                ptv = bank_bf(bk(2))[:, 0:512].rearrange("p (h c) -> p h c", h=4)
                for hl in range(4):
                    TR(ptv[0:W, hl, 0:W], H.P0[0:W, hl, 0:W], ident_b[0:W, 0:W], [KH("P0"), "cbf"], [("ps", bk(2))])
                CP("act", B.PT[0][0:W, :, 0:W], ptv[0:W, :, 0:W], [("ps", bk(2))], [K_("PT0")])
                TT("pool", B.Y[0][0:W, :, 0:W], I4[0:W, :, 0:W], H.P0[0:W, :, 0:W], ALU.subtract, ["c4", KH("P0")], [K_("Y0")])
                n_it = 5 if W == 128 else 1
                v4 = lambda b_: bank(bk(b_))[0:W, :].rearrange("p (h c) -> p h c", h=4)[:, :, 0:W]
                for it in range(n_it):
                    yield
                    last = it == n_it - 1
                    a, b2 = it % 2, 1 - it % 2
                    for hl in range(4):
                        MM(v4(3)[:, hl, :], Pl[a][0:W, hl, 0:W], B.PT[a][0:W, hl, 0:W], [pkey[a], K_("PT%d" % a)], [("ps", bk(3))])
                    CP("act", B.PT[b2][0:W, :, 0:W], v4(3), [("ps", bk(3))], [K_("PT%d" % b2)])
                    yield
                    if not last:
                        for hl in range(4):
                            MM(v4(0)[:, hl, :], B.PT[a][0:W, hl, 0:W], Pl[a][0:W, hl, 0:W], [pkey[a], K_("PT%d" % a)], [("ps", bk(0))])
                        CP("act", Pl[b2][0:W, :, 0:W], v4(0), [("ps", bk(0))], [pkey[b2]])
                    yield
                    for hl in range(4):
                        MM(v4(2)[:, hl, :], B.PT[b2][0:W, hl, 0:W], B.Y[a][0:W, hl, 0:W], [K_("PT%d" % b2), K_("Y%d" % a)], [("ps", bk(2))])
                    TT("dve", B.Y[b2][0:W, :, 0:W], B.Y[a][0:W, :, 0:W], v4(2), ALU.add, [K_("Y%d" % a), ("ps", bk(2))], [K_("Y%d" % b2)])
                TTb = B.Y[n_it % 2]
                tk = K_("Y%d" % (n_it % 2))
                yield
                u0v = bank(bk(4))[0:W, :].rearrange("p (h d) -> p h d", h=4)
                for hl in range(4):
                    MM(u0v[:, hl, :], TTb[0:W, hl, 0:W], H.vb[0:W, hl, :], [tk, KH("vb")], [("ps", bk(4))])
                CP("act", I_.u0[0:W, :, :], u0v, [("ps", bk(4))], [KI("u0")])
                wTv = bank(bk(5))[:, :].rearrange("p (h c) -> p h c", h=4)[:, :, 0:W]
                for hl in range(4):
                    MM(wTv[:, hl, :], H.kbg[0:W, hl, :], TTb[0:W, hl, 0:W], [tk, KH("kbg")], [("ps", bk(5))])
                CP("dve" if g_ == 0 else "act", I_.wT[:, :, 0:W], wTv, [("ps", bk(5))], [KI("wT")])

        def gdn_seq(g_, ti, W, sib, r0, C_, dtarr, dtkey, ob):
            B = GS[g_]
            I_ = B.si[sib]
            KI = lambda n: (n, g_, sib)
            hs = slice(4 * g_, 4 * g_ + 4)
            Sk = ("S", g_)
            rs = slice(r0, r0 + C_)
            wsv = bank(6)[:, :].rearrange("p (h d) -> p h d", h=4)
            for hl in range(4):
                MM(wsv[rs, hl, :], I_.wT[:, hl, rs], Sb[:, 4 * g_ + hl, :], [KI("wT"), ("Sb", g_)], [("ps", 6)])
            TT("dve", I_.vn[rs, :, :], I_.u0[rs, :, :], wsv[rs, :, :], ALU.subtract, [KI("u0"), ("ps", 6)], [KI("vn")])
            yield
            TT("pool", Sf[:, hs, :], Sf[:, hs, :], bc(dtarr[:, ti, hs], 128, 128), ALU.mult, [("Sf", g_), dtkey], [("Sf", g_)])
            ov = bank(7)[:, :].rearrange("p (h c) -> p h c", h=4)
            for hl in range(4):
                MM(ov[:, hl, rs], Sb[:, 4 * g_ + hl, :], I_.qd[:, hl, rs], [("Sb", g_), KI("qd")], [("ps", 7)], start=True, stop=False)
                MM(ov[:, hl, rs], I_.vn[rs, hl, :], I_.aT[rs, hl, rs], [KI("vn"), KI("aT")], [("ps", 7)], start=False, stop=True)
            CP("act", oTt[ob][:, hs, rs], ov[:, :, rs], [("ps", 7)], [("oTt", ob, g_)])
            yield
            snv = bank(6)[:, :].rearrange("p (h d) -> p h d", h=4)
            for hl in range(4):
                MM(snv[:, hl, :], I_.kdec[rs, hl, :], I_.vn[rs, hl, :], [KI("kdec"), KI("vn")], [("ps", 6)])
            TT("dve", Sf[:, hs, :], Sf[:, hs, :], snv, ALU.add, [("Sf", g_), ("ps", 6)], [("Sf", g_)])
            CP("act", Sb[:, hs, :], Sf[:, hs, :], [("Sf", g_)], [("Sb", g_)])

        def gdn_norm(W, c0, gcol0, ob):
            for g_ in range(2):
                hs = slice(4 * g_, 4 * g_ + 4)
                sq_ = sqn[:, 0:4 * W].rearrange("p (h c) -> p h c", h=4)
                rs_ = rsn[:, 0:4 * W].rearrange("p (h c) -> p h c", h=4)
                tn_ = tnn[:, 0:4 * W].rearrange("p (h c) -> p h c", h=4)
                ACT(sq_, oTt[ob][:, hs, 0:W], AF.Square, [("oTt", ob, g_)], ["sqn"])
                yield
                msv = bank(0)[:, 0:4 * W].rearrange("p (h c) -> p h c", h=4)
                MM(msv, ones_b, sq_, ["sqn", "cbf"], [("ps", 0)])
                ACT(rs_, msv, AF.Ln, [("ps", 0)], ["rsn"], scale=1.0 / 128, bias=EPS)
                yield
                ACT(rs_, rs_, AF.Exp, ["rsn"], ["rsn"], scale=-0.5)
                yield
                STT(tn_, oTt[ob][:, hs, 0:W], ndo_col, rs_, ALU.mult, ALU.mult, [("oTt", ob, g_), "rsn", "prm"], ["tnn"])
                TT("dve", obt[ob][:, hs, 0:W], tn_, szb[:, hs, c0:c0 + W], ALU.mult, ["tnn", "szb"], [("obt", ob)])
                yield
            DMA(ob_scr_v[:, :, gcol0:gcol0 + W], obt[ob][:, :, 0:W], [("obt", ob)], [("ob_scr", gcol0)])

        ob_scr_v = ob_scr.rearrange("p (h t) -> p h t", h=8)

        def lockstep(gens):
            gens = list(gens)
            while gens:
                for gg_ in list(gens):
                    try:
                        next(gg_)
                    except StopIteration:
                        gens.remove(gg_)
        MSET("pool", Sf[:].rearrange("p a b -> p (a b)"), 0.0, [], [("Sf", 0), ("Sf", 1)])
        MSET("pool", Sb[:].rearrange("p a b -> p (a b)"), 0.0, [], [("Sb", 0), ("Sb", 1)])
        rot = 0
        nob = 0
        for b_, (t0, BT) in enumerate(BLKS):
            lastb = b_ == len(BLKS) - 1
            nk = [("nT", j) for j in range(t0 // 128, (t0 + BT) // 128)]
            subs = [(o, min(512, BT - o)) for o in range(0, BT, 512)]
            for h in range(8):
                W_ = Wh[h % 2]
                preh = pre2[h % 2]
                wk_ = ("Wh", h % 2)
                gk = ("qkvb", h // 4)
                DMA(W_[:], wbf_v[:, :, 512 * h:512 * (h + 1)], [("wbf", h)], [wk_])
                for qi in range(4):
                    for (so, sn) in subs:
                        pb = rot % 8
                        rot += 1
                        for kc in range(8):
                            MM(bank(pb)[:, 0:sn], W_[:, kc, 128 * qi:128 * (qi + 1)], nT[:, kc, t0 + so:t0 + so + sn], [wk_] + nk, [("ps", pb)], start=(kc == 0), stop=(kc == 7))
                        if qi == 3:
                            ACT(szb[:, h, so:so + sn], bank(pb)[:, 0:sn], AF.Silu, [("ps", pb)], ["szb"])
                        else:
                            CP("act" if (rot % 2) else "dve", preh[:, qi, 3 + so:3 + so + sn], bank(pb)[:, 0:sn], [("ps", pb)], [("pre", h % 2, qi)])
                if lastb:
                    for qi in range(4):
                        for kc in range(8):
                            MM(bank(5)[:, 16 * qi:16 * (qi + 1)], W_[:, kc, 128 * qi:128 * (qi + 1)], nT[:, kc, NP_:NT_], [wk_, ("nT", 16)], [("ps", 5)], start=(kc == 0), stop=(kc == 7))
                    CP("dve", preS[:, :, :, 3:7], bank(5)[:, 0:48].rearrange("p (q s t) -> p q s t", q=3, s=4), [("ps", 5)], [("preS", "new")])
                    ACT(szb[:, h, BT:BT + 16], bank(5)[:, 48:64], AF.Silu, [("ps", 5)], ["szb"])
                    for kc in range(8):
                        MM(bank(4)[0:19, 0:384], nT[:, kc, NP_ - 3:NT_], W_[:, kc, 0:384], [wk_, ("nT", 15), ("nT", 16)], [("ps", 4)], start=(kc == 0), stop=(kc == 7))
                    CP("act", dco[0:19, :], bank(4)[0:19, 0:384], [("ps", 4)], ["tnn"])
                    dco_v = dco[:, :].rearrange("r (q c) -> r q c", q=3)
                    DMA(dconv_p_v[:, :, 128 * h:128 * (h + 1)], dco_v[0:3, :, :], ["tnn"], [("dconv_p", h)])
                    for s_ in range(4):
                        DMA(dconv_s[s_].rearrange("r (q c) -> r q c", q=3)[:, :, 128 * h:128 * (h + 1)], dco_v[3 + 4 * s_ + 1:3 + 4 * s_ + 4, :, :], ["tnn"], [("dconv_s", h, s_)])
                    DMA(dcbh[0:12, :, :], w_in_dc[:, :, 128 * h:128 * (h + 1)], [], ["rsn"])
                    for qi in range(3):
                        TR(bank(4)[:, 384 + 12 * qi:384 + 12 * (qi + 1)], dcbh[0:12, qi, :], ident_f[0:12, 0:12], ["rsn", "cst"], [("ps", 4)])
                    CP("dve", preS[:, :, :, 0:3], bank(4)[:, 384:420].rearrange("p (q s i) -> p q s i", q=3, s=4), [("ps", 4)], [("preS", "buf")])
                CP("pool", preh[:, :, 0:3], carry[:, h, :, :], ["carry"], [("pre", h % 2, "c")])
                wc = lambda j, qi: prm[:, wcd0 + 4 * (8 * qi + h) + j:wcd0 + 4 * (8 * qi + h) + j + 1]
                for qi in range(3):
                    pk = [("pre", h % 2, qi), ("pre", h % 2, "c"), "prm"]
                    ACT(cacc3[qi][:, 0:BT], preh[:, qi, 3:3 + BT], AF.Copy, pk, [("cacc", qi)], scale=wc(3, qi))
                for qi in range(3):
                    pk = [("pre", h % 2, qi), ("pre", h % 2, "c"), "prm"]
                    for j in (2, 1, 0):
                        STT(cacc3[qi][:, 0:BT], preh[:, qi, j:j + BT], wc(j, qi), cacc3[qi][:, 0:BT], ALU.mult, ALU.add, pk + [("cacc", qi)], [("cacc", qi)])
                for qi in range(3):
                    ACT(qkvb[:, h, qi, 0:BT], cacc3[qi][:, 0:BT], AF.Silu, [("cacc", qi)], [gk])
                if lastb:
                    for qi in range(3):
                        ACT(caccS[:], preS[:, qi, :, 3:7], AF.Copy, [("preS", "new"), "prm"], ["caccS"], scale=wc(3, qi))
                        for j in (2, 1, 0):
                            STT(caccS[:], preS[:, qi, :, j:j + 4], wc(j, qi), caccS[:], ALU.mult, ALU.add, [("preS", "new"), ("preS", "buf"), "prm", "caccS"], ["caccS"])
                        ACT(qkvb[:, h, qi, BT:BT + 16].rearrange("p (s t) -> p s t", s=4), caccS[:], AF.Silu, ["caccS"], [gk])
                CP("pool", carry[:, h, :, :], preh[:, :, BT:BT + 3], [("pre", h % 2, 0), ("pre", h % 2, 1), ("pre", h % 2, 2)], ["carry"])
            tl_list = [(t0 // 128 + tl, 128, 128 * tl, t0 + 128 * tl) for tl in range(BT // 128)]
            if lastb:
                tl_list += [(16 + s_, 4, BT + 4 * s_, NP_ + 4 * s_) for s_ in range(4)]
            barrier()
            NTL = len(tl_list)
            pA = lambda n_: [gdn_prep(g_, tl_list[n_][0], tl_list[n_][1], tl_list[n_][2], n_ % 3, n_ % 2, "A") for g_ in range(2)]
            pB = lambda n_: [gdn_prep(g_, tl_list[n_][0], tl_list[n_][1], tl_list[n_][2], n_ % 3, n_ % 2, "B") for g_ in range(2)]
            lockstep(pA(0))
            lockstep((pA(1) if NTL > 1 else []) + pB(0))
            pend_norm = None
            for n_, (ti, W, c0, gcol) in enumerate(tl_list):
                sib = n_ % 3
                ob = nob % 2
                nob += 1
                gens = []
                if n_ + 2 < NTL:
                    gens += pA(n_ + 2)
                if n_ + 1 < NTL:
                    gens += pB(n_ + 1)
                if pend_norm is not None:
                    gens += [pend_norm]
                if W == 128:
                    def seq2(g_, ti=ti, W=W, sib=sib, ob=ob):
                        yield from gdn_seq(g_, ti, W, sib, 0, 64, dtA, ("dtA", ti), ob)
                        yield from gdn_seq(g_, ti, W, sib, 64, 64, dtB, ("dtB", ti), ob)
                    gens = [seq2(0), seq2(1)] + gens
                    lockstep(gens)
                    if ti == 15:
                        DMA(delta_p.rearrange("h k v -> k h v"), Sf[:], [("Sf", 0), ("Sf", 1)], ["delta_p"])
                else:
                    s_ = ti - 16
                    DMA(Sf[:], s0_in[s_].rearrange("h k v -> k h v"), ["delta_p"] + [("delta_s", j) for j in range(s_)], [("Sf", 0), ("Sf", 1)])
                    for g_ in range(2):
                        CP("act", Sb[:, 4 * g_:4 * g_ + 4, :], Sf[:, 4 * g_:4 * g_ + 4, :], [("Sf", g_)], [("Sb", g_)])
                    gens = [gdn_seq(g_, ti, W, sib, 0, 4, dtA, ("dtA", ti), ob) for g_ in range(2)] + gens
                    lockstep(gens)
                    DMA(delta_s[s_].rearrange("h k v -> k h v"), Sf[:], [("Sf", 0), ("Sf", 1)], [("delta_s", s_)])
                pend_norm = gdn_norm(W, c0, gcol, ob)
            lockstep([pend_norm])
            if not lastb:
                barrier()

        if "obT" in dbg:
            dbg_out["obT"] = dout("dbg_obT", [128, 8 * NT_], BF16)
            DMA(dbg_out["obT"], ob_scr, [("ob_scr", gc) for gc in list(range(0, NP_, 128)) + [NP_ + 4 * s_ for s_ in range(4)]], ["dbg_obT"])
        if stop_after == 3:
            S.finalize(st)
            return nc, S, dbg_out

        barrier()
        AR.off = 0
        oaT = AR.alloc([64, 4, NT_], BF16)
        obT = AR.alloc([128, 8, NT_], BF16)
        ob_all = [("ob_scr", gc) for gc in list(range(0, NP_, 128)) + [NP_ + 4 * s_ for s_ in range(4)]]
        oa_scr_v = oa_scr.rearrange("p (a b) -> p a b", a=4)
        ob_scr_v3 = ob_scr.rearrange("p (a b) -> p a b", a=8)
        for bi3, (t0, tn) in enumerate(TB):
            DMA(oaT[:, :, t0:t0 + tn], oa_scr_v[:, :, t0:t0 + tn], ["oa_scr"], [("oaT3", bi3)])
            DMA(obT[:, :, t0:t0 + tn], ob_scr_v3[:, :, t0:t0 + tn], ob_all, [("obT3", bi3)])
        mergedT = AR.alloc([128, 8, NT_], BF16)
        wout = AR.alloc([128, 8, D], BF16)
        wg3 = [[AR.alloc([128, 8, 128], BF16) for _ in range(3)] for _ in range(2)]
        wa3 = [AR.alloc([64, 4, 128], BF16) for _ in range(2)]
        sg3 = [AR.alloc([128, 512], F32) for _ in range(4)]
        m3 = [AR.alloc([128, 512], F32) for _ in range(4)]
        xt3 = [AR.alloc([128, D], F32) for _ in range(2)]
        h3t = [AR.alloc([128, D], F32) for _ in range(2)]
        tmp3 = [(AR.alloc([128, D], BF16), AR.alloc([128, 1], F32), AR.alloc([128, 1], F32), AR.alloc([128, D], BF16)) for _ in range(2)]
        DMA(wout[:], w_out.rearrange("(kc p) n -> p kc n", p=128), [], ["wout"], q="pool")
        w_bb_v = w_bb.rearrange("(kc p) n -> p kc n", p=128)
        w_ba_v = w_ba.rearrange("(h d) n -> d h n", d=64)
        for f in range(8):
            wb = f % 2
            DMA(wg3[wb][0][:], w_in_v[:, :, C_GA + 128 * f:C_GA + 128 * (f + 1)], [], [("wg3", wb, 0)], q="pool")
            DMA(wg3[wb][1][:], w_in_v[:, :, C_GB + 128 * f:C_GB + 128 * (f + 1)], [], [("wg3", wb, 1)], q="pool")
            DMA(wg3[wb][2][:], w_bb_v[:, :, 128 * f:128 * (f + 1)], [], [("wg3", wb, 2)], q="pool")
            DMA(wa3[wb][:], w_ba_v[:, :, 128 * f:128 * (f + 1)], [], [("wa3", wb)], q="pool")
            for bi, (t0, tn) in enumerate(TB):
                nk = [("nT", j) for j in range(t0 // 128, (t0 + tn + 127) // 128)]
                p0 = 4 * ((f * 5 + bi) % 2)
                s0 = 2 * ((f * 5 + bi) % 2)
                for kc in range(8):
                    MM(bank(p0 + 0)[:, 0:tn], wg3[wb][0][:, kc, :], nT[:, kc, t0:t0 + tn], [("wg3", wb, 0)] + nk, [("ps", p0 + 0)], start=(kc == 0), stop=(kc == 7))
                for kc in range(8):
                    MM(bank(p0 + 1)[:, 0:tn], wg3[wb][1][:, kc, :], nT[:, kc, t0:t0 + tn], [("wg3", wb, 1)] + nk, [("ps", p0 + 1)], start=(kc == 0), stop=(kc == 7))
                for hh in range(4):
                    MM(bank(p0 + 2)[:, 0:tn], wa3[wb][0:64, hh, :], oaT[0:64, hh, t0:t0 + tn], [("wa3", wb), ("oaT3", bi)], [("ps", p0 + 2)], start=(hh == 0), stop=(hh == 3))
                for kc in range(8):
                    MM(bank(p0 + 3)[:, 0:tn], wg3[wb][2][:, kc, :], obT[:, kc, t0:t0 + tn], [("wg3", wb, 2), ("obT3", bi)], [("ps", p0 + 3)], start=(kc == 0), stop=(kc == 7))
                ACT(sg3[s0 + 0][:, 0:tn], bank(p0 + 0)[:, 0:tn], AF.Sigmoid, [("ps", p0 + 0)], [("sg3", s0 + 0)])
                ACT(sg3[s0 + 1][:, 0:tn], bank(p0 + 1)[:, 0:tn], AF.Sigmoid, [("ps", p0 + 1)], [("sg3", s0 + 1)])
                TT("dve", m3[s0 + 0][:, 0:tn], sg3[s0 + 0][:, 0:tn], bank(p0 + 2)[:, 0:tn], ALU.mult, [("sg3", s0 + 0), ("ps", p0 + 2)], [("m3", s0 + 0)])
                TT("dve", m3[s0 + 1][:, 0:tn], sg3[s0 + 1][:, 0:tn], bank(p0 + 3)[:, 0:tn], ALU.mult, [("sg3", s0 + 1), ("ps", p0 + 3)], [("m3", s0 + 1)])
                TT("dve", mergedT[:, f, t0:t0 + tn], m3[s0 + 0][:, 0:tn], m3[s0 + 1][:, 0:tn], ALU.add, [("m3", s0 + 0), ("m3", s0 + 1)], [("mergedT", f, bi)])
        if "merged" in dbg:
            dbg_out["merged"] = dout("dbg_merged", [128, 8 * NT_], BF16)
            S.dma(lambda e: e.dma_start(out=dbg_out["merged"], in_=mergedT[:].rearrange("p a b -> p (a b)")), r=[("mergedT", f, bi) for f in range(8) for bi in range(5)], w=["dbg_merged"])
        mK = lambda bi: [("mergedT", f, bi) for f in range(8)]
        pend2 = None
        for i in range(NTILE + 1):
            rows = 128 if i < NTILE else NS_
            c0 = i * 128
            b = i % 2
            src = x_p[i * 128:(i + 1) * 128, :] if i < NTILE else x_s
            DMA(xt3[b][0:rows, :], src, [], [("xt3", b)])
            for half in range(2):
                pb = (4 if i % 2 == 0 else 2) + half
                for kc in range(8):
                    MM(bank(pb)[0:rows, :], mergedT[:, kc, c0:c0 + rows], wout[:, kc, 512 * half:512 * (half + 1)], ["wout"] + mK(min(i // 4, 4)), [("ps", pb)], start=(kc == 0), stop=(kc == 7))
                TT("dve", h3t[b][0:rows, 512 * half:512 * (half + 1)], xt3[b][0:rows, 512 * half:512 * (half + 1)], bank(pb)[0:rows, :], ALU.add, [("xt3", b), ("ps", pb)], [("h3t", b)])
            DMA(hscr[c0:c0 + rows, :], h3t[b][0:rows, :], [("h3t", b)], [("hscr", i)])
            nxt2 = norm_tile_to_nT(i, h3t[b], ("h3t", b), rows, 1, tmp3[b], 6 + b, "n2_%d_" % b, defer=True)
            if pend2 is not None:
                pend2()
            pend2 = nxt2
        pend2()
        if stop_after == 4:
            if "h1" in dbg:
                pass
            S.finalize(st)
            return nc, S, dbg_out

        barrier()
        AR.off = 0
        memT = AR.alloc([128, 8, 256], BF16)
        wmkv = AR.alloc([128, 8, D], BF16)
        wmq = AR.alloc([128, 8, 512], BF16)
        wmo = AR.alloc([128, 4, D], BF16)
        kTm = AR.alloc([128, 4, 256], BF16)
        vmem = AR.alloc([128, 2, 512], BF16)
        qmT = AR.alloc([128, 4, NT_], BF16)
        omT = AR.alloc([128, 4, NT_], BF16)
        qms = AR.alloc([16, 512], F32)
        xt4 = [AR.alloc([128, D], F32) for _ in range(2)]
        h4t = [AR.alloc([128, D], F32) for _ in range(2)]
        tmp4 = [(AR.alloc([128, D], BF16), AR.alloc([128, 1], F32), AR.alloc([128, 1], F32), AR.alloc([128, D], BF16)) for _ in range(2)]
        pT4 = [AR.alloc([128, 512], BF16) for _ in range(4)]
        rc4 = [AR.alloc([128, 512], F32) for _ in range(2)]
        kvo4 = [AR.alloc([128, 512], F32) for _ in range(2)]
        DMA(wmkv[:], w_mkv.rearrange("(kc p) n -> p kc n", p=128), [], ["wmkv"], q="pool")
        DMA(wmq[:], w_mq.rearrange("(kc p) n -> p kc n", p=128), [], ["wmq"], q="pool")
        DMA(wmo[:], w_mo.rearrange("(kc p) n -> p kc n", p=128), [], ["wmo"], q="pool")
        MQ = 128 ** -0.5
        for m_ in range(2):
            DMA(xt4[m_][:], mem_p[128 * m_:128 * (m_ + 1), :], [], [("xt4", m_)])
            norm_tile_to_nT(m_, xt4[m_], ("xt4", m_), 128, 2, tmp4[m_], m_, "nm_%d_" % m_, dst=memT[:, :, 128 * m_:128 * (m_ + 1)], dkey=("memT", m_))
        for m_ in range(2):
            for half in range(2):
                pb = 2 + half
                for kc in range(8):
                    MM(bank(pb)[:], memT[:, kc, 128 * m_:128 * (m_ + 1)], wmkv[:, kc, 512 * half:512 * (half + 1)], ["wmkv", ("memT", m_)], [("ps", pb)], start=(kc == 0), stop=(kc == 7))
                ko = kvo4[half]
                CP("act", ko[:], bank(pb)[:], [("ps", pb)], [("kvo4", half)])
                DMA((memk_p if half == 0 else memv_p)[128 * m_:128 * (m_ + 1), :], ko[:], [("kvo4", half)], [("memkv_p", m_, half)])
                if half == 1:
                    CP("dve", vmem[:, m_, :], bank(pb)[:], [("ps", pb)], [("vmem", m_)])
        for hh in range(4):
            for kc in range(8):
                MM(bank(4)[:, 0:256], wmkv[:, kc, 128 * hh:128 * (hh + 1)], memT[:, kc, :], ["wmkv", ("memT", 0), ("memT", 1)], [("ps", 4)], start=(kc == 0), stop=(kc == 7))
            CP("act", kTm[:, hh, :], bank(4)[:, 0:256], [("ps", 4)], [("kTm", hh)])
        rot = 0
        for hh in range(4):
            for bi, (t0, tn) in enumerate(TB):
                pb = rot % 2
                rot += 1
                nk = [("nT", j) for j in range(t0 // 128, (t0 + tn + 127) // 128)]
                for kc in range(8):
                    MM(bank(pb)[:, 0:tn], wmq[:, kc, 128 * hh:128 * (hh + 1)], nT[:, kc, t0:t0 + tn], ["wmq"] + nk, [("ps", pb)], start=(kc == 0), stop=(kc == 7))
                ACT(qmT[:, hh, t0:t0 + tn], bank(pb)[:, 0:tn], AF.Copy, [("ps", pb)], [("qmT", hh, bi)], scale=MQ)
        for kc in range(8):
            MM(bank(5)[0:16, :], nT[:, kc, NP_:NT_], wmq[:, kc, :], ["wmq", ("nT", 16)], [("ps", 5)], start=(kc == 0), stop=(kc == 7))
        ACT(qms[0:16, :], bank(5)[0:16, :], AF.Copy, [("ps", 5)], ["qms"], scale=MQ)
        DMA(qmscr, qms[0:16, :], ["qms"], ["qmscr"])
        for bi, (t0, tn) in enumerate(TB[:4]):
            for hh in range(4):
                q0 = 4 * ((bi * 4 + hh) % 2)
                pq = 2 * ((bi * 4 + hh) % 2)
                for m_ in range(2):
                    MM(bank(q0 + m_)[:, 0:tn], kTm[:, hh, 128 * m_:128 * (m_ + 1)], qmT[:, hh, t0:t0 + tn], [("kTm", hh), ("qmT", hh, bi)], [("ps", q0 + m_)])
                    ACT(pT4[pq + m_][:, 0:tn], bank(q0 + m_)[:, 0:tn], AF.Exp, [("ps", q0 + m_)], [("pT4", pq + m_)])
                for m_ in range(2):
                    MM(bank(q0 + 2)[:, 0:tn], vmem[:, m_, 128 * hh:128 * (hh + 1)], pT4[pq + m_][:, 0:tn], [("vmem", m_), ("pT4", pq + m_)], [("ps", q0 + 2)], start=(m_ == 0), stop=(m_ == 1))
                for m_ in range(2):
                    MM(bank(q0 + 3)[:, 0:tn], ones_b, pT4[pq + m_][:, 0:tn], ["cbf", ("pT4", pq + m_)], [("ps", q0 + 3)], start=(m_ == 0), stop=(m_ == 1))
                rb = (bi * 4 + hh) % 2
                ACT(rc4[rb][:, 0:tn], bank(q0 + 3)[:, 0:tn], AF.Ln, [("ps", q0 + 3)], [("rc4", rb)])
                ACT(rc4[rb][:, 0:tn], rc4[rb][:, 0:tn], AF.Exp, [("rc4", rb)], [("rc4", rb)], scale=-1.0)
                TT("dve", omT[:, hh, t0:t0 + tn], rc4[rb][:, 0:tn], bank(q0 + 2)[:, 0:tn], ALU.mult, [("rc4", rb), ("ps", q0 + 2)], [("omT", hh, bi)])
        Kt = [AR.alloc([128, 512], F32) for _ in range(4)]
        Vt = [AR.alloc([128, 512], BF16) for _ in range(4)]
        qb4 = [AR.alloc([128, 512], F32) for _ in range(2)]
        pr4 = [AR.alloc([128, 512], F32) for _ in range(2)]
        sc4 = [AR.alloc([128, 4], F32) for _ in range(2)]
        P4 = [AR.alloc([128, 4], BF16) for _ in range(4)]
        om4 = AR.alloc([4, 512], F32)
        rcp4 = AR.alloc([4, 1], F32)
        osb4 = bank(7)[:, 0:64].rearrange("p (h j) -> p h j", h=4)

        def d_load(s_):
            for m_ in range(2):
                kv = 2 * (s_ % 2) + m_
                DMA(Kt[kv][:], cmk[s_, 128 * m_:128 * (m_ + 1), :], [], [("Kt", kv)])
                DMA(Vt[kv][:], cmv[s_, 128 * m_:128 * (m_ + 1), :], [], [("Vt", kv)], q="pool")

        def d_s1(j):
            s_ = j // 4
            qb = qb4[j % 2]
            DMA(qb[:], qmscr[j].partition_broadcast(128), ["qmscr"], [("qb4", j % 2)])
            for m_ in range(2):
                kv = 2 * (s_ % 2) + m_
                b_ = (2 * j + m_) % 4
                TT("dve", pr4[m_][:], Kt[kv][:], qb[:], ALU.mult, [("Kt", kv), ("qb4", j % 2)], [("pr4", m_)])
                RED(sc4[m_][:], pr4[m_][:].rearrange("p (h d) -> p h d", h=4), [("pr4", m_)], [("sc4", m_)])
                ACT(P4[b_][:], sc4[m_][:], AF.Exp, [("sc4", m_)], [("P4", b_)])

        def d_s2(j):
            s_ = j // 4
            for m_ in range(2):
                kv = 2 * (s_ % 2) + m_
                b_ = (2 * j + m_) % 4
                MM(bank(6)[0:4, :], P4[b_][:], Vt[kv][:], [("P4", b_), ("Vt", kv)], [("ps", 6)], start=(m_ == 0), stop=(m_ == 1))
                MM(bank(5)[0:4, 0:1], P4[b_][:], ones_b[:, 0:1], [("P4", b_), "cbf"], [("ps", 5)], start=(m_ == 0), stop=(m_ == 1))
            S.dve(lambda e: e.reciprocal(out=rcp4[0:4, :], in_=bank(5)[0:4, 0:1]), [("ps", 5)], ["rcp4"])
            STT(om4[0:4, :], bank(6)[0:4, :], rcp4[0:4, 0:1], C("mbdm", 512, 4), ALU.mult, ALU.mult, [("ps", 6), "rcp4", "cst"], ["om4"])
            for hh in range(4):
                MM(osb4[:, hh, j:j + 1], om4[0:4, 128 * hh:128 * (hh + 1)], C("ones", 1, 4), ["om4", "cst"], [("ps", 7)], skip_group_check=True)

        d_load(0)
        d_load(1)
        d_s1(0)
        for j in range(16):
            if j + 1 < 16:
                d_s1(j + 1)
            d_s2(j)
            if (j + 1) % 4 == 0 and (j + 1) // 4 + 1 < 4:
                d_load((j + 1) // 4 + 1)
        CP("act", omT[:, :, NP_:NT_], osb4, [("ps", 7)], [("omT", "s")])
        if "omT" in dbg:
            dbg_out["omT"] = dout("dbg_omT", [128, 4 * NT_], BF16)
            S.dma(lambda e: e.dma_start(out=dbg_out["omT"], in_=omT[:].rearrange("p a b -> p (a b)")), r=[("omT", hh, bi) for hh in range(4) for bi in range(4)] + [("omT", "s")], w=["dbg_omT"])
        pend4 = None
        for i in range(NTILE + 1):
            rows = 128 if i < NTILE else NS_
            c0 = i * 128
            b = i % 2
            DMA(xt4[b][0:rows, :], hscr[c0:c0 + rows, :], [("hscr", i)], [("xt4", b)])
            omK = [("omT", hh, i // 4) for hh in range(4)] if i < NTILE else [("omT", "s")]
            for half in range(2):
                pb = (2 if i % 2 == 0 else 4) + half
                for kc in range(4):
                    MM(bank(pb)[0:rows, :], omT[:, kc, c0:c0 + rows], wmo[:, kc, 512 * half:512 * (half + 1)], ["wmo"] + omK, [("ps", pb)], start=(kc == 0), stop=(kc == 3))
                TT("dve", h4t[b][0:rows, 512 * half:512 * (half + 1)], xt4[b][0:rows, 512 * half:512 * (half + 1)], bank(pb)[0:rows, :], ALU.add, [("xt4", b), ("ps", pb)], [("h4t", b)])
            DMA(hscr[c0:c0 + rows, :], h4t[b][0:rows, :], [("h4t", b)], [("hscr", i)])
            nxt4 = norm_tile_to_nT(i, h4t[b], ("h4t", b), rows, 3, tmp4[b], b, "n3_%d_" % b, defer=True)
            if pend4 is not None:
                pend4()
            pend4 = nxt4
        pend4()
        if stop_after == 5:
            S.finalize(st)
            return nc, S, dbg_out

        barrier()
        AR.off = 0
        wdn = AR.alloc([128, 22, D], BF16)
        hid = AR.alloc([128, 22, 512], BF16)
        hidS = AR.alloc([128, 22, 16], BF16)
        NWGU = 6
        wgu = [[AR.alloc([128, 8, 128], BF16) for _ in range(2)] for _ in range(NWGU)]
        gpre = [AR.alloc([128, 514], F32) for _ in range(4)]
        gacc = [AR.alloc([128, 512], F32) for _ in range(4)]
        gpreS = AR.alloc([128, 4, 6], F32)
        gaccS = AR.alloc([128, 4, 4], F32)
        carry = AR.alloc([128, 22, 2], F32)
        fcb = AR.alloc([128, 22, 8], F32)
        fci = AR.alloc([8, FFN], F32)
        fco = AR.alloc([18, FFN], F32)
        xt5 = [AR.alloc([128, D], F32) for _ in range(2)]
        h5t = [AR.alloc([128, D], F32) for _ in range(2)]
        y5 = [AR.alloc([128, D], F32) for _ in range(2)]
        junk5 = AR.alloc([128, D], BF16)
        st5 = [AR.alloc([128, 2], F32) for _ in range(2)]
        DMA(wdn[:], w_dn.rearrange("(c p) n -> p c n", p=128), [], ["wdn"], q="pool")
        DMA(fci[0:8, :], fconv_in, [], ["fci"])
        MSET("pool", carry[:].rearrange("p a b -> p (a b)"), 0.0, [], ["carry"])
        for c in range(22):
            TR(bank(7)[:, 8 * c:8 * (c + 1)], fci[0:8, 128 * c:128 * (c + 1)], ident_f[0:8, 0:8], ["fci", "cst"], [("ps", 7)])
        CP("act", fcb[:].rearrange("p a b -> p (a b)"), bank(7)[:, 0:176], [("ps", 7)], ["fcb"])
        w_up_v = w_up.rearrange("(kc p) n -> p kc n", p=128)
        wfc0, bfc0 = PRM["wfc"], PRM["bfc"]
        nw5 = 0
        rot = 0
        pendB = []
        for bi, (t0, tn) in enumerate(TB[:4]):
            nk = [("nT", j) for j in range(t0 // 128, (t0 + tn) // 128)]
            lastb = bi == 3
            for c in range(22):
                wb = nw5 % NWGU
                nw5 += 1
                wg_, wu_ = wgu[wb]
                DMA(wg_[:], w_up_v[:, :, 128 * c:128 * (c + 1)], [], [("wgu", wb, 0)], q="pool")
                DMA(wu_[:], w_up_v[:, :, FFN + 128 * c:FFN + 128 * (c + 1)], [], [("wgu", wb, 1)], q="pool")
                gb = rot % 4
                pg, pu = rot % 3, 3 + rot % 3
                rot += 1
                for kc in range(8):
                    MM(bank(pg)[:, 0:tn], wg_[:, kc, :], nT[:, kc, t0:t0 + tn], [("wgu", wb, 0)] + nk, [("ps", pg)], start=(kc == 0), stop=(kc == 7))
                for kc in range(8):
                    MM(bank(pu)[:, 0:tn], wu_[:, kc, :], nT[:, kc, t0:t0 + tn], [("wgu", wb, 1)] + nk, [("ps", pu)], start=(kc == 0), stop=(kc == 7))
                w_ = lambda j, c=c: prm[:, wfc0 + 3 * c + j:wfc0 + 3 * c + j + 1]
                bcol = prm[:, bfc0 + c:bfc0 + c + 1]
                gp, ga = gpre[gb], gacc[gb]
                CP("act", gp[:, 0:2], carry[:, c, :], ["carry"], [("gpre", gb)])
                CP("act", gp[:, 2:2 + tn], bank(pg)[:, 0:tn], [("ps", pg)], [("gpre", gb)])
                ACT(ga[:, 0:tn], gp[:, 2:2 + tn], AF.Identity, [("gpre", gb), "prm"], [("gacc", gb)], scale=w_(2), bias=bcol)
                STT(ga[:, 0:tn], gp[:, 1:1 + tn], w_(1), ga[:, 0:tn], ALU.mult, ALU.add, [("gpre", gb), ("gacc", gb), "prm"], [("gacc", gb)])
                STT(ga[:, 0:tn], gp[:, 0:tn], w_(0), ga[:, 0:tn], ALU.mult, ALU.add, [("gpre", gb), ("gacc", gb), "prm"], [("gacc", gb)])
                CP("act", carry[:, c, :], gp[:, tn:tn + 2], [("gpre", gb)], ["carry"])
                def stageB(ga=ga, gb=gb, pu=pu, c=c, tn=tn):
                    ACT(ga[:, 0:tn], ga[:, 0:tn], AF.Silu, [("gacc", gb)], [("gacc", gb)])
                    TT("dve", hid[:, c, 0:tn], ga[:, 0:tn], bank(pu)[:, 0:tn], ALU.mult, [("gacc", gb), ("ps", pu)], [("hid", c)])
                for fB in pendB:
                    fB()
                pendB = [stageB]
                if lastb:
                    for kc in range(8):
                        MM(bank(6)[:, 0:16], wg_[:, kc, :], nT[:, kc, NP_:NT_], [("wgu", wb, 0), ("nT", 16)], [("ps", 6)], start=(kc == 0), stop=(kc == 7))
                    for kc in range(8):
                        MM(bank(6)[:, 16:32], wu_[:, kc, :], nT[:, kc, NP_:NT_], [("wgu", wb, 1), ("nT", 16)], [("ps", 6)], start=(kc == 0), stop=(kc == 7))
                    for kc in range(8):
                        MM(bank(6)[0:18, 32:160], nT[:, kc, NP_ - 2:NT_], wg_[:, kc, :], [("wgu", wb, 0), ("nT", 15), ("nT", 16)], [("ps", 6)], start=(kc == 0), stop=(kc == 7))
                    CP("act", fco[0:18, 128 * c:128 * (c + 1)], bank(6)[0:18, 32:160], [("ps", 6)], ["fco"])
                    CP("act", gpreS[:, :, 0:2], fcb[:, c, :].rearrange("p (s i) -> p s i", s=4), ["fcb"], ["gpreS"])
                    CP("act", gpreS[:, :, 2:6], bank(6)[:, 0:16].rearrange("p (s t) -> p s t", s=4), [("ps", 6)], ["gpreS"])
                    ACT(gaccS[:], gpreS[:, :, 2:6], AF.Identity, ["gpreS", "prm"], ["gaccS"], scale=w_(2), bias=bcol)
                    STT(gaccS[:], gpreS[:, :, 1:5], w_(1), gaccS[:], ALU.mult, ALU.add, ["gpreS", "gaccS", "prm"], ["gaccS"])
                    STT(gaccS[:], gpreS[:, :, 0:4], w_(0), gaccS[:], ALU.mult, ALU.add, ["gpreS", "gaccS", "prm"], ["gaccS"])
                    ACT(gaccS[:], gaccS[:], AF.Silu, ["gaccS"], ["gaccS"])
                    TT("dve", hidS[:, c, :].rearrange("p (s t) -> p s t", s=4), gaccS[:], bank(6)[:, 16:32].rearrange("p (s t) -> p s t", s=4), ALU.mult, ["gaccS", ("ps", 6)], [("hidS", c)])
            for fB in pendB:
                fB()
            pendB = []
            tiles5 = [(4 * bi + k, 128, hid, 128 * k, [("hid", c) for c in range(22)]) for k in range(4)]
            if lastb:
                tiles5.append((16, NS_, hidS, 0, [("hidS", c) for c in range(22)]))
            for (i, rows, hsrc, hc0, hk) in tiles5:
                b = i % 2
                c0 = i * 128
                DMA(xt5[b][0:rows, :], hscr[c0:c0 + rows, :], [("hscr", i)], [("xt5", b)])
                for half in range(2):
                    pb = 4 + half
                    for c in range(22):
                        MM(bank(pb)[0:rows, :], hsrc[:, c, hc0:hc0 + rows], wdn[:, c, 512 * half:512 * (half + 1)], ["wdn"] + hk, [("ps", pb)], start=(c == 0), stop=(c == 21))
                    TT("dve", h5t[b][0:rows, 512 * half:512 * (half + 1)], xt5[b][0:rows, 512 * half:512 * (half + 1)], bank(pb)[0:rows, :], ALU.add, [("xt5", b), ("ps", pb)], [("h5t", b)])
                S.act(lambda e, b=b, rows=rows: e.activation(out=junk5[0:rows, :], in_=h5t[b][0:rows, :], func=AF.Square, accum_out=st5[b][0:rows, 0:1]), [("h5t", b)], ["junk5", ("st5", b)])
                ACT(st5[b][0:rows, 1:2], st5[b][0:rows, 0:1], AF.Ln, [("st5", b)], [("st5", b)], scale=1.0 / D, bias=EPS)
                ACT(st5[b][0:rows, 1:2], st5[b][0:rows, 1:2], AF.Exp, [("st5", b)], [("st5", b)], scale=-0.5)
                STT(y5[b][0:rows, :], h5t[b][0:rows, :], st5[b][0:rows, 1:2], gfin[0:rows, :], ALU.mult, ALU.mult, [("h5t", b), ("st5", b), "gfin"], [("y5", b)])
                DMA((y_p[c0:c0 + rows, :] if i < NTILE else y_s), y5[b][0:rows, :], [("y5", b)], [("y", i)])
        DMA(fconv_p, fco[0:2, :], ["fco"], ["fconv_p"])
        for s_ in range(4):
            DMA(fconv_s[s_], fco[2 + 4 * s_ + 2:2 + 4 * s_ + 4, :], ["fco"], [("fconv_s", s_)])
        S.finalize(st)
    return nc, S, dbg_out


def prep_core_inputs(inp, core, shared=None):
    f = lambda a: np.ascontiguousarray(a, dtype=np.float32)
    if shared is None:
        shared = {}
        cst, tb, tbs, tbn = host_constants(np.asarray(inp["rel_bias"], np.float32))
        prm = np.zeros((128, PRM_W), np.float32)
        gl = [inp["norm_mix"][0], inp["norm_mem_q"][0], inp["norm_mem_kv"][0], inp["norm_ffn"][0]]
        for i, gvec in enumerate(gl):
            prm[:, PRM["gT"] + 8 * i:PRM["gT"] + 8 * (i + 1)] = np.asarray(gvec).reshape(8, 128).T
        prm[:, PRM["wcd"]:PRM["wcd"] + 96] = np.asarray(inp["w_conv_delta"][0]).T.reshape(24, 128, 4).transpose(1, 0, 2).reshape(128, 96)
        prm[:, PRM["ndo"]] = np.asarray(inp["norm_delta_out"][0])
        prm[:, PRM["wfc"]:PRM["wfc"] + 66] = np.asarray(inp["w_ffn_conv"][0]).T.reshape(22, 128, 3).transpose(1, 0, 2).reshape(128, 66)
        prm[:, PRM["bfc"]:PRM["bfc"] + 22] = np.asarray(inp["b_ffn_conv"][0]).reshape(22, 128).T
        shared.update(
            w_in=f(inp["w_in"][0]), w_ba=f(inp["w_branch_a"][0]), w_bb=f(inp["w_branch_b"][0]), w_out=f(inp["w_out"][0]),
            w_mq=f(inp["w_mem_q"][0]), w_mkv=f(inp["w_mem_kv"][0]), w_mo=f(inp["w_mem_o"][0]), w_up=f(inp["w_ffn_up"][0]),
            w_dn=f(inp["w_ffn_down"][0]), cst=cst, prm=prm, tb=tb.reshape(128, -1), tbs=tbs, tbn=tbn,
            alog=f(inp["a_log"][0]), dtb=f(inp["dt_bias"][0]), gfin=f(inp["norm_final"]))
    sl = slice(4 * core, 4 * core + 4)
    m = dict(shared)
    m.update(
        x_p=f(inp["x_prompt"][core]), x_s=f(inp["x_sample"][sl]).reshape(16, D),
        cache0=f(inp["cache_dil0_kv"][0, sl]).reshape(4, 128, 512), cache1=f(inp["cache_dil1_kv"][0, sl]).reshape(4, 512, 512),
        cache2=f(inp["cache_dil2_kv"][0, sl]).reshape(4, 2048, 512), s0=f(inp["state_delta"][0, sl]),
        dconv_in=f(inp["state_delta_conv"][0, sl]).reshape(12, 3072), cmk=f(inp["cache_mem_k"][0, sl]).reshape(4, 256, 512),
        cmv=f(inp["cache_mem_v"][0, sl]).reshape(4, 256, 512), fconv_in=f(inp["state_ffn_conv"][0, sl]).reshape(8, FFN),
        mem_p=f(inp["mem_prompt"][core]))
    return m, shared


_PROG = None


def kernel(**inp):
    global _PROG
    inp = {k: np.asarray(v) for k, v in inp.items()}
    if _PROG is None:
        _PROG = build_program()[0]
    nc = _PROG
    in_maps = []
    shared = None
    for core in range(8):
        m, shared = prep_core_inputs(inp, core, shared)
        in_maps.append(m)
    res = run_bass_kernel_spmd(nc, in_maps, core_ids=list(range(8)))
    R = res.results
    st = lambda name: np.stack([np.asarray(r[name], np.float32) for r in R], 0)
    y_prompt = st("y_p")
    y_sample = st("y_s").reshape(32, 4, D)
    outs = [y_prompt, y_sample]
    for g in range(3):
        outs.append(st("dil%d_p" % g).reshape(1, 8, WINS[g], 2, 4, 64))
    outs.append(st("delta_p").reshape(1, 8, 8, 128, 128))
    outs.append(st("dconv_p").reshape(1, 8, 3, 3072))
    outs.append(st("memk_p").reshape(1, 8, 256, 4, 128))
    outs.append(st("memv_p").reshape(1, 8, 256, 4, 128))
    outs.append(st("fconv_p").reshape(1, 8, 2, FFN))
    for g in range(3):
        outs.append(st("dil%d_s" % g).reshape(1, 32, WINS[g], 2, 4, 64))
    outs.append(st("delta_s").reshape(1, 32, 8, 128, 128))
    outs.append(st("dconv_s").reshape(1, 32, 3, 3072))
    outs.append(st("fconv_s").reshape(1, 32, 2, FFN))
    return tuple(outs)
```
